# Optimizing a Trainium2 kernel written in Bass

```python
import math
import jax, jax.numpy as jnp
from jax import lax
import numpy as np

D_MODEL = 1024
BATCH = 8
SEQ = 2048
DEPTH = 1
DEC_BATCH = 128
DEC_SEQ = 1
PAST_LEN = 16384
PAGE_SIZE = 128

D_LRU = D_MODEL // 2
D_S5 = D_MODEL - D_LRU
D_MIX = D_LRU + D_S5
LRU_HEADS = 8
LRU_HEAD_DIM = D_LRU // LRU_HEADS
CONV_W = 4
LRU_C = 8.0
S5_GROUP = 16
S5_GROUPS = D_S5 // S5_GROUP
S5_STATE = 64
D_FF = int(math.ceil(8 * D_MODEL / 3 / 256)) * 256
EPS = 1e-6
DT_MIN = 1e-3
DT_MAX = 1e-1

kernel_name = 'hymba_rglru_s5_adaln_decode_step'

F32 = jnp.float32


def rmsnorm(x, g):
    x32 = x.astype(F32)
    y = x32 * lax.rsqrt(jnp.mean(x32 * x32, axis=-1, keepdims=True) + EPS)
    return y * g.astype(F32)


def _real_combine(l, r):
    a_l, b_l = l
    a_r, b_r = r
    return a_l * a_r, a_r * b_l + b_r


def linear_scan(a, b, h0):
    a_cum, h = lax.associative_scan(_real_combine, (a, b), axis=1)
    return h + a_cum * h0[:, None]


def _complex_combine(l, r):
    ar_l, ai_l, br_l, bi_l = l
    ar_r, ai_r, br_r, bi_r = r
    ar = ar_l * ar_r - ai_l * ai_r
    ai = ar_l * ai_r + ai_l * ar_r
    br = ar_r * br_l - ai_r * bi_l + br_r
    bi = ar_r * bi_l + ai_r * br_l + bi_r
    return ar, ai, br, bi


def complex_linear_scan(ar, ai, br, bi, h0r, h0i):
    cr, ci, hr, hi = lax.associative_scan(_complex_combine, (ar, ai, br, bi), axis=1)
    h0r_ = h0r[:, None]
    h0i_ = h0i[:, None]
    return hr + cr * h0r_ - ci * h0i_, hi + cr * h0i_ + ci * h0r_


def rg_lru(xc, h0, wa, ba, wx, bx, lam):
    B, T, _ = xc.shape
    xh = xc.reshape(B, T, LRU_HEADS, LRU_HEAD_DIM)
    r = jax.nn.sigmoid(jnp.einsum('bthi,hij->bthj', xh, wa.astype(F32)).reshape(B, T, D_LRU) + ba.astype(F32))
    i = jax.nn.sigmoid(jnp.einsum('bthi,hij->bthj', xh, wx.astype(F32)).reshape(B, T, D_LRU) + bx.astype(F32))
    log_a = LRU_C * r * jax.nn.log_sigmoid(lam.astype(F32))
    a = jnp.exp(log_a)
    mult = jnp.sqrt(-jnp.expm1(2.0 * log_a))
    hs = linear_scan(a, mult * (i * xc), h0.astype(F32))
    return hs, hs[:, -1]


def s5_layer(u, h0r, h0i, lam_re, lam_im, log_dt, b_re, b_im, c_re, c_im, d, w_glu):
    B, T, _ = u.shape
    ug = u.reshape(B, T, S5_GROUPS, S5_GROUP)
    lr = lam_re.astype(F32)
    li = lam_im.astype(F32)
    dt = jnp.exp(log_dt.astype(F32))[:, None]
    mag = jnp.exp(lr * dt)
    ab_r = mag * jnp.cos(li * dt)
    ab_i = mag * jnp.sin(li * dt)
    den = lr * lr + li * li
    fr = ((ab_r - 1.0) * lr + ab_i * li) / den
    fi = (ab_i * lr - (ab_r - 1.0) * li) / den
    br32 = b_re.astype(F32)
    bi32 = b_im.astype(F32)
    bb_r = fr[..., None] * br32 - fi[..., None] * bi32
    bb_i = fr[..., None] * bi32 + fi[..., None] * br32
    bu_r = jnp.einsum('btgc,gnc->btgn', ug, bb_r)
    bu_i = jnp.einsum('btgc,gnc->btgn', ug, bb_i)
    ar = jnp.broadcast_to(ab_r, bu_r.shape)
    ai = jnp.broadcast_to(ab_i, bu_i.shape)
    hr, hi = complex_linear_scan(ar, ai, bu_r, bu_i, h0r.astype(F32), h0i.astype(F32))
    y = jnp.einsum('btgn,gcn->btgc', hr, c_re.astype(F32)) - jnp.einsum('btgn,gcn->btgc', hi, c_im.astype(F32))
    y = y.reshape(B, T, D_S5) + d.astype(F32) * u
    z = jax.nn.gelu(y) @ w_glu.astype(F32)
    za, zb = jnp.split(z, 2, axis=-1)
    return za * jax.nn.sigmoid(zb), hr[:, -1], hi[:, -1]


def decoder_layer(x, c, conv_buf, lru_h0, s5_h0r, s5_h0i, p):
    T = x.shape[1]
    mod = jax.nn.silu(c.astype(F32)) @ p['ada_w'].astype(F32) + p['ada_b'].astype(F32)
    sh1, sc1, g1, sh2, sc2, g2 = jnp.split(mod[:, None, :], 6, axis=-1)

    hn = rmsnorm(x, p['norm1_g']) * (1.0 + sc1) + sh1
    proj = hn @ p['w_in'].astype(F32)
    lru_x = proj[..., :D_LRU]
    lru_gate = proj[..., D_LRU:2 * D_LRU]
    s5_u = proj[..., 2 * D_LRU:]

    xc_full = jnp.concatenate([conv_buf.astype(F32), lru_x], axis=1)
    conv_w = p['conv_w'].astype(F32)
    conv = p['conv_b'].astype(F32) + sum(conv_w[k] * xc_full[:, k:k + T] for k in range(CONV_W))
    new_conv = xc_full[:, -(CONV_W - 1):]
    hs, lru_last = rg_lru(conv, lru_h0, p['lru_wa'], p['lru_ba'], p['lru_wx'], p['lru_bx'], p['lru_lambda'])
    lru_out = hs * jax.nn.gelu(lru_gate)

    s5_out, s5_r, s5_i = s5_layer(s5_u, s5_h0r, s5_h0i, p['s5_lambda_re'], p['s5_lambda_im'], p['s5_log_dt'],
                                  p['s5_b_re'], p['s5_b_im'], p['s5_c_re'], p['s5_c_im'], p['s5_d'], p['s5_w_glu'])

    merged = jnp.concatenate([rmsnorm(lru_out, p['g_lru_out']), rmsnorm(s5_out, p['g_s5_out'])], axis=-1)
    x32 = x.astype(F32) + g1 * (merged @ p['w_out'].astype(F32))

    hn2 = rmsnorm(x32, p['norm2_g']) * (1.0 + sc2) + sh2
    ffn = (jax.nn.silu(hn2 @ p['ffn_w_gate'].astype(F32)) * (hn2 @ p['ffn_w_up'].astype(F32))) @ p['ffn_w_down'].astype(F32)
    x32 = x32 + g2 * ffn
    return x32.astype(x.dtype), new_conv, lru_last, s5_r, s5_i


def setup_inputs(seed: int = 0) -> dict:
    key = jax.random.key(seed)
    ks = jax.random.split(key, 40)
    nrm = lambda k, shape, s: jax.random.normal(k, shape, F32) * s
    u = jax.random.uniform(ks[12], (DEPTH, D_LRU), F32, 0.9, 0.999)
    s = u ** (1.0 / LRU_C)
    lru_lambda = jnp.log(s) - jnp.log1p(-s)
    n_idx = jnp.arange(S5_STATE, dtype=F32)
    s5_lambda_re = -0.5 + nrm(ks[13], (DEPTH, S5_GROUPS, S5_STATE), 0.01)
    s5_lambda_im = math.pi * n_idx + nrm(ks[14], (DEPTH, S5_GROUPS, S5_STATE), 0.01)
    s5_log_dt = jax.random.uniform(ks[15], (DEPTH, S5_GROUPS), F32, math.log(DT_MIN), math.log(DT_MAX))
    return {
        'x_prompt': nrm(ks[0], (BATCH, SEQ, D_MODEL), 1.0),
        'x_sample': nrm(ks[1], (DEC_BATCH, DEC_SEQ, D_MODEL), 1.0),
        'state_conv': nrm(ks[2], (DEPTH, DEC_BATCH, CONV_W - 1, D_LRU), 1.0),
        'state_lru': nrm(ks[3], (DEPTH, DEC_BATCH, D_LRU), 0.5),
        'state_s5_re': nrm(ks[4], (DEPTH, DEC_BATCH, S5_GROUPS, S5_STATE), 0.5),
        'state_s5_im': nrm(ks[5], (DEPTH, DEC_BATCH, S5_GROUPS, S5_STATE), 0.5),
        'c_prompt': nrm(ks[6], (BATCH, D_MODEL), 1.0),
        'c_sample': nrm(ks[7], (DEC_BATCH, D_MODEL), 1.0),
        'ada_w': nrm(ks[8], (DEPTH, D_MODEL, 6 * D_MODEL), 0.5 * D_MODEL ** -0.5),
        'ada_b': nrm(ks[9], (DEPTH, 6 * D_MODEL), 0.01),
        'norm1_g': 1.0 + nrm(ks[10], (DEPTH, D_MODEL), 0.01),
        'w_in': nrm(ks[11], (DEPTH, D_MODEL, 2 * D_LRU + D_S5), D_MODEL ** -0.5),
        'conv_w': nrm(ks[16], (DEPTH, CONV_W, D_LRU), CONV_W ** -0.5),
        'conv_b': nrm(ks[17], (DEPTH, D_LRU), 0.01),
        'lru_wa': nrm(ks[18], (DEPTH, LRU_HEADS, LRU_HEAD_DIM, LRU_HEAD_DIM), LRU_HEAD_DIM ** -0.5),
        'lru_ba': nrm(ks[19], (DEPTH, D_LRU), 0.01),
        'lru_wx': nrm(ks[20], (DEPTH, LRU_HEADS, LRU_HEAD_DIM, LRU_HEAD_DIM), LRU_HEAD_DIM ** -0.5),
        'lru_bx': nrm(ks[21], (DEPTH, D_LRU), 0.01),
        'lru_lambda': lru_lambda,
        's5_lambda_re': s5_lambda_re,
        's5_lambda_im': s5_lambda_im,
        's5_log_dt': s5_log_dt,
        's5_b_re': nrm(ks[22], (DEPTH, S5_GROUPS, S5_STATE, S5_GROUP), (2 * S5_GROUP) ** -0.5),
        's5_b_im': nrm(ks[23], (DEPTH, S5_GROUPS, S5_STATE, S5_GROUP), (2 * S5_GROUP) ** -0.5),
        's5_c_re': nrm(ks[24], (DEPTH, S5_GROUPS, S5_GROUP, S5_STATE), (2 * S5_STATE) ** -0.5),
        's5_c_im': nrm(ks[25], (DEPTH, S5_GROUPS, S5_GROUP, S5_STATE), (2 * S5_STATE) ** -0.5),
        's5_d': nrm(ks[26], (DEPTH, D_S5), 1.0),
        's5_w_glu': nrm(ks[27], (DEPTH, D_S5, 2 * D_S5), D_S5 ** -0.5),
        'g_lru_out': 1.0 + nrm(ks[28], (DEPTH, D_LRU), 0.01),
        'g_s5_out': 1.0 + nrm(ks[29], (DEPTH, D_S5), 0.01),
        'w_out': nrm(ks[30], (DEPTH, D_MIX, D_MODEL), D_MIX ** -0.5),
        'norm2_g': 1.0 + nrm(ks[31], (DEPTH, D_MODEL), 0.01),
        'ffn_w_gate': nrm(ks[32], (DEPTH, D_MODEL, D_FF), D_MODEL ** -0.5),
        'ffn_w_up': nrm(ks[33], (DEPTH, D_MODEL, D_FF), D_MODEL ** -0.5),
        'ffn_w_down': nrm(ks[34], (DEPTH, D_FF, D_MODEL), D_FF ** -0.5),
        'final_norm_g': 1.0 + nrm(ks[35], (D_MODEL,), 0.01),
    }


def reference(x_prompt, x_sample, state_conv, state_lru, state_s5_re, state_s5_im, c_prompt, c_sample,
              ada_w, ada_b, norm1_g, w_in, conv_w, conv_b, lru_wa, lru_ba, lru_wx, lru_bx, lru_lambda,
              s5_lambda_re, s5_lambda_im, s5_log_dt, s5_b_re, s5_b_im, s5_c_re, s5_c_im, s5_d, s5_w_glu,
              g_lru_out, g_s5_out, w_out, norm2_g, ffn_w_gate, ffn_w_up, ffn_w_down, final_norm_g):
    bp = x_prompt.shape[0]
    yp = x_prompt
    ys = x_sample
    conv_p, lru_p, s5r_p, s5i_p = [], [], [], []
    conv_s, lru_s, s5r_s, s5i_s = [], [], [], []
    for l in range(DEPTH):
        p = dict(ada_w=ada_w[l], ada_b=ada_b[l], norm1_g=norm1_g[l], w_in=w_in[l], conv_w=conv_w[l],
                 conv_b=conv_b[l], lru_wa=lru_wa[l], lru_ba=lru_ba[l], lru_wx=lru_wx[l], lru_bx=lru_bx[l],
                 lru_lambda=lru_lambda[l], s5_lambda_re=s5_lambda_re[l], s5_lambda_im=s5_lambda_im[l],
                 s5_log_dt=s5_log_dt[l], s5_b_re=s5_b_re[l], s5_b_im=s5_b_im[l], s5_c_re=s5_c_re[l],
                 s5_c_im=s5_c_im[l], s5_d=s5_d[l], s5_w_glu=s5_w_glu[l], g_lru_out=g_lru_out[l],
                 g_s5_out=g_s5_out[l], w_out=w_out[l], norm2_g=norm2_g[l], ffn_w_gate=ffn_w_gate[l],
                 ffn_w_up=ffn_w_up[l], ffn_w_down=ffn_w_down[l])
        yp, cp, hp, rp, ip = decoder_layer(
            yp, c_prompt,
            jnp.zeros((bp, CONV_W - 1, D_LRU), F32), jnp.zeros((bp, D_LRU), F32),
            jnp.zeros((bp, S5_GROUPS, S5_STATE), F32), jnp.zeros((bp, S5_GROUPS, S5_STATE), F32), p)
        ys, cs, hs, rs, is_ = decoder_layer(
            ys, c_sample, state_conv[l], state_lru[l], state_s5_re[l], state_s5_im[l], p)
        conv_p.append(cp); lru_p.append(hp); s5r_p.append(rp); s5i_p.append(ip)
        conv_s.append(cs); lru_s.append(hs); s5r_s.append(rs); s5i_s.append(is_)
    y_prompt = rmsnorm(yp, final_norm_g).astype(x_prompt.dtype)
    y_sample = rmsnorm(ys, final_norm_g).astype(x_sample.dtype)
    conv_prompt = jnp.stack(conv_p, 0).astype(state_conv.dtype)
    lru_prompt = jnp.stack(lru_p, 0).astype(state_lru.dtype)
    s5_re_prompt = jnp.stack(s5r_p, 0).astype(state_s5_re.dtype)
    s5_im_prompt = jnp.stack(s5i_p, 0).astype(state_s5_im.dtype)
    conv_sample = jnp.stack(conv_s, 0).astype(state_conv.dtype)
    lru_sample = jnp.stack(lru_s, 0).astype(state_lru.dtype)
    s5_re_sample = jnp.stack(s5r_s, 0).astype(state_s5_re.dtype)
    s5_im_sample = jnp.stack(s5i_s, 0).astype(state_s5_im.dtype)
    return (y_prompt, y_sample, conv_prompt, lru_prompt, s5_re_prompt, s5_im_prompt,
            conv_sample, lru_sample, s5_re_sample, s5_im_sample)
```

```python
import math
import contextlib
import numpy as np
import concourse.bass as bass
import concourse.mybir as mybir
from concourse.bass_utils import run_bass_kernel_spmd

F32 = mybir.dt.float32
BF16 = mybir.dt.bfloat16
AF = mybir.ActivationFunctionType
ALU = mybir.AluOpType

ENGINES = ("tensor", "vector", "scalar", "gpsimd", "sync")


class Buf:
    __slots__ = ("name", "writers", "readers", "excl")

    def __init__(self, name, excl=False):
        self.name = name
        self.writers = []
        self.readers = []
        self.excl = excl


class Op:
    __slots__ = ("eng", "fn", "deps", "is_dma", "grp", "grp_val", "signal", "sig_val", "finish")

    def __init__(self, eng, fn, is_dma=False, grp=None):
        self.eng = eng
        self.fn = fn
        self.deps = []
        self.is_dma = is_dma
        self.grp = grp
        self.grp_val = 0
        self.signal = False
        self.sig_val = 0
        self.finish = 0.0


class Prog:
    def __init__(self, nc):
        self.nc = nc
        self.ops = {e: [] for e in ENGINES}
        self.grp_count = {}
        self.all_ops = []
        self.barrier_op = None
        self.eng_free = {e: 0.0 for e in ENGINES}
        self.step_max = 0.0
        self.next_cost = None

    def _add_deps(self, op, reads, writes, partial):
        deps = []
        if self.barrier_op is not None:
            deps.append(self.barrier_op)
        for b in reads:
            deps.extend(b.writers)
            if b.excl:
                deps.extend(r for r in b.readers if r.eng != op.eng)
        for b in writes:
            deps.extend(b.readers)
            if partial:
                deps.extend(w for w in b.writers
                            if not ((w.is_dma and op.is_dma) or (w.eng == op.eng and not w.is_dma and not op.is_dma)))
            else:
                deps.extend(b.writers)
        seen = set()
        for d in deps:
            if d is op or id(d) in seen:
                continue
            seen.add(id(d))
            if d.eng == "tensor" and op.eng == "tensor" and not d.is_dma and not op.is_dma:
                continue
            op.deps.append(d)
        for b in reads:
            b.readers.append(op)
        for b in writes:
            if partial and not b.readers:
                b.writers.append(op)
            else:
                b.writers = [op]
                b.readers = []

    def _push(self, o):
        cost = self.next_cost if self.next_cost is not None else 0.3
        self.next_cost = None
        ready = max([d.finish for d in o.deps], default=0.0)
        if o.is_dma:
            start = max(self.eng_free[o.eng], ready)
            self.eng_free[o.eng] = start + 0.1
            o.finish = start + 2.0 + cost
        else:
            start = max(self.eng_free[o.eng], ready)
            o.finish = start + cost
            self.eng_free[o.eng] = o.finish
        if o.finish > self.step_max:
            self.step_max = o.finish
        self.ops[o.eng].append(o)
        self.all_ops.append(o)
        return o

    def op(self, eng, fn, reads=(), writes=(), partial=False):
        o = Op(eng, fn)
        self._add_deps(o, reads, writes, partial)
        return self._push(o)

    def dma(self, eng, out, in_, reads=(), writes=(), partial=False, **kw):
        grp = writes[0] if writes else reads[0]
        key = id(grp)
        o = Op(eng, lambda e: e.dma_start(out=out, in_=in_, **kw), is_dma=True, grp=key)
        self.grp_count[key] = self.grp_count.get(key, 0) + 1
        o.grp_val = 16 * self.grp_count[key]
        self._add_deps(o, reads, writes, partial)
        return self._push(o)

    def barrier(self, fn, eng="vector"):
        o = Op(eng, fn)
        for e in ENGINES:
            for prev in reversed(self.ops[e]):
                if not prev.is_dma:
                    o.deps.append(prev)
                    break
        lastg = {}
        for prev in self.all_ops:
            if prev.is_dma:
                lastg[prev.grp] = prev
        o.deps.extend(lastg.values())
        self.barrier_op = o
        return self._push(o)

    def emit(self, final_wait_eng="sync"):
        nc = self.nc
        for o in self.all_ops:
            for d in o.deps:
                if not d.is_dma:
                    d.signal = True
        last_sig = {}
        for e in ENGINES:
            c = 0
            for o in self.ops[e]:
                if o.signal:
                    c += 1
                    o.sig_val = c
            last_sig[e] = c
        stack = contextlib.ExitStack()
        esem = {e: stack.enter_context(nc.semaphore("s_" + e)) for e in ENGINES}
        gsem = {}
        for k in self.grp_count:
            gsem[k] = stack.enter_context(nc.semaphore("g%d" % len(gsem)))
        final = [(k, 16 * n) for k, n in self.grp_count.items()]
        ops = self.ops
        block = stack.enter_context(nc.Block())

        def make(ename):
            def body(eng):
                waited_e = {e: 0 for e in ENGINES}
                waited_g = {}
                for o in ops[ename]:
                    need_g = {}
                    need_e = {}
                    for d in o.deps:
                        if d.is_dma:
                            if need_g.get(d.grp, 0) < d.grp_val:
                                need_g[d.grp] = d.grp_val
                        else:
                            if need_e.get(d.eng, 0) < d.sig_val:
                                need_e[d.eng] = d.sig_val
                    for gk, gv in need_g.items():
                        if waited_g.get(gk, 0) < gv:
                            eng.wait_ge(gsem[gk], gv)
                            waited_g[gk] = gv
                    for ek, ev in need_e.items():
                        if waited_e[ek] < ev:
                            eng.wait_ge(esem[ek], ev)
                            waited_e[ek] = ev
                    ins = o.fn(eng)
                    if o.is_dma:
                        ins.then_inc(gsem[o.grp], 16)
                    elif o.signal:
                        ins.then_inc(esem[ename], 1)
                if ename == final_wait_eng:
                    for k, v in final:
                        if waited_g.get(k, 0) < v:
                            eng.wait_ge(gsem[k], v)
                    for e in ENGINES:
                        if e != ename and last_sig[e] and waited_e[e] < last_sig[e]:
                            eng.wait_ge(esem[e], last_sig[e])
            return body

        for e in ENGINES:
            getattr(block, e)(make(e))
        stack.close()


class Ring:
    def __init__(self, items):
        self.items = items
        self.i = 0

    def next(self):
        it = self.items[self.i % len(self.items)]
        self.i += 1
        return it


P = 128
D = 1024
KD = 8
T = 2048
NS = 16
NB = NS + 1
DL = 512
DFF = 2816
NF = 22
CH = 256
NCH = T // CH
L = 4
NK = CH // L
CH2 = 512
NCH2 = T // CH2
EPS = 1e-6
MAGIC = 12582912.0
TWO_PI = 2.0 * math.pi

C_ADB, C_N1G, C_N2G, C_CW, C_CB, C_BA, C_BX, C_LAM, C_SD, C_GLO, C_GSO, C_FNG = (
    0, 48, 56, 64, 80, 84, 88, 92, 96, 100, 104, 108)
M_SH1, M_SC1, M_G1, M_SH2, M_SC2, M_G2 = 0, 8, 16, 24, 32, 40


class Ctx:
    pass


DEBUG_STOP = None


class _Stop(Exception):
    pass


def build_program():
    try:
        return _build_program()
    except _Stop as e:
        return e.args[0]


def _build_program():
    nc = bass.Bass("TRN2", target_bir_lowering=False)
    din = lambda name, shape: nc.dram_tensor(name, list(shape), F32, kind="ExternalInput").ap()
    dout = lambda name, shape: nc.dram_tensor(name, list(shape), F32, kind="ExternalOutput").ap()
    xp_d = din("xp", (T, D)); xs_d = din("xs", (NS, D))
    sconv_d = din("sconv", (NS, 3, DL)); slru_d = din("slru", (NS, DL))
    ss5r_d = din("ss5r", (NS, 2048)); ss5i_d = din("ss5i", (NS, 2048))
    call_d = din("c_all", (NB, D))
    adaw_d = din("ada_w", (D, 6 * D))
    rowpack_d = din("rowpack", (P, P)); s5pack_d = din("s5pack", (48, P))
    win_d = din("w_in", (D, 1536)); wa_d = din("lru_wa", (8, 64, 64)); wx_d = din("lru_wx", (8, 64, 64))
    bre_d = din("s5_b_re", (32, 64, 16)); bim_d = din("s5_b_im", (32, 64, 16))
    cre_d = din("s5_c_re", (32, 16, 64)); cim_d = din("s5_c_im", (32, 16, 64))
    wglu_d = din("w_glu", (DL, 2 * DL)); wout_d = din("w_out", (D, D))
    wg_d = din("ffn_w_gate", (D, DFF)); wu_d = din("ffn_w_up", (D, DFF)); wd_d = din("ffn_w_down", (DFF, D))

    yp_d = dout("yp", (T, D)); ys_d = dout("ys", (NS, D))
    convp_d = dout("convp", (3, DL)); lrup_d = dout("lrup", (4, P))
    s5rp_d = dout("s5rp", (16, P)); s5ip_d = dout("s5ip", (16, P))
    convs_d = dout("convs", (NS, 3, DL)); lrus_d = dout("lrus", (NS, DL))
    s5rs_d = dout("s5rs", (NS, 2048)); s5is_d = dout("s5is", (NS, 2048))
    x1_d = nc.dram_tensor("x1_scratch", [T + NS, D], F32, kind="Internal").ap()

    pr = Prog(nc)
    B = Buf

    def dbg_stop(tag):
        if DEBUG_STOP == tag:
            pr.emit()
            raise _Stop(nc)
    es = contextlib.ExitStack()
    sb = lambda name, shape, dt=F32: es.enter_context(nc.sbuf_tensor(name, list(shape), dt))
    ps = lambda name, shape, dt=F32: es.enter_context(nc.psum_tensor(name, list(shape), dt))

    def _n(ap):
        n = 1
        for s_ in ap.shape[1:]:
            n *= int(s_)
        return n

    def _cost(eng, out, f):
        n = _n(out)
        if eng == "vector":
            return 0.08 + n * f / 960.0
        if eng == "scalar":
            return 0.22 + n / 1400.0
        if eng == "gpsimd":
            return 0.15 + n * 2.3 / 1000.0
        return 0.3

    def tt(eng, out, in0, in1, op, r, w, partial=False):
        pr.next_cost = _cost(eng, out, 1.5)
        pr.op(eng, lambda e: e.tensor_tensor(out=out, in0=in0, in1=in1, op=op), r, w, partial)

    def ts(eng, out, in0, s1, s2, op0, op1, r, w, partial=False):
        pr.next_cost = _cost(eng, out, 1.0)
        if op1 is None:
            pr.op(eng, lambda e: e.tensor_scalar(out=out, in0=in0, scalar1=s1, scalar2=None, op0=op0), r, w, partial)
        else:
            pr.op(eng, lambda e: e.tensor_scalar(out=out, in0=in0, scalar1=s1, scalar2=s2, op0=op0, op1=op1), r, w, partial)

    def stt(out, in0, scalar, in1, op0, op1, r, w, partial=False):
        pr.next_cost = _cost("vector", out, 1.7)
        pr.op("vector", lambda e: e.scalar_tensor_tensor(out=out, in0=in0, scalar=scalar, in1=in1, op0=op0, op1=op1), r, w, partial)

    def cp(eng, out, in_, r, w, partial=False):
        pr.next_cost = _cost(eng, out, 1.0)
        if eng == "scalar":
            pr.op(eng, lambda e: e.copy(out=out, in_=in_), r, w, partial)
        else:
            pr.op(eng, lambda e: e.tensor_copy(out=out, in_=in_), r, w, partial)

    def act(out, in_, func, r, w, scale=None, bias=None, accum=None, partial=False):
        pr.next_cost = _cost("scalar", out, 1.0)
        kw = {}
        if scale is not None:
            kw["scale"] = scale
        if bias is not None:
            kw["bias"] = bias
        if accum is not None:
            kw["accum_out"] = accum
        pr.op("scalar", lambda e: e.activation(out=out, in_=in_, func=func, **kw), r, w, partial)

    def mset(eng, ap, val, w, partial=False):
        pr.op(eng, lambda e: e.memset(ap, val), (), w, partial)

    def mm(out, lhsT, rhs, start, stop, r, w, tp=None):
        pr.next_cost = 0.05 + max(64, _n(out)) / 2400.0 * (4.0 if lhsT.dtype == F32 else 1.0)
        if tp is None:
            pr.op("tensor", lambda e: e.matmul(out, lhsT, rhs, start=start, stop=stop), r, w, True)
        else:
            pr.op("tensor", lambda e: e.matmul(out, lhsT, rhs, start=start, stop=stop, tile_position=tp), r, w, True)

    def tr(out, in_, ident, r, w):
        pr.next_cost = 0.05 + max(64, _n(out)) / 2400.0
        pr.op("tensor", lambda e: e.transpose(out, in_, ident), r, w, True)

    def scan(out, d0, d1, init, r, w, partial=False):
        pr.next_cost = _cost("vector", out, 2.0)
        pr.op("vector", lambda e: e.tensor_tensor_scan(out=out, data0=d0, data1=d1, initial=init, op0=ALU.mult, op1=ALU.add), r, w, partial)

    def recip(out, in_, r, w, partial=False):
        pr.op("vector", lambda e: e.reciprocal(out=out, in_=in_), r, w, partial)

    def round_to_int(out, in_, tmp, r, w):
        ts("vector", tmp, in_, MAGIC, None, ALU.add, None, r, w)
        ts("vector", out, tmp, MAGIC, None, ALU.subtract, None, r, w)

    pA = [ps("pA%d" % i, (P, 512)) for i in range(4)]
    pS = [ps("pS%d" % i, (P, 512)) for i in range(2)]
    pT = [ps("pT%d" % i, (P, 1024), BF16) for i in range(1)]
    pD = ps("pD", (P, 512))
    bA = [B("pA%d" % i, True) for i in range(4)]
    bS = [B("pS%d" % i, True) for i in range(2)]
    bT = [B("pT%d" % i, True) for i in range(1)]
    bD = B("pD", True)
    ringA = Ring(list(zip(pA, bA)))
    ringS = Ring(list(zip(pS, bS)))
    ringT = Ring(list(zip(pT, bT)))
    ringD = Ring([(pD, bD)])
    ringAll = Ring(list(zip(pA + pS + [pD], bA + bS + [bD])))
    ringA01 = Ring([(pA[0], bA[0]), (pA[1], bA[1])])
    ringA23 = Ring([(pS[0], bS[0]), (pS[1], bS[1]), (pA[2], bA[2]), (pA[3], bA[3])])

    identF = sb("identF", (P, P)); identB = sb("identB", (P, P), BF16)
    onesF = sb("onesF", (P, P)); onesB = sb("onesB", (P, P), BF16)
    cols = sb("cols", (P, P))
    pmod = sb("pmod", (P, 32))
    modT = sb("modT", (P, 48, NB))
    hn2Ts = sb("hn2Ts", (P, KD, NS), BF16)
    barr_t = sb("barr_t", (P, 1))
    b_const = B("const"); b_cols = B("cols"); b_pmod = B("pmod"); b_modT = B("modT"); b_hn2Ts = B("hn2Ts")
    b_x1s_scr = B("x1s_scr")

    mset("gpsimd", identF[:], 1.0, [b_const])
    pr.op("gpsimd", lambda e: e.affine_select(out=identF[:], in_=identF[:], pattern=[[-1, P]], compare_op=ALU.is_equal,
                                              fill=0.0, base=0, channel_multiplier=1), [b_const], [b_const])
    cp("gpsimd", identB[:], identF[:], [b_const], [b_const])
    mset("gpsimd", onesF[:], 1.0, [b_const])
    mset("gpsimd", onesB[:], 1.0, [b_const])
    dbg_stop('consts')

    def bcast_rows(dst, bdst, src_cols, bsrc, nrows, diag, b_diag, first=True, ring=None):
        for h in range(2):
            for i in range(4):
                ts("vector", diag[:, i * P:(i + 1) * P], identF[:], src_cols[:, 4 * h + i:4 * h + i + 1], None, ALU.mult, None,
                   [b_const, bsrc], [b_diag], i > 0)
            pM, bM = (ring or ringA).next()
            mm(pM[0:nrows, :], onesF[:, 0:nrows], diag[:], True, True, [b_const, b_diag], [bM])
            cp("scalar", dst[0:nrows, h * 512:(h + 1) * 512], pM[0:nrows, :], [bM], [bdst], (h > 0) or not first)

    s1 = contextlib.ExitStack()
    sb1 = lambda name, shape, dt=F32: s1.enter_context(nc.sbuf_tensor(name, list(shape), dt))
    win_sb = sb1("win_sb", (P, KD, 1536), BF16); b_win = B("win")
    wa_blk = sb1("wa_blk", (P, 4, P), BF16); wx_blk = sb1("wx_blk", (P, 4, P), BF16); b_wab = B("wa"); b_wxb = B("wx")
    wglu_sb = sb1("wglu_sb", (P, 4, 2 * DL), BF16); b_wglu = B("wglu")
    wout_sb = sb1("wout_sb", (P, KD, D), BF16); b_wout = B("wout")
    lruc = sb1("lruc", (P, 16)); b_lruc = B("lruc")
    s5t = sb1("s5t", (P, 16, 16)); b_s5t = B("s5t")
    W1 = sb1("W1", (P, 2, 4, L, P), BF16); b_W1 = B("W1")
    CAb = sb1("CAb", (P, 2, 16, L, 32), BF16); b_CAb = B("CAb")
    Kblk = sb1("Kblk", (P, 4, L, P), BF16); b_Kblk = B("Kblk")
    CT0 = sb1("CT0", (P, 2, 16, 32), BF16); b_CT0 = B("CT0")
    rot = sb1("rot", (P, 2, 16, NK)); b_rot = B("rot")
    rho_tab = sb1("rho_tab", (P, 16, NK)); b_rho = B("rho_tab")
    Dblk = sb1("Dblk", (P, 4, P), BF16); b_Dblk = B("Dblk")
    cT = sb1("cT", (P, KD, NB), BF16); b_cT = B("cT")
    cTf = sb1("cTf", (P, KD, NB)); b_cTf = B("cTf")

    st_ = contextlib.ExitStack()
    sbt = lambda name, shape, dt=F32: st_.enter_context(nc.sbuf_tensor(name, list(shape), dt))
    jv = sbt("jv", (P, 64)); b_jv = B("jv")
    for j in range(64):
        mset("gpsimd", jv[:, j:j + 1], float(j), [b_jv], j > 0)
    G1bc = sbt("G1bc", (P, D)); b_G1 = B("G1bc")
    rowpack_sb = sbt("rowpack_sb", (P, P)); b_rowpack = B("rowpack")
    pr.dma("sync", rowpack_sb[:], rowpack_d[:, :], writes=[b_rowpack])
    s5pack_sb = sbt("s5pack_sb", (48, P)); b_s5pack = B("s5pack")
    pr.dma("sync", s5pack_sb[:], s5pack_d[:, :], writes=[b_s5pack])
    call_sb = sbt("call_sb", (NB, D)); b_call = B("call")
    pr.dma("sync", call_sb[:], call_d[:, :], writes=[b_call])

    Bsrc = sbt("Bsrc", (P, 2, 16, 32)); b_Bsrc = B("Bsrc")
    mset("vector", Bsrc[:], 0.0, [b_Bsrc])
    for ri, bd in enumerate((bre_d, bim_d)):
        bv = bd.rearrange("(p two) n c -> two n p c", two=2)
        pr.dma("sync", Bsrc[0:64, ri, :, 0:16], bv[0], writes=[b_Bsrc], partial=True)
        pr.dma("sync", Bsrc[64:128, ri, :, 16:32], bv[1], writes=[b_Bsrc], partial=True)
    Csrc = sbt("Csrc", (P, 2, 4, P)); b_Csrc = B("Csrc")
    mset("vector", Csrc[:], 0.0, [b_Csrc])
    for ri, cd in enumerate((cre_d, cim_d)):
        cv = cd.rearrange("(q pl two) c n -> pl two c q n", pl=4, two=2)
        for pl in range(4):
            for g2 in range(2):
                p0 = pl * 32 + g2 * 16
                pr.dma("sync", Csrc[p0:p0 + 16, ri, :, g2 * 64:(g2 + 1) * 64], cv[pl, g2], writes=[b_Csrc], partial=True)
    NAB = 4
    adaw_bufs = [sbt("adaw%d" % i, (P, KD, 512), BF16) for i in range(NAB)]
    b_adaw = [B("adaw%d" % i) for i in range(NAB)]
    adaw_v = adaw_d.rearrange("(k p) n -> p k n", p=P)

    def load_adaw(j):
        pr.dma("gpsimd", adaw_bufs[j % NAB][:], adaw_v[:, :, j * 512:(j + 1) * 512], writes=[b_adaw[j % NAB]])

    for j in range(NAB):
        load_adaw(j)

    pS_, bS_ = ringS.next()
    tr(pS_[:, 0:P], rowpack_sb[:], identF[:], [b_rowpack, b_const], [bS_])
    cp("vector", cols[:], pS_[:, 0:P], [bS_], [b_cols])

    csil = sbt("csil", (NB, D)); b_csil = B("csil")
    act(csil[:], call_sb[:], AF.Silu, [b_call], [b_csil])
    pS_, bS_ = ringS.next()
    for k in range(KD):
        tr(pS_[:, k * NB:(k + 1) * NB], csil[:, k * P:(k + 1) * P], identF[0:NB, 0:NB], [b_csil, b_const], [bS_])
    cp("vector", cT[:], pS_[:, 0:KD * NB].rearrange("p (k n) -> p k n", n=NB), [bS_], [b_cT])
    cp("vector", cTf[:], pS_[:, 0:KD * NB].rearrange("p (k n) -> p k n", n=NB), [bS_], [b_cTf])
    dbg_stop('loads')

    win_v = win_d.rearrange("(k p) n -> p k n", p=P)
    for k in range(KD):
        pr.dma("gpsimd", win_sb[:, k, :], win_v[:, k, :], writes=[b_win], partial=True)
    mset("vector", wa_blk[:], 0.0, [b_wab]); mset("vector", wx_blk[:], 0.0, [b_wxb])
    for (blk, src, bb) in ((wa_blk, wa_d, b_wab), (wx_blk, wx_d, b_wxb)):
        sv = src.rearrange("(t two) i j -> two i t j", two=2)
        pr.dma("gpsimd", blk[0:64, :, 0:64], sv[0], writes=[bb], partial=True)
        pr.dma("gpsimd", blk[64:128, :, 64:128], sv[1], writes=[bb], partial=True)
    dbg_stop('weights')

    def gen_modT():
        pMa, bMa = pA[2], bA[2]
        for j in range(6):
            buf = adaw_bufs[j % NAB]; bb = b_adaw[j % NAB]
            for i in range(4):
                c = j * 4 + i
                for k in range(KD):
                    mm(pMa[:, c * NB:(c + 1) * NB], buf[:, k, i * P:(i + 1) * P], cT[:, k, :], k == 0, k == KD - 1, [b_cT, bb], [bMa])
            if j + NAB < 6:
                load_adaw(j + NAB)
            yield
        tt("vector", modT[:, 0:24, :], pMa[:, 0:24 * NB].rearrange("p (c n) -> p c n", n=NB),
           cols[:, C_ADB:C_ADB + 24].unsqueeze(2).to_broadcast([P, 24, NB]), ALU.add, [bMa, b_cols], [b_modT])
        stt(pmod[:, 0:8], modT[:, M_SC1:M_SC1 + 8, NS], 1.0, cols[:, C_N1G:C_N1G + 8], ALU.add, ALU.mult, [b_modT, b_cols], [b_pmod])
        cp("vector", pmod[:, 8:16], modT[:, M_SH1:M_SH1 + 8, NS], [b_modT], [b_pmod], True)
        diag = sbt("diag", (P, 512)); b_diag = B("diag")
        bcast_rows(G1bc, b_G1, modT[:, M_G1:M_G1 + 8, NS], b_modT, P, diag, b_diag, ring=Ring([(pA[2], bA[2])]))
        yield
        for k in range(KD):
            gc = (C_GLO + k) if k < 4 else (C_GSO + k - 4)
            ts("vector", wout_sb[:, k, :], wout_sb[:, k, :], cols[:, gc:gc + 1], None, ALU.mult, None, [b_wout, b_cols], [b_wout])
            if k % 4 == 3:
                yield
        for k in range(KD):
            tt("vector" if k % 2 else "gpsimd", wout_sb[:, k, :], wout_sb[:, k, :], G1bc[:], ALU.mult, [b_wout, b_G1], [b_wout])
            if k % 2 == 1:
                yield

    ringM = Ring([(pA[2], bA[2]), (pA[3], bA[3])])
    ringA2 = Ring([(pA[0], bA[0]), (pA[1], bA[1])])
    gm = gen_modT()

    def pump(n=1):
        for _ in range(n):
            next(gm, None)

    wglu_v = wglu_d.rearrange("(k p) n -> p k n", p=P)
    for k in range(4):
        pr.dma("gpsimd", wglu_sb[:, k, :], wglu_v[:, k, :], writes=[b_wglu], partial=True)
    wout_v = wout_d.rearrange("(k p) n -> p k n", p=P)
    for k in range(KD):
        pr.dma("gpsimd", wout_sb[:, k, :], wout_v[:, k, :], writes=[b_wout], partial=True)

    act(lruc[:, 0:4], cols[:, C_LAM:C_LAM + 4], AF.Exp, [b_cols], [b_lruc], scale=-1.0)
    Rl_ = [b_lruc]
    ts("vector", lruc[:, 4:8], lruc[:, 0:4], -0.25, 1.0 / 3.0, ALU.mult, ALU.add, Rl_, Rl_)
    tt("vector", lruc[:, 4:8], lruc[:, 4:8], lruc[:, 0:4], ALU.mult, Rl_, Rl_)
    ts("vector", lruc[:, 4:8], lruc[:, 4:8], -1.0, 0.5, ALU.mult, ALU.add, Rl_, Rl_)
    tt("vector", lruc[:, 4:8], lruc[:, 4:8], lruc[:, 0:4], ALU.mult, Rl_, Rl_)
    ts("vector", lruc[:, 4:8], lruc[:, 4:8], -1.0, 1.0, ALU.mult, ALU.add, Rl_, Rl_)
    tt("vector", lruc[:, 4:8], lruc[:, 4:8], lruc[:, 0:4], ALU.mult, Rl_, Rl_)
    ts("vector", lruc[:, 8:12], lruc[:, 4:8], -8.0, None, ALU.mult, None, Rl_, Rl_)
    ts("vector", lruc[:, 12:16], lruc[:, 4:8], -16.0, None, ALU.mult, None, Rl_, Rl_)
    cl = lruc[:, 8:12]; cl2 = lruc[:, 12:16]
    dbg_stop('lruc')

    s5c = sbt("s5c", (P, 48)); b_s5c = B("s5c")
    pS_, bS_ = ringS.next()
    tr(pS_[:, 0:48], s5pack_sb[:], identF[0:48, 0:48], [b_s5pack, b_const], [bS_])
    cp("vector", s5c[:], pS_[:, 0:48], [bS_], [b_s5c])
    lr = s5c[:, 0:16]; li = s5c[:, 16:32]; ldt = s5c[:, 32:48]
    V = lambda i: s5t[:, i, :]
    I_DT, I_LRDT, I_TT, I_TR, I_TMP, I_TMP2, I_DEN, I_M1, I_FR, I_FI, I_ABR, I_ABI, I_F8, I_RHO, I_CN, I_SN = range(16)
    R = [b_s5t, b_s5c]; W = [b_s5t]
    act(V(I_DT), ldt, AF.Exp, R, W)
    tt("vector", V(I_LRDT), lr, V(I_DT), ALU.mult, R, W)
    tt("vector", V(I_TT), li, V(I_DT), ALU.mult, R, W)
    ts("vector", V(I_TT), V(I_TT), 1.0 / TWO_PI, None, ALU.mult, None, R, W)
    round_to_int(V(I_TMP2), V(I_TT), V(I_TMP), R, W)
    tt("vector", V(I_TR), V(I_TT), V(I_TMP2), ALU.subtract, R, W)
    pump(2)
    NJ = L + 1
    pw = sbt("pw", (P, 6, 16, NJ)); b_pw = B("pw")

    def bc3(ap2, n):
        return ap2.unsqueeze(2).to_broadcast([P, 16, n])

    def jb3(n):
        return jv[:, 0:n].unsqueeze(1).to_broadcast([P, 16, n])

    def sincos(dst_sin, dst_cos, ang, tmp, tmp2, Rr, Ww):
        round_to_int(tmp2, ang, tmp, Rr, Ww)
        tt("vector", tmp2, ang, tmp2, ALU.subtract, Rr, Ww)
        act(dst_sin, tmp2, AF.Sin, Rr, Ww, scale=TWO_PI)
        ts("vector", tmp, ang, 0.25, None, ALU.add, None, Rr, Ww)
        round_to_int(tmp2, tmp, dst_cos, Rr, Ww)
        tt("vector", tmp2, tmp, tmp2, ALU.subtract, Rr, Ww)
        act(dst_cos, tmp2, AF.Sin, Rr, Ww, scale=TWO_PI)

    Rp = [b_pw, b_s5t, b_jv]; Wp = [b_pw]
    tt("vector", pw[:, 0], bc3(V(I_LRDT), NJ), jb3(NJ), ALU.mult, Rp, Wp)
    act(pw[:, 0], pw[:, 0], AF.Exp, Rp, Wp)
    tt("vector", pw[:, 1], bc3(V(I_TR), NJ), jb3(NJ), ALU.mult, Rp, Wp)
    sincos(pw[:, 5], pw[:, 4], pw[:, 1], pw[:, 2], pw[:, 3], Rp, Wp)
    tt("vector", pw[:, 4], pw[:, 4], pw[:, 0], ALU.mult, Rp, Wp)
    tt("vector", pw[:, 5], pw[:, 5], pw[:, 0], ALU.mult, Rp, Wp)
    PwRe = pw[:, 4]; PwIm = pw[:, 5]
    Rq = [b_pw, b_s5t, b_s5c]
    cp("vector", V(I_ABR), PwRe[:, :, 1], Rq, W)
    cp("vector", V(I_ABI), PwIm[:, :, 1], Rq, W)
    pump(1)
    tt("vector", V(I_DEN), lr, lr, ALU.mult, Rq, W)
    tt("vector", V(I_TMP), li, li, ALU.mult, Rq, W)
    tt("vector", V(I_DEN), V(I_DEN), V(I_TMP), ALU.add, Rq, W)
    recip(V(I_DEN), V(I_DEN), Rq, W)
    ts("vector", V(I_M1), V(I_ABR), -1.0, None, ALU.add, None, Rq, W)
    tt("vector", V(I_FR), V(I_M1), lr, ALU.mult, Rq, W)
    tt("vector", V(I_TMP), V(I_ABI), li, ALU.mult, Rq, W)
    tt("vector", V(I_FR), V(I_FR), V(I_TMP), ALU.add, Rq, W)
    tt("vector", V(I_FR), V(I_FR), V(I_DEN), ALU.mult, Rq, W)
    tt("vector", V(I_FI), V(I_ABI), lr, ALU.mult, Rq, W)
    tt("vector", V(I_TMP), V(I_M1), li, ALU.mult, Rq, W)
    tt("vector", V(I_FI), V(I_FI), V(I_TMP), ALU.subtract, Rq, W)
    tt("vector", V(I_FI), V(I_FI), V(I_DEN), ALU.mult, Rq, W)
    yy = sbt("yy", (P, 3, 16, L)); b_yy = B("yy")
    Ry = [b_yy, b_pw, b_s5t]; Wy = [b_yy]
    tt("vector", yy[:, 0], PwRe[:, :, 0:L], bc3(V(I_FR), L), ALU.mult, Ry, Wy)
    tt("vector", yy[:, 2], PwIm[:, :, 0:L], bc3(V(I_FI), L), ALU.mult, Ry, Wy)
    tt("vector", yy[:, 0], yy[:, 0], yy[:, 2], ALU.subtract, Ry, Wy)
    tt("vector", yy[:, 1], PwIm[:, :, 0:L], bc3(V(I_FR), L), ALU.mult, Ry, Wy)
    tt("vector", yy[:, 2], PwRe[:, :, 0:L], bc3(V(I_FI), L), ALU.mult, Ry, Wy)
    tt("vector", yy[:, 1], yy[:, 1], yy[:, 2], ALU.add, Ry, Wy)
    Yre = yy[:, 0]; Yim = yy[:, 1]
    dbg_stop('s5a')

    pump(1)
    CT = sbt("CT", (P, 2, 16, 32)); b_CT = B("CT")
    for ri in range(2):
        pS_, bS_ = ringS.next()
        for q in range(4):
            tr(pS_[:, q * P:(q + 1) * P], Csrc[:, ri, q, :], identF[:], [b_Csrc, b_const], [bS_])
        cp("vector", CT[:, ri].rearrange("p a b -> p (a b)"), pS_[:, :], [bS_], [b_CT], ri > 0)
    cp("vector", CT0[:, 0], CT[:, 0], [b_CT], [b_CT0])
    ts("vector", CT0[:, 1], CT[:, 1], -1.0, None, ALU.mult, None, [b_CT], [b_CT0], True)
    dbg_stop('s5b')

    Xb = [sbt("Xb%d" % i, (P, 2, 4, L, 32), BF16) for i in range(2)]; b_Xb = [B("Xb%d" % i) for i in range(2)]
    xtmp = sbt("xtmp", (P, 2, 4, L, 16)); b_xtmp = [B("xtmp_h0"), B("xtmp_h1")]
    CTn = sbt("CTn", (P, 16, 32)); b_CTn = B("CTn")
    ts("vector", CTn[:], CT[:, 1], -1.0, None, ALU.mult, None, [b_CT], [b_CTn])
    mset("vector", Kblk[:], 0.0, [b_Kblk])
    mset("vector", CAb[:], 0.0, [b_CAb])
    mset("vector", Xb[0][:], 0.0, [b_Xb[0]])
    mset("vector", Xb[1][:], 0.0, [b_Xb[1]])

    def half_prod(eng, hb, out_ap, a1, b1, a2, b2, op, r, w, p0):
        ps_ = slice(64 * hb, 64 * hb + 64); cs_ = slice(16 * hb, 16 * hb + 16)
        ya = lambda t: t[ps_, p0:p0 + 4, :].unsqueeze(3).to_broadcast([64, 4, L, 16])
        bb_ = lambda t: t[ps_, p0:p0 + 4, cs_].unsqueeze(2).to_broadcast([64, 4, L, 16])
        bt = b_xtmp[hb]
        tt(eng, xtmp[ps_, 0], ya(a1), bb_(b1), ALU.mult, r, [bt])
        tt(eng, xtmp[ps_, 1], ya(a2), bb_(b2), ALU.mult, r + [bt], [bt])
        tt(eng, out_ap[ps_, :, :, cs_], xtmp[ps_, 0], xtmp[ps_, 1], op, [bt], w, True)

    for q in range(4):
        p0 = q * 4
        xb = Xb[q % 2]; bxb = b_Xb[q % 2]
        PwRe1 = PwRe[:, :, 1:L + 1]; PwIm1 = PwIm[:, :, 1:L + 1]
        RX = [b_yy, b_Bsrc]; RCA = [b_pw, b_CT, b_CTn]
        for hb in range(2):
            half_prod("vector", hb, xb[:, 0], Yre, Bsrc[:, 0], Yim, Bsrc[:, 1], ALU.subtract, RX, [bxb], p0)
            half_prod("vector", hb, xb[:, 1], Yre, Bsrc[:, 1], Yim, Bsrc[:, 0], ALU.add, RX, [bxb], p0)
        half_prod("vector", 0, CAb[:, 0, p0:p0 + 4], PwRe1, CT[:, 0], PwIm1, CT[:, 1], ALU.subtract, RCA, [b_CAb], p0)
        pump(1)
        half_prod("gpsimd", 1, CAb[:, 0, p0:p0 + 4], PwRe1, CT[:, 0], PwIm1, CT[:, 1], ALU.subtract, RCA, [b_CAb], p0)
        pump(1)
        half_prod("gpsimd", 0, CAb[:, 1, p0:p0 + 4], PwRe1, CTn, PwIm1, CT[:, 0], ALU.subtract, RCA, [b_CAb], p0)
        pump(1)
        half_prod("gpsimd", 1, CAb[:, 1, p0:p0 + 4], PwRe1, CTn, PwIm1, CT[:, 0], ALU.subtract, RCA, [b_CAb], p0)
        pump(1)
        for ri in range(2):
            for jq in range(L // 4):
                pM, bM = ringA2.next()
                for pl in range(4):
                    for jj in range(4):
                        j = jq * 4 + jj
                        mm(pM[pl * 32:(pl + 1) * 32, jj * P:(jj + 1) * P], xb[:, ri, pl, L - 1 - j, :], identB[:],
                           True, True, [bxb, b_const], [bM], tp=(0, pl * 32))
                cp("scalar", W1[:, ri, q, jq * 4:(jq + 1) * 4, :].rearrange("p a b -> p (a b)"), pM[:, :], [bM], [b_W1], True)
        pump(1)
        for tq in range(L // 4):
            pM, bM = ringA2.next()
            for pl in range(4):
                p = q * 4 + pl
                for t4 in range(4):
                    tau = tq * 4 + t4
                    o = pM[pl * 32:(pl + 1) * 32, t4 * P + pl * 32: t4 * P + pl * 32 + 32]
                    mm(o, xb[:, 0, pl, tau, :], CT0[:, 0, p, :], True, False, [bxb, b_CT0], [bM], tp=(0, pl * 32))
                    mm(o, xb[:, 1, pl, tau, :], CT0[:, 1, p, :], False, True, [bxb, b_CT0], [bM], tp=(0, pl * 32))
            for pl in range(4):
                src = pM[pl * 32:(pl + 1) * 32, :].rearrange("p (t c) -> p t c", c=P)[:, :, pl * 32:pl * 32 + 32]
                cp("scalar" if pl % 2 else "vector", Kblk[pl * 32:(pl + 1) * 32, q, tq * 4:(tq + 1) * 4, pl * 32:pl * 32 + 32], src,
                   [bM], [b_Kblk], True)

    for q in range(4):
        ts("vector", Dblk[:, q, :], identF[:], cols[:, C_SD + q:C_SD + q + 1], None, ALU.mult, None, [b_const, b_cols], [b_Dblk], q > 0)
    for q in range(4):
        tt("vector", Kblk[:, q, 0, :], Kblk[:, q, 0, :], Dblk[:, q, :], ALU.add, [b_Kblk, b_Dblk], [b_Kblk])
    dbg_stop('s5c')
    lv2 = sbt("lv2", (P, 3, 16, NK)); b_lv2 = B("lv2")
    ts("vector", V(I_F8), V(I_TR), float(L), None, ALU.mult, None, Rq, W)
    round_to_int(V(I_TMP2), V(I_F8), V(I_TMP), Rq, W)
    tt("vector", V(I_F8), V(I_F8), V(I_TMP2), ALU.subtract, Rq, W)
    Rl = [b_lv2, b_s5t, b_jv, b_rot]; Wl = [b_lv2, b_rot]
    tt("vector", lv2[:, 0], bc3(V(I_F8), NK), jb3(NK), ALU.mult, Rl, Wl)
    sincos(rot[:, 1], rot[:, 0], lv2[:, 0], lv2[:, 1], lv2[:, 2], Rl, Wl)
    Rc = rot[:, 0]; Rs = rot[:, 1]
    s5u = sbt("s5u", (P, 3, 16)); b_s5u = B("s5u")
    ts("vector", s5u[:, 0], V(I_F8), float(NK), None, ALU.mult, None, [b_s5t], [b_s5u])
    sincos(V(I_SN), V(I_CN), s5u[:, 0], s5u[:, 1], s5u[:, 2], [b_s5u, b_s5t], [b_s5t, b_s5u])
    act(V(I_RHO), V(I_LRDT), AF.Exp, Rq, W, scale=float(L))
    cp("vector", rho_tab[:], bc3(V(I_RHO), NK), [b_s5t], [b_rho])
    dbg_stop('s5d')
    ts("vector", rho_tab[:, :, 0:1], rho_tab[:, :, 0:1], 0.0, None, ALU.mult, None, [b_rho], [b_rho])

    pump(40)
    dbg_stop('setup')
    pr.barrier(lambda e: e.memset(barr_t[:], 0.0))
    st_.close()

    def rms_stats(x_ap, nrows, st, bst, bx, junk_ap, bjunk):
        act(junk_ap, x_ap, AF.Square, [bx], [bjunk, bst], accum=st[0:nrows, 0:1])
        act(st[0:nrows, 1:2], st[0:nrows, 0:1], AF.Ln, [bst], [bst], scale=1.0 / D, bias=EPS)
        act(st[0:nrows, 1:2], st[0:nrows, 1:2], AF.Exp, [bst], [bst], scale=-0.5)

    def run(gen):
        for _ in gen:
            pass

    def proj_in(cx, hT, bh, n, lrux_t, blx, col0, ring=None):
        per = min(12, 512 // n)
        ot = 0
        while ot < 12:
            pM, bM = (ring or ringA).next()
            cnt = min(per, 12 - ot)
            for i in range(cnt):
                o = ot + i
                for k in range(KD):
                    mm(pM[:, i * n:(i + 1) * n], win_sb[:, k, o * P:(o + 1) * P], hT[:, k, 0:n], k == 0, k == KD - 1, [b_win, bh], [bM])
            for i in range(cnt):
                o = ot + i
                src = pM[:, i * n:(i + 1) * n]
                if o < 4:
                    cp("scalar", lrux_t[:, o, col0:col0 + n], src, [bM], [blx], True)
                elif o < 8:
                    act(cx.gg[:, o - 4, 0:n], src, AF.Gelu_apprx_tanh, [bM], [cx.b_gg], partial=True)
                else:
                    cp("scalar", cx.ub[:, o - 8, 0:n], src, [bM], [cx.b_ub], True)
            ot += cnt
            yield

    def lru_gates(cx, n, ring=None):
        conv, aa, bb_, convb = cx.conv, cx.aa, cx.bb, cx.convb
        b_conv, b_aa, b_bb, b_convb = cx.b_conv, cx.b_aa, cx.b_bb, cx.b_convb
        cp("scalar", convb[:, :, 0:n], conv[:, :, 0:n], [b_conv], [b_convb])
        yield
        rg = ring or ringA
        groups = [(0, 1), (2, 3)] if n > 128 else [(0, 1, 2, 3)]
        for grp in groups:
            dr, dbr = rg.next()
            di, dbi = rg.next()
            for i, t in enumerate(grp):
                o = i * n
                mm(dr[:, o:o + n], wa_blk[:, t, :], convb[:, t, 0:n], True, True, [b_wab, b_convb], [dbr])
                mm(di[:, o:o + n], wx_blk[:, t, :], convb[:, t, 0:n], True, True, [b_wxb, b_convb], [dbi])
            for i, t in enumerate(grp):
                o = i * n
                act(aa[:, t, 0:n], dr[:, o:o + n], AF.Sigmoid, [dbr, b_cols], [b_aa], bias=cols[:, C_BA + t:C_BA + t + 1], partial=True)
                act(bb_[:, t, 0:n], di[:, o:o + n], AF.Sigmoid, [dbi, b_cols], [b_bb], bias=cols[:, C_BX + t:C_BX + t + 1], partial=True)
            yield
        tt("gpsimd", bb_[:, :, 0:n], bb_[:, :, 0:n], conv[:, :, 0:n], ALU.mult, [b_bb, b_conv], [b_bb])
        for t in range(4):
            act(conv[:, t, 0:n], aa[:, t, 0:n], AF.Exp, [b_aa, b_lruc], [b_conv], scale=cl2[:, t:t + 1], partial=t > 0)
        yield
        for t in range(4):
            act(aa[:, t, 0:n], aa[:, t, 0:n], AF.Exp, [b_aa, b_lruc, b_conv], [b_aa], scale=cl[:, t:t + 1], partial=t > 0)
        yield
        ts("gpsimd", conv[:, :, 0:n], conv[:, :, 0:n], -1.0, 1.0, ALU.mult, ALU.add, [b_conv], [b_conv])
        act(conv[:, :, 0:n], conv[:, :, 0:n], AF.Ln, [b_conv], [b_conv], bias=1e-30)
        act(conv[:, :, 0:n], conv[:, :, 0:n], AF.Exp, [b_conv], [b_conv], scale=0.5)
        yield
        tt("gpsimd", bb_[:, :, 0:n], bb_[:, :, 0:n], conv[:, :, 0:n], ALU.mult, [b_bb, b_conv], [b_bb])
        yield

    def s5_glu_and_merge(cx, n, mg, bmg, ring=None):
        gy, sg, s5o, sq, rstd, gg = cx.gy, cx.sg, cx.s5o, cx.sq, cx.rstd, cx.gg
        b_gy, b_sg, b_s5o, b_sq, b_rstd, b_gg = cx.b_gy, cx.b_sg, cx.b_s5o, cx.b_sq, cx.b_rstd, cx.b_gg
        act(sq[:, 0:4, 0:n], gg[:, :, 0:n], AF.Square, [b_gg], [b_sq])
        yield
        per = max(1, min(4, 512 // n))
        for grp in range(0, 4, per):
            pZa, bZa = (ring or ringA).next()
            pZb, bZb = (ring or ringA).next()
            cnt = min(per, 4 - grp)
            for i in range(cnt):
                t = grp + i
                for k in range(4):
                    mm(pZa[:, i * n:(i + 1) * n], wglu_sb[:, k, t * P:(t + 1) * P], gy[:, k, 0:n], k == 0, k == 3, [b_wglu, b_gy], [bZa])
                for k in range(4):
                    mm(pZb[:, i * n:(i + 1) * n], wglu_sb[:, k, (4 + t) * P:(5 + t) * P], gy[:, k, 0:n], k == 0, k == 3, [b_wglu, b_gy], [bZb])
            for i in range(cnt):
                t = grp + i
                act(sg[:, t, 0:n], pZb[:, i * n:(i + 1) * n], AF.Sigmoid, [bZb], [b_sg], partial=True)
                tt("vector", s5o[:, t, 0:n], pZa[:, i * n:(i + 1) * n], sg[:, t, 0:n], ALU.mult, [bZa, b_sg], [b_s5o], True)
            yield
        tt("gpsimd", sq[:, 4:8, 0:n], s5o[:, :, 0:n], s5o[:, :, 0:n], ALU.mult, [b_s5o], [b_sq], True)
        pQ, bQ = (ring or ringA).next()
        for h in range(2):
            for t in range(4):
                mm(pQ[:, h * n:(h + 1) * n], onesB[:], sq[:, h * 4 + t, 0:n], t == 0, t == 3, [b_const, b_sq], [bQ])
        act(rstd[:, :, 0:n], pQ[:, 0:2 * n].rearrange("p (h n) -> p h n", h=2), AF.Ln, [bQ], [b_rstd], scale=1.0 / DL, bias=EPS)
        act(rstd[:, :, 0:n], rstd[:, :, 0:n], AF.Exp, [b_rstd], [b_rstd], scale=-0.5)
        yield
        tt("gpsimd", mg[:, 0:4, 0:n], gg[:, :, 0:n], rstd[:, 0, 0:n].unsqueeze(1).to_broadcast([P, 4, n]), ALU.mult,
           [b_gg, b_rstd], [bmg], True)
        tt("gpsimd", mg[:, 4:8, 0:n], s5o[:, :, 0:n], rstd[:, 1, 0:n].unsqueeze(1).to_broadcast([P, 4, n]), ALU.mult,
           [b_s5o, b_rstd], [bmg], True)
        yield

    def make_ctx(alloc, n, alias, pfx):
        cx = Ctx()
        def mk(name, shape, dt=F32):
            return alloc(pfx + name, shape, dt), B(pfx + name)
        cx.gg, cx.b_gg = mk("gg", (P, 4, n)); cx.ub, cx.b_ub = mk("ub", (P, 4, n), BF16)
        cx.conv, cx.b_conv = mk("conv", (P, 4, n)); cx.convb, cx.b_convb = mk("convb", (P, 4, n), BF16)
        cx.aa, cx.b_aa = mk("aa", (P, 4, n)); cx.bb, cx.b_bb = mk("bb", (P, 4, n))
        cx.gy, cx.b_gy = mk("gy", (P, 4, n), BF16)
        cx.sg, cx.b_sg = mk("sg", (P, 4, n))
        if alias:
            cx.s5o, cx.b_s5o = cx.bb, cx.b_bb
        else:
            cx.s5o, cx.b_s5o = mk("s5o", (P, 4, n))
        cx.sq, cx.b_sq = mk("sq", (P, 8, n), BF16); cx.rstd, cx.b_rstd = mk("rstd", (P, 2, n))
        return cx

    def clone_ctx_A(cx, alloc, n, pfx):
        c2 = Ctx()
        c2.__dict__.update(cx.__dict__)
        c2.gg = alloc(pfx + "gg", [P, 4, n], F32); c2.b_gg = B(pfx + "gg")
        c2.ub = alloc(pfx + "ub", [P, 4, n], BF16); c2.b_ub = B(pfx + "ub")
        return c2

    sp = contextlib.ExitStack()
    sbp = lambda name, shape, dt=F32: sp.enter_context(nc.sbuf_tensor(name, list(shape), dt))
    NXT = 2
    xt = [sbp("xt%d" % i, (P, D)) for i in range(NXT)]; b_xt = [B("xt%d" % i) for i in range(NXT)]
    ringX = Ring(list(zip(xt, b_xt)))
    stat = [sbp("stat%d" % i, (P, 4)) for i in range(4)]; b_stat = [B("stat%d" % i) for i in range(4)]
    ringStat = Ring(list(zip(stat, b_stat)))
    xn = [sbp("xn%d" % i, (P, D), BF16) for i in range(1)]; b_xn = [B("xn%d" % i) for i in range(1)]
    ringXn = Ring(list(zip(xn, b_xn)))
    hnT = sbp("hnT", (P, KD, CH), BF16); b_hnT = B("hnT")
    lrux = [sbp("lrux%d" % i, (P, 4, CH + 3)) for i in range(2)]; b_lrux = [B("lrux%d" % i) for i in range(2)]
    cx0 = make_ctx(sbp, CH, False, "p_")
    gg3 = [(cx0.gg, cx0.b_gg)] + [(sbp("gg%d" % i, (P, 4, CH)), B("gg%d" % i)) for i in (1, 2)]
    ub2 = [(cx0.ub, cx0.b_ub), (sbp("ub1", (P, 4, CH), BF16), B("ub1"))]

    class _CtxList:
        def __getitem__(self, c):
            cx = Ctx()
            cx.__dict__.update(cx0.__dict__)
            cx.gg, cx.b_gg = gg3[c % 3]
            cx.ub, cx.b_ub = ub2[c % 2]
            return cx
    cxs_p = _CtxList()
    hs = cx0.conv; b_hs = cx0.b_conv
    hcar = sbp("hcar", (P, 4)); b_hcar = B("hcar")
    s5e = sbp("s5e", (P, 4, 16, NK)); b_s5x = [B("s5e%d" % i) for i in range(4)]
    Hprev = sbp("Hprev", (P, 2, 16, NK), BF16); b_Hprev = B("Hprev")
    carry = sbp("carry", (P, 4, 16)); b_carry = B("carry")
    ctmp = sbp("ctmp", (P, 4, 16)); b_ctmp = B("ctmp")
    mrg2 = [sbp("mrg%d" % i, (P, KD, CH), BF16) for i in range(2)]; b_mrg2 = [B("mrg%d" % i) for i in range(2)]
    xres = [sbp("xres%d" % i, (P, 512)) for i in range(2)]; b_xres = [B("xres%d" % i) for i in range(2)]
    ringXres = Ring(list(zip(xres, b_xres)))

    mset("vector", carry[:], 0.0, [b_carry])
    mset("vector", lrux[0][:], 0.0, [b_lrux[0]])
    V_RHO = V(I_RHO); V_CN = V(I_CN); V_SN = V(I_SN)
    fl = lambda a: a.rearrange("p a m -> p (a m)")
    qv = lambda a, pl: a.rearrange("p (q pl) m -> p pl q m", pl=4)[:, pl]
    x1_tiles = {}

    def genA(c):
        cx = cxs_p[c]
        for tt_i in range(CH // P):
            tok0 = c * CH + tt_i * P
            xT, bx = ringX.next()
            pr.dma("sync", xT[:], xp_d[tok0:tok0 + P, :], writes=[bx])
            st, bst = ringStat.next()
            xnT, bxn = ringXn.next()
            rms_stats(xT[:], P, st, bst, bx, xnT[:], bxn)
            act(xnT[:], xT[:], AF.Identity, [bx, bst], [bxn], scale=st[:, 1:2])
            yield
            pTr, bTr = ringT.next()
            for k in range(KD):
                tr(pTr[:, k * P:(k + 1) * P], xnT[:, k * P:(k + 1) * P], identB[:], [bxn, b_const], [bTr])
            yield
            for k in range(KD):
                act(hnT[:, k, tt_i * P:(tt_i + 1) * P], pTr[:, k * P:(k + 1) * P], AF.Identity, [bTr, b_pmod], [b_hnT],
                    scale=pmod[:, k:k + 1], bias=pmod[:, 8 + k:9 + k], partial=True)
                if k % 4 == 3:
                    yield
        yield from proj_in(cx, hnT, b_hnT, CH, lrux[c % 2], b_lrux[c % 2], 3, ring=ringA01)

    def genB(c):
        cx = cxs_p[c]
        lx = lrux[c % 2]; blx = b_lrux[c % 2]
        lxn = lrux[(c + 1) % 2]; blxn = b_lrux[(c + 1) % 2]
        cp("gpsimd", lxn[:, :, 0:3], lx[:, :, CH:CH + 3], [blx], [blxn], True)
        conv = cx.conv; b_conv = cx.b_conv
        for t in range(4):
            ts("vector", conv[:, t, :], lx[:, t, 0:CH], cols[:, C_CW + t:C_CW + t + 1], cols[:, C_CB + t:C_CB + t + 1], ALU.mult, ALU.add,
               [blx, b_cols], [b_conv], t > 0)
            for k in range(1, 4):
                stt(conv[:, t, :], lx[:, t, k:k + CH], cols[:, C_CW + 4 * k + t:C_CW + 4 * k + t + 1], conv[:, t, :], ALU.mult, ALU.add,
                    [blx, b_cols, b_conv], [b_conv], True)
            yield
        yield from lru_gates(cx, CH, ring=ringA23)
        for t in range(4):
            init = 0.0 if c == 0 else hcar[:, t:t + 1]
            scan(hs[:, t, :], cx.aa[:, t, :], cx.bb[:, t, :], init, [cx.b_aa, cx.b_bb, b_hcar], [b_hs], t > 0)
            if t % 2 == 1:
                yield
        cp("vector", hcar[:], hs[:, :, CH - 1], [b_hs], [b_hcar])
        tt("gpsimd", cx.gg[:], cx.gg[:], hs[:], ALU.mult, [cx.b_gg, b_hs], [cx.b_gg])
        yield

    def genC(c):
        cx = cxs_p[c]
        ub = cx.ub; b_ub = cx.b_ub
        Epr = s5e[:, 0]; Epi = s5e[:, 1]; t1 = s5e[:, 2]; t2 = s5e[:, 3]
        bEpr, bEpi, bT1, bT2 = b_s5x
        for ri in range(2):
            for plh in range(2):
                banks = {2 * plh: ringA23.next(), 2 * plh + 1: ringA23.next()}
                for q in range(4):
                    for pl in (2 * plh, 2 * plh + 1):
                        pE, bE = banks[pl]
                        uv = ub[pl * 32:(pl + 1) * 32, q, :].rearrange("p (m j) -> p j m", j=L)
                        for j in range(L):
                            mm(pE[:, q * NK:(q + 1) * NK], W1[pl * 32:(pl + 1) * 32, ri, q, j, :], uv[:, j, :], j == 0, j == L - 1,
                               [b_W1, b_ub], [bE], tp=(pl * 32, 0))
                for pl in (2 * plh, 2 * plh + 1):
                    pE, bE = banks[pl]
                    Ev = pE[:, 0:4 * NK].rearrange("p (q m) -> p q m", m=NK)
                    if ri == 0:
                        tt("vector", qv(Epr, pl), Ev, qv(Rc, pl), ALU.mult, [bE, b_rot], [bEpr], True)
                        tt("vector", qv(t2, pl), Ev, qv(Rs, pl), ALU.mult, [bE, b_rot], [bT2], True)
                    else:
                        tt("vector", qv(t1, pl), Ev, qv(Rs, pl), ALU.mult, [bE, b_rot], [bT1], True)
                        tt("vector", qv(Epi, pl), Ev, qv(Rc, pl), ALU.mult, [bE, b_rot], [bEpi], True)
                yield
        tt("vector", Epr, Epr, t1, ALU.add, [bEpr, bT1], [bEpr])
        tt("gpsimd", Epi, Epi, t2, ALU.subtract, [bEpi, bT2], [bEpi])
        yield
        tt("vector", ctmp[:, 0], carry[:, 2], V_RHO, ALU.mult, [b_carry, b_s5t], [b_ctmp])
        tt("vector", ctmp[:, 1], carry[:, 3], V_RHO, ALU.mult, [b_carry, b_s5t, b_ctmp], [b_ctmp])
        tt("vector", Epr[:, :, 0], Epr[:, :, 0], ctmp[:, 0], ALU.add, [bEpr, b_ctmp], [bEpr])
        tt("vector", Epi[:, :, 0], Epi[:, :, 0], ctmp[:, 1], ALU.add, [bEpi, b_ctmp], [bEpi])
        yield
        scan(fl(t1), fl(rho_tab[:]), fl(Epr), 0.0, [bEpr, b_rho], [bT1])
        scan(fl(t2), fl(rho_tab[:]), fl(Epi), 0.0, [bEpi, b_rho], [bT2])
        yield
        cp("vector", Hprev[:, 0, :, 0], carry[:, 0], [b_carry], [b_Hprev])
        cp("vector", Hprev[:, 1, :, 0], carry[:, 1], [b_carry, b_Hprev], [b_Hprev])
        gl_r = t1[:, :, NK - 1]; gl_i = t2[:, :, NK - 1]
        RC = [bT1, bT2, b_s5t, b_ctmp]
        tt("vector", ctmp[:, 0], gl_r, V_CN, ALU.mult, RC, [b_ctmp])
        tt("vector", ctmp[:, 1], gl_i, V_SN, ALU.mult, RC, [b_ctmp])
        tt("vector", ctmp[:, 2], gl_i, V_CN, ALU.mult, RC, [b_ctmp])
        tt("vector", ctmp[:, 3], gl_r, V_SN, ALU.mult, RC, [b_ctmp])
        yield
        tt("vector", carry[:, 2], ctmp[:, 0], ctmp[:, 1], ALU.subtract, [b_ctmp, b_carry], [b_carry])
        tt("vector", carry[:, 3], ctmp[:, 2], ctmp[:, 3], ALU.add, [b_ctmp, b_carry], [b_carry])
        tt("vector", Epr, t1, Rc, ALU.mult, [bT1, b_rot], [bEpr])
        tt("gpsimd", Epi, t2, Rc, ALU.mult, [bT2, b_rot], [bEpi])
        yield
        tt("vector", t1, t1, Rs, ALU.mult, [bT1, b_rot], [bT1])
        tt("gpsimd", t2, t2, Rs, ALU.mult, [bT2, b_rot], [bT2])
        yield
        tt("vector", Hprev[:, 0, :, 1:NK], Epr[:, :, 0:NK - 1], t2[:, :, 0:NK - 1], ALU.subtract, [bEpr, bT2, b_Hprev], [b_Hprev])
        tt("gpsimd", Hprev[:, 1, :, 1:NK], Epi[:, :, 0:NK - 1], t1[:, :, 0:NK - 1], ALU.add, [bEpi, bT1, b_Hprev], [b_Hprev])
        yield
        tt("vector", carry[:, 0], Epr[:, :, NK - 1], t2[:, :, NK - 1], ALU.subtract, [bEpr, bT2, b_carry], [b_carry])
        tt("vector", carry[:, 1], Epi[:, :, NK - 1], t1[:, :, NK - 1], ALU.add, [bEpi, bT1, b_carry], [b_carry])
        yield
        gy = cx.gy; b_gy = cx.b_gy
        for qq in range(2):
            pY, bY = ringA23.next()
            for qi in range(2):
                q = qq * 2 + qi
                uvq = ub[:, q, :].rearrange("p (m j) -> p j m", j=L)
                for j in range(L):
                    o0 = qi * CH + j * NK
                    for i in range(j + 1):
                        mm(pY[:, o0:o0 + NK], Kblk[:, q, j - i, :], uvq[:, i, :], i == 0, False, [b_Kblk, b_ub], [bY])
                    for pl in range(4):
                        p = q * 4 + pl
                        mm(pY[pl * 32:(pl + 1) * 32, o0:o0 + NK], CAb[:, 0, p, j, :], Hprev[:, 0, p, :], False, False,
                           [b_CAb, b_Hprev], [bY], tp=(0, pl * 32))
                        mm(pY[pl * 32:(pl + 1) * 32, o0:o0 + NK], CAb[:, 1, p, j, :], Hprev[:, 1, p, :], False, True,
                           [b_CAb, b_Hprev], [bY], tp=(0, pl * 32))
            for qi in range(2):
                q = qq * 2 + qi
                src = pY[:, qi * CH:(qi + 1) * CH].rearrange("p (j m) -> p j m", j=L)
                act(gy[:, q, :].rearrange("p (m j) -> p j m", j=L), src, AF.Gelu_apprx_tanh, [bY], [b_gy], partial=q > 0)
            yield

    def genD1(c):
        yield from s5_glu_and_merge(cxs_p[c], CH, mrg2[c % 2], b_mrg2[c % 2], ring=ringA23)

    def genD2(c):
        mrg = mrg2[c % 2]; b_mrg = b_mrg2[c % 2]
        for tt_i in range(CH // P):
            tok0 = c * CH + tt_i * P
            x1_tiles[tok0] = [B("x1s_%d_0" % tok0), B("x1s_%d_1" % tok0)]
            for h in range(2):
                bscr = x1_tiles[tok0][h]
                xr_, bxr = ringXres.next()
                pr.dma("sync", xr_[:], xp_d[tok0:tok0 + P, h * 512:(h + 1) * 512], writes=[bxr])
                pO, bO = ringD.next()
                for k in range(KD):
                    mm(pO[:, :], mrg[:, k, tt_i * P:(tt_i + 1) * P], wout_sb[:, k, h * 512:(h + 1) * 512], k == 0, k == KD - 1,
                       [b_mrg, b_wout], [bO])
                yield
                tt("vector", xr_[:], pO[:, :], xr_[:], ALU.add, [bO, bxr], [bxr])
                pr.dma("sync", x1_d[tok0:tok0 + P, h * 512:(h + 1) * 512], xr_[:], reads=[bxr], writes=[bscr])
                yield

    adawM = [sbp("adawM%d" % i, (P, KD, P)) for i in range(2)]; b_adawM = [B("adawM%d" % i) for i in range(2)]

    def genM(k):
        for jj in range(4 * k, 4 * k + 4):
            col0 = 3072 + jj * P
            buf = adawM[jj % 2]; bb = b_adawM[jj % 2]
            pr.dma("sync", buf[:], adaw_v[:, :, col0:col0 + P], writes=[bb])
            yield
            pM, bM = ringA23.next()
            for kk in range(KD):
                mm(pM[:, 0:NB], buf[:, kk, :], cTf[:, kk, :], kk == 0, kk == KD - 1, [b_cTf, bb], [bM])
            c0 = 24 + jj
            tt("vector", modT[:, c0, :], pM[:, 0:NB], cols[:, C_ADB + c0:C_ADB + c0 + 1].to_broadcast([P, NB]), ALU.add,
               [bM, b_cols], [b_modT], True)
            yield
        if k == 5:
            stt(pmod[:, 16:24], modT[:, M_SC2:M_SC2 + 8, NS], 1.0, cols[:, C_N2G:C_N2G + 8], ALU.add, ALU.mult, [b_modT, b_cols], [b_pmod], True)
            cp("vector", pmod[:, 24:32], modT[:, M_SH2:M_SH2 + 8, NS], [b_modT], [b_pmod], True)
            yield

    def rr(named):
        alive = {nm: g for nm, g in named}
        tlast = {nm: 0.0 for nm, _ in named}
        while alive:
            nm = min(alive, key=lambda k_: tlast[k_])
            pr.step_max = 0.0
            try:
                next(alive[nm])
                tlast[nm] = max(tlast[nm], pr.step_max)
            except StopIteration:
                del alive[nm]

    run(genA(0))
    for k in range(NCH + 2):
        streams = []
        if 0 <= k - 1 < NCH:
            streams.append(("D1", genD1(k - 1)))
        if k < NCH:
            streams.append(("C", genC(k)))
            streams.append(("B", genB(k)))
        if 0 <= k - 2 < NCH:
            streams.append(("D2", genD2(k - 2)))
        if k + 1 < NCH:
            streams.append(("A", genA(k + 1)))
        if k < 6:
            streams.append(("M", genM(k)))
        rr(streams)

    outst = xt[0]; b_outst = b_xt[0]
    lastc = NCH - 1
    lxl = lrux[lastc % 2]; blxl = b_lrux[lastc % 2]
    pS_, bS_ = ringS.next()
    for t in range(4):
        tr(pS_[0:3, t * P:(t + 1) * P], lxl[:, t, CH:CH + 3], identF[:], [blxl, b_const], [bS_])
    cp("vector", outst[0:3, 0:512], pS_[0:3, :], [bS_], [b_outst])
    pr.dma("sync", convp_d[:, :], outst[0:3, 0:512], reads=[b_outst])
    pS_, bS_ = ringS.next()
    tr(pS_[0:4, 0:P], hcar[:], identF[:], [b_hcar, b_const], [bS_])
    tr(pS_[0:16, P:2 * P], carry[:, 0], identF[:], [b_carry, b_const], [bS_])
    tr(pS_[0:16, 2 * P:3 * P], carry[:, 1], identF[:], [b_carry, b_const], [bS_])
    b_outst2 = B("outst2")
    cp("vector", outst[0:16, 512:512 + 3 * P], pS_[0:16, 0:3 * P], [bS_], [b_outst2])
    pr.dma("sync", lrup_d[:, :], outst[0:4, 512:512 + P], reads=[b_outst2])
    pr.dma("sync", s5rp_d[:, :], outst[0:16, 512 + P:512 + 2 * P], reads=[b_outst2])
    pr.dma("sync", s5ip_d[:, :], outst[0:16, 512 + 2 * P:512 + 3 * P], reads=[b_outst2])

    dbg_stop('prompt')
    pr.barrier(lambda e: e.memset(barr_t[:], 0.0))
    sp.close()

    ss = contextlib.ExitStack()
    sbs = lambda name, shape, dt=F32: ss.enter_context(nc.sbuf_tensor(name, list(shape), dt))
    n = NS
    cxs = make_ctx(sbs, NS, False, "s_")
    xs_sb = sbs("xs_sb", (NS, D)); b_xs = B("xs")
    pr.dma("sync", xs_sb[:], xs_d[:, :], writes=[b_xs])
    sst = sbs("sst", (NS, 4)); b_sst = B("sst")
    stmp = sbs("stmp", (NS, D)); b_stmp = B("stmp")
    hsb = sbs("hsb", (NS, D), BF16); b_hsb = B("hsb")
    smod = sbs("smod", (P, 3, KD, NS)); b_smod = B("smod")
    hTs = sbs("hTs", (P, KD, NS), BF16); b_hTs = B("hTs")
    g1s = sbs("g1s", (NS, D)); b_g1s = B("g1s")
    x1s = sbs("x1s", (NS, D)); b_x1s = B("x1s")
    outs = sbs("outs", (NS, 1024)); b_outs = B("outs")

    def sample_norm_mod(x_ap, bx, gcol0, sc_off, sh_off, dstT, bdst):
        rms_stats(x_ap, NS, sst, b_sst, bx, hsb[:], b_hsb)
        ts("vector", hsb[:], x_ap, sst[:, 1:2], None, ALU.mult, None, [bx, b_sst], [b_hsb])
        pTr, bTr = ringT.next()
        for k in range(KD):
            tr(pTr[:, k * NS:(k + 1) * NS], hsb[:, k * P:(k + 1) * P], identB[0:NS, 0:NS], [b_hsb, b_const], [bTr])
        stt(smod[:, 0], modT[:, sc_off:sc_off + 8, 0:NS], 1.0, cols[:, gcol0:gcol0 + 8].unsqueeze(2).to_broadcast([P, KD, NS]),
            ALU.add, ALU.mult, [b_modT, b_cols], [b_smod])
        tt("vector", smod[:, 1], pTr[:, 0:KD * NS].rearrange("p (k n) -> p k n", n=NS), smod[:, 0], ALU.mult, [bTr, b_smod], [b_smod])
        tt("vector", dstT[:, :, :], smod[:, 1], modT[:, sh_off:sh_off + 8, 0:NS], ALU.add, [b_smod, b_modT], [bdst])

    def gate_rows(dst, bdst, goff):
        for h in range(2):
            pM, bM = ringA.next()
            for i in range(4):
                k = h * 4 + i
                tr(pM[0:NS, i * P:(i + 1) * P], modT[:, goff + k, 0:NS], identF[:], [b_modT, b_const], [bM])
            cp("vector", dst[:, h * 512:(h + 1) * 512], pM[0:NS, :], [bM], [bdst], h > 0)

    sconv_sb = sbs("sconv_sb", (NS, 3 * DL)); b_sconv = B("sconv")
    pr.dma("sync", sconv_sb[:], sconv_d.rearrange("b k c -> b (k c)"), writes=[b_sconv])
    slru_sb = sbs("slru_sb", (NS, DL)); b_slru = B("slru")
    pr.dma("sync", slru_sb[:], slru_d[:, :], writes=[b_slru])
    s5io = [sbs("s5io%d" % i, (NS, 2048)) for i in range(2)]; b_s5io = [B("s5io%d" % i) for i in range(2)]
    for ri, sd in enumerate((ss5r_d, ss5i_d)):
        pr.dma("sync", s5io[ri][:], sd[:, :], writes=[b_s5io[ri]])
    for k in range(KD):
        pr.dma("gpsimd", wout_sb[:, k, :], wout_v[:, k, :], writes=[b_wout], partial=k > 0)
    pr.dma("sync", convs_d[:, 0:2, :].rearrange("b k c -> b (k c)"), sconv_sb[:, DL:3 * DL], reads=[b_sconv])

    sample_norm_mod(xs_sb[:], b_xs, C_N1G, M_SC1, M_SH1, hTs, b_hTs)
    lxs = sbs("lxs", (P, 4, NS)); blxs = B("lxs")
    run(proj_in(cxs, hTs, b_hTs, NS, lxs, blxs, 0))
    sstT = sbs("sstT", (P, 16, NS)); b_sstT = B("sstT")
    hss = sbs("hss", (P, 4, NS)); bhss = B("hss")
    h0 = sbs("h0", (P, 2, 16, NS)); b_h0 = B("h0")
    hn_ = sbs("hn_", (P, 2, 16, NS)); b_hn = B("hn_")
    hnb = sbs("hnb", (P, 2, 16, NS), BF16); b_hnb = B("hnb")
    htmp = sbs("htmp", (P, 2, 16, NS)); b_htmp = B("htmp")
    ubs = cxs.ub; b_ubs = cxs.b_ub

    def gen_s_lru():
        pS_, bS_ = ringS.next()
        for i in range(12):
            tr(pS_[:, i * NS:(i + 1) * NS], sconv_sb[:, i * P:(i + 1) * P], identF[0:NS, 0:NS], [b_sconv, b_const], [bS_])
        for i in range(4):
            tr(pS_[:, (12 + i) * NS:(13 + i) * NS], slru_sb[:, i * P:(i + 1) * P], identF[0:NS, 0:NS], [b_slru, b_const], [bS_])
        cp("vector", sstT[:].rearrange("p a n -> p (a n)"), pS_[:, 0:16 * NS], [bS_], [b_sstT])
        yield
        conv = cxs.conv; b_conv = cxs.b_conv
        for t in range(4):
            ts("vector", conv[:, t, :], sstT[:, t, :], cols[:, C_CW + t:C_CW + t + 1], cols[:, C_CB + t:C_CB + t + 1], ALU.mult, ALU.add,
               [b_sstT, b_cols], [b_conv], t > 0)
            for k in range(1, 3):
                stt(conv[:, t, :], sstT[:, k * 4 + t, :], cols[:, C_CW + 4 * k + t:C_CW + 4 * k + t + 1], conv[:, t, :], ALU.mult, ALU.add,
                    [b_sstT, b_cols, b_conv], [b_conv], True)
            stt(conv[:, t, :], lxs[:, t, :], cols[:, C_CW + 12 + t:C_CW + 12 + t + 1], conv[:, t, :], ALU.mult, ALU.add,
                [blxs, b_cols, b_conv], [b_conv], True)
            yield
        yield from lru_gates(cxs, NS)
        tt("vector", hss[:], cxs.aa[:], sstT[:, 12:16, :], ALU.mult, [cxs.b_aa, b_sstT], [bhss])
        tt("vector", hss[:], hss[:], cxs.bb[:], ALU.add, [bhss, cxs.b_bb], [bhss])
        tt("vector", cxs.gg[:], cxs.gg[:], hss[:], ALU.mult, [cxs.b_gg, bhss], [cxs.b_gg])
        yield
        pS_, bS_ = ringS.next()
        for t in range(4):
            tr(pS_[0:NS, t * P:(t + 1) * P], lxs[:, t, :], identF[:], [blxs, b_const], [bS_])
        cp("vector", outs[:, 0:512], pS_[0:NS, :], [bS_], [b_outs])
        pr.dma("sync", convs_d[:, 2, :], outs[:, 0:512], reads=[b_outs])
        yield
        pS_, bS_ = ringS.next()
        for t in range(4):
            tr(pS_[0:NS, t * P:(t + 1) * P], hss[:, t, :], identF[:], [bhss, b_const], [bS_])
        b_outs2 = B("outs2")
        cp("vector", outs[:, 512:1024], pS_[0:NS, :], [bS_], [b_outs2])
        pr.dma("sync", lrus_d[:, :], outs[:, 512:1024], reads=[b_outs2])
        yield

    def gen_s_s5():
        gate_rows(g1s, b_g1s, M_G1)
        yield
        for ri in range(2):
            pS_, bS_ = ringS.next()
            for p in range(16):
                tr(pS_[:, p * NS:(p + 1) * NS], s5io[ri][:, p * P:(p + 1) * P], identF[0:NS, 0:NS], [b_s5io[ri], b_const], [bS_])
            cp("vector", h0[:, ri].rearrange("p a n -> p (a n)"), pS_[:, 0:16 * NS], [bS_], [b_h0], ri > 0)
            yield
        abr3 = V(I_ABR).unsqueeze(2).to_broadcast([P, 16, NS]); abi3 = V(I_ABI).unsqueeze(2).to_broadcast([P, 16, NS])
        tt("vector", hn_[:, 0], h0[:, 0], abr3, ALU.mult, [b_h0, b_s5t], [b_hn])
        tt("gpsimd", htmp[:, 0], h0[:, 1], abi3, ALU.mult, [b_h0, b_s5t], [b_htmp])
        yield
        tt("vector", hn_[:, 0], hn_[:, 0], htmp[:, 0], ALU.subtract, [b_hn, b_htmp], [b_hn])
        tt("vector", hn_[:, 1], h0[:, 1], abr3, ALU.mult, [b_h0, b_s5t, b_hn], [b_hn])
        tt("gpsimd", htmp[:, 1], h0[:, 0], abi3, ALU.mult, [b_h0, b_s5t, b_htmp], [b_htmp])
        yield
        tt("vector", hn_[:, 1], hn_[:, 1], htmp[:, 1], ALU.add, [b_hn, b_htmp], [b_hn])
        qs = lambda a, pl: a.rearrange("p (q pl) n -> p pl q n", pl=4)[:, pl]
        for ri in range(2):
            banks = [ringA.next() for _ in range(4)]
            for q in range(4):
                for pl in range(4):
                    pBu, bBu = banks[pl]
                    mm(pBu[:, q * NS:(q + 1) * NS], W1[pl * 32:(pl + 1) * 32, ri, q, L - 1, :], ubs[pl * 32:(pl + 1) * 32, q, :], True, True,
                       [b_W1, b_ubs], [bBu], tp=(pl * 32, 0))
            for pl in range(4):
                pBu, bBu = banks[pl]
                tt("vector", qs(hn_[:, ri], pl), qs(hn_[:, ri], pl), pBu[:, 0:4 * NS].rearrange("p (q n) -> p q n", n=NS), ALU.add,
                   [b_hn, bBu], [b_hn])
            yield
        cp("vector", hnb[:], hn_[:], [b_hn], [b_hnb])
        pY, bY = ringA.next()
        for q in range(4):
            mm(pY[:, q * NS:(q + 1) * NS], Dblk[:, q, :], ubs[:, q, :], True, False, [b_Dblk, b_ubs], [bY])
            for pl in range(4):
                p = q * 4 + pl
                o = pY[pl * 32:(pl + 1) * 32, q * NS:(q + 1) * NS]
                mm(o, CT0[:, 0, p, :], hnb[:, 0, p, :], False, False, [b_CT0, b_hnb], [bY], tp=(0, pl * 32))
                mm(o, CT0[:, 1, p, :], hnb[:, 1, p, :], False, True, [b_CT0, b_hnb], [bY], tp=(0, pl * 32))
        act(cxs.gy[:, :, :], pY[:, 0:4 * NS].rearrange("p (q n) -> p q n", n=NS), AF.Gelu_apprx_tanh, [bY], [cxs.b_gy])
        yield
        for ri, sd in enumerate((s5rs_d, s5is_d)):
            for g4 in range(4):
                pS_, bS_ = ringS.next()
                for i in range(4):
                    p = g4 * 4 + i
                    tr(pS_[0:NS, i * P:(i + 1) * P], hn_[:, ri, p, :], identF[:], [b_hn, b_const], [bS_])
                cp("scalar", s5io[ri][:, g4 * 512:(g4 + 1) * 512], pS_[0:NS, :], [bS_], [b_s5io[ri]], g4 > 0)
                yield
            pr.dma("sync", sd[:, :], s5io[ri][:], reads=[b_s5io[ri]])

    def rr_plain(gens):
        gens = list(gens)
        while gens:
            for g in list(gens):
                try:
                    next(g)
                except StopIteration:
                    gens.remove(g)

    rr_plain([gen_s_lru(), gen_s_s5()])
    mgs = sbs("mgs", (P, KD, NS), BF16); bmgs = B("mgs")
    run(s5_glu_and_merge(cxs, NS, mgs, bmgs))
    for k in range(KD):
        gc = (C_GLO + k) if k < 4 else (C_GSO + k - 4)
        ts("vector", wout_sb[:, k, :], wout_sb[:, k, :], cols[:, gc:gc + 1], None, ALU.mult, None, [b_wout, b_cols], [b_wout])
    for h in range(2):
        pO, bO = ringA.next()
        for k in range(KD):
            mm(pO[0:NS, :], mgs[:, k, :], wout_sb[:, k, h * 512:(h + 1) * 512], k == 0, k == KD - 1, [bmgs, b_wout], [bO])
        tt("vector", stmp[:, h * 512:(h + 1) * 512], pO[0:NS, :], g1s[:, h * 512:(h + 1) * 512], ALU.mult,
           [bO, b_g1s], [b_stmp], h > 0)
    tt("vector", x1s[:], xs_sb[:], stmp[:], ALU.add, [b_xs, b_stmp], [b_x1s])
    pr.dma("sync", x1_d[T:T + NS, :], x1s[:], reads=[b_x1s], writes=[b_x1s_scr])
    sample_norm_mod(x1s[:], b_x1s, C_N2G, M_SC2, M_SH2, hn2Ts, b_hn2Ts)

    dbg_stop('sample')
    pr.barrier(lambda e: e.memset(barr_t[:], 0.0))
    ss.close()
    s1.close()
    s2 = contextlib.ExitStack()
    sb2 = lambda name, shape, dt=F32: s2.enter_context(nc.sbuf_tensor(name, list(shape), dt))
    wg_sb = sb2("wg_sb", (P, KD, DFF), BF16)
    wu_sb = sb2("wu_sb", (P, KD, DFF), BF16)
    wd_sb = sb2("wd_sb", (P, NF, D), BF16)
    NBLK = (DFF + 511) // 512
    b_wg = [B("wg%d" % i) for i in range(NBLK)]; b_wu = [B("wu%d" % i) for i in range(NBLK)]
    b_wd = [B("wd%d" % i) for i in range(NF // 2)]
    wg_v = wg_d.rearrange("(k p) n -> p k n", p=P)
    wu_v = wu_d.rearrange("(k p) n -> p k n", p=P)
    wd_v = wd_d.rearrange("(f p) n -> p f n", p=P)
    for blk in range(NBLK):
        c0 = blk * 512; c1 = min(DFF, c0 + 512)
        pr.dma("gpsimd", wg_sb[:, :, c0:c1], wg_v[:, :, c0:c1], writes=[b_wg[blk]])
        pr.dma("gpsimd", wu_sb[:, :, c0:c1], wu_v[:, :, c0:c1], writes=[b_wu[blk]])
    for i in range(NF // 2):
        pr.dma("gpsimd", wd_sb[:, 2 * i:2 * i + 2, :], wd_v[:, 2 * i:2 * i + 2, :], writes=[b_wd[i]])

    G2bc = sb2("G2bc", (P, D)); FNGbc = sb2("FNGbc", (P, D)); b_G2 = B("G2bc"); b_FNG = B("FNGbc")
    xa = [sb2("xa%d" % i, (P, D)) for i in range(2)]; b_xa = [B("xa%d" % i) for i in range(2)]
    xr = [sb2("xr%d" % i, (P, D)) for i in range(2)]; b_xr = [B("xr%d" % i) for i in range(2)]
    ringXa = Ring(list(zip(xa, b_xa))); ringXr = Ring(list(zip(xr, b_xr)))
    stat2 = [sb2("stat2_%d" % i, (P, 4)) for i in range(4)]; b_stat2 = [B("stat2_%d" % i) for i in range(4)]
    ringStat2 = Ring(list(zip(stat2, b_stat2)))
    xn2 = [sb2("xn2_%d" % i, (P, D), BF16) for i in range(2)]; b_xn2 = [B("xn2_%d" % i) for i in range(2)]
    ringXn2 = Ring(list(zip(xn2, b_xn2)))
    hn2T = [sb2("hn2T%d" % i, (P, KD, CH2), BF16) for i in range(2)]; b_hn2T = [B("hn2T%d" % i) for i in range(2)]
    actT = sb2("actT", (P, NF, CH2), BF16); b_actT = B("actT")
    sil = [sb2("sil%d" % i, (P, CH2), BF16) for i in range(1)]; b_sil = [B("sil%d" % i) for i in range(1)]
    ringSil = Ring(list(zip(sil, b_sil)))
    tmp2x = sb2("tmp2x", (P, 512)); b_tmp2x = B("tmp2x")

    bcast_rows(G2bc, b_G2, modT[:, M_G2:M_G2 + 8, NS], b_modT, P, tmp2x, b_tmp2x)
    bcast_rows(FNGbc, b_FNG, cols[:, C_FNG:C_FNG + 8], b_cols, P, tmp2x, b_tmp2x)

    actTs = sb2("actTs", (P, NF, NS), BF16); b_actTs = B("actTs")

    def gen_gate_up(hT_ap, bh, n, with_sample=False):
        for f in range(NF):
            if with_sample:
                pGs, bGs = ringAll.next()
                for k in range(KD):
                    mm(pGs[:, 0:NS], wg_sb[:, k, f * P:(f + 1) * P], hn2Ts[:, k, :], k == 0, k == KD - 1, [b_wg[f // 4], b_hn2Ts], [bGs])
                for k in range(KD):
                    mm(pGs[:, NS:2 * NS], wu_sb[:, k, f * P:(f + 1) * P], hn2Ts[:, k, :], k == 0, k == KD - 1, [b_wu[f // 4], b_hn2Ts], [bGs])
                sls, bsls = ringSil.next()
                act(sls[:, 0:NS], pGs[:, 0:NS], AF.Silu, [bGs], [bsls])
                tt("vector", actTs[:, f, :], pGs[:, NS:2 * NS], sls[:, 0:NS], ALU.mult, [bGs, bsls], [b_actTs], f > 0)
            pG, bG = ringAll.next()
            pU, bU = ringAll.next()
            for k in range(KD):
                mm(pG[:, 0:n], wg_sb[:, k, f * P:(f + 1) * P], hT_ap[:, k, :], k == 0, k == KD - 1, [b_wg[f // 4], bh], [bG])
            for k in range(KD):
                mm(pU[:, 0:n], wu_sb[:, k, f * P:(f + 1) * P], hT_ap[:, k, :], k == 0, k == KD - 1, [b_wu[f // 4], bh], [bU])
            sl, bsl = ringSil.next()
            act(sl[:, 0:n], pG[:, 0:n], AF.Silu, [bG], [bsl])
            tt("vector", actT[:, f, 0:n], pU[:, 0:n], sl[:, 0:n], ALU.mult, [bU, bsl], [b_actT], f > 0)
            yield

    def gen_down(x_ap, bx, rows, col0, g2_ap, bg2, out_ap, junk_ap, bjunk, act_src=None):
        aT, b_aT = act_src if act_src is not None else (actT, b_actT)
        for h in range(2):
            pO, bO = ringAll.next()
            for f in range(NF):
                mm(pO[0:rows, :], aT[:, f, col0:col0 + rows], wd_sb[:, f, h * 512:(h + 1) * 512], f == 0, f == NF - 1,
                   [b_aT, b_wd[f // 2]], [bO])
            tt("vector", tmp2x[0:rows, :], pO[0:rows, :], g2_ap[0:rows, h * 512:(h + 1) * 512], ALU.mult, [bO, bg2], [b_tmp2x])
            tt("gpsimd", x_ap[:, h * 512:(h + 1) * 512], x_ap[:, h * 512:(h + 1) * 512], tmp2x[0:rows, :], ALU.add, [bx, b_tmp2x], [bx])
            yield
        st, bst = ringStat2.next()
        rms_stats(x_ap, rows, st, bst, bx, junk_ap, bjunk)
        stt(x_ap, x_ap, st[0:rows, 1:2], FNGbc[0:rows, :], ALU.mult, ALU.mult, [bx, bst, b_FNG], [bx])
        pr.dma("sync", out_ap, x_ap, reads=[bx])
        yield

    def gen_norm2(c):
        hT = hn2T[c % 2]; bh = b_hn2T[c % 2]
        for tt_i in range(CH2 // P):
            tok0 = c * CH2 + tt_i * P
            xT, bx = ringXa.next()
            pr.dma("sync", xT[:], x1_d[tok0:tok0 + P, :], reads=x1_tiles[tok0], writes=[bx])
            st, bst = ringStat2.next()
            xnT, bxn = ringXn2.next()
            rms_stats(xT[:], P, st, bst, bx, xnT[:], bxn)
            ts("vector", xnT[:], xT[:], st[:, 1:2], None, ALU.mult, None, [bx, bst], [bxn])
            yield
            pTr, bTr = ringT.next()
            for k in range(KD):
                tr(pTr[:, k * P:(k + 1) * P], xnT[:, k * P:(k + 1) * P], identB[:], [bxn, b_const], [bTr])
            for k in range(KD):
                ts("vector", hT[:, k, tt_i * P:(tt_i + 1) * P], pTr[:, k * P:(k + 1) * P],
                   pmod[:, 16 + k:17 + k], pmod[:, 24 + k:25 + k], ALU.mult, ALU.add, [bTr, b_pmod], [bh], True)
            yield

    def gen_chunk(c):
        yield from gen_gate_up(hn2T[c % 2][:, :, :], b_hn2T[c % 2], CH2, with_sample=(c == 0))
        for tt_i in range(CH2 // P):
            tok0 = c * CH2 + tt_i * P
            xT, bx = ringXr.next()
            pr.dma("sync", xT[:], x1_d[tok0:tok0 + P, :], reads=x1_tiles[tok0], writes=[bx])
            xnT, bxn = ringXn2.next()
            yield from gen_down(xT[:], bx, P, tt_i * P, G2bc, b_G2, yp_d[tok0:tok0 + P, :], xnT[:], bxn)

    for _ in gen_norm2(0):
        pass
    for c in range(NCH2):
        main = gen_chunk(c)
        side = gen_norm2(c + 1) if c + 1 < NCH2 else iter(())
        step = 0
        for _ in main:
            step += 1
            if step % 3 == 0:
                next(side, None)
        for _ in side:
            pass
    xS, bxS = ringXr.next()
    gS, bgS = ringXa.next()
    pr.dma("sync", xS[0:NS, :], x1_d[T:T + NS, :], reads=[b_x1s_scr], writes=[bxS])
    for h in range(2):
        pM, bM = ringAll.next()
        for i in range(4):
            k = h * 4 + i
            tr(pM[0:NS, i * P:(i + 1) * P], modT[:, M_G2 + k, 0:NS], identF[:], [b_modT, b_const], [bM])
        cp("vector", gS[0:NS, h * 512:(h + 1) * 512], pM[0:NS, :], [bM], [bgS], h > 0)
    xnT, bxn = ringXn2.next()
    for _ in gen_down(xS[0:NS, :], bxS, NS, 0, gS, bgS, ys_d[:, :], xnT[0:NS, :], bxn, act_src=(actTs, b_actTs)):
        pass

    pr.emit()
    s2.close()
    es.close()
    return nc


_CACHE = {}


def kernel(**inputs):
    f32 = lambda a: np.ascontiguousarray(np.asarray(a, dtype=np.float32))
    g = {k: f32(v) for k, v in inputs.items()}
    if "nc" not in _CACHE:
        _CACHE["nc"] = build_program()
    nc = _CACHE["nc"]
    rowpack = np.zeros((128, 128), np.float32)
    rowpack[0:48] = g["ada_b"][0].reshape(48, 128)
    rowpack[48:56] = g["norm1_g"][0].reshape(8, 128)
    rowpack[56:64] = g["norm2_g"][0].reshape(8, 128)
    rowpack[64:80] = g["conv_w"][0].reshape(16, 128)
    rowpack[80:84] = g["conv_b"][0].reshape(4, 128)
    rowpack[84:88] = g["lru_ba"][0].reshape(4, 128)
    rowpack[88:92] = g["lru_bx"][0].reshape(4, 128)
    rowpack[92:96] = g["lru_lambda"][0].reshape(4, 128)
    rowpack[96:100] = g["s5_d"][0].reshape(4, 128)
    rowpack[100:104] = g["g_lru_out"][0].reshape(4, 128)
    rowpack[104:108] = g["g_s5_out"][0].reshape(4, 128)
    rowpack[108:116] = g["final_norm_g"].reshape(8, 128)
    s5pack = np.zeros((48, 128), np.float32)
    s5pack[0:16] = g["s5_lambda_re"][0].reshape(16, 128)
    s5pack[16:32] = g["s5_lambda_im"][0].reshape(16, 128)
    s5pack[32:48] = np.repeat(g["s5_log_dt"][0].reshape(16, 2, 1), 64, axis=2).reshape(16, 128)
    shared = {
        "ada_w": g["ada_w"][0], "rowpack": rowpack, "s5pack": s5pack,
        "w_in": g["w_in"][0], "lru_wa": g["lru_wa"][0], "lru_wx": g["lru_wx"][0],
        "s5_b_re": g["s5_b_re"][0], "s5_b_im": g["s5_b_im"][0], "s5_c_re": g["s5_c_re"][0], "s5_c_im": g["s5_c_im"][0],
        "w_glu": g["s5_w_glu"][0], "w_out": g["w_out"][0],
        "ffn_w_gate": g["ffn_w_gate"][0], "ffn_w_up": g["ffn_w_up"][0], "ffn_w_down": g["ffn_w_down"][0],
    }
    shared = {k: np.ascontiguousarray(v) for k, v in shared.items()}
    in_maps = []
    for i in range(8):
        sl = slice(16 * i, 16 * i + 16)
        m = dict(shared)
        m["xp"] = g["x_prompt"][i]
        m["xs"] = np.ascontiguousarray(g["x_sample"][sl, 0, :])
        m["sconv"] = np.ascontiguousarray(g["state_conv"][0, sl])
        m["slru"] = np.ascontiguousarray(g["state_lru"][0, sl])
        m["ss5r"] = np.ascontiguousarray(g["state_s5_re"][0, sl].reshape(16, 2048))
        m["ss5i"] = np.ascontiguousarray(g["state_s5_im"][0, sl].reshape(16, 2048))
        m["c_all"] = np.ascontiguousarray(np.concatenate([g["c_sample"][sl], g["c_prompt"][i:i + 1]], axis=0))
        in_maps.append(m)
    res = run_bass_kernel_spmd(nc, in_maps, core_ids=list(range(8)))
    r = res.results
    cat = lambda key, shp: np.stack([np.asarray(r[i][key], dtype=np.float32).reshape(shp) for i in range(8)], 0)
    y_prompt = cat("yp", (T, D))
    y_sample = cat("ys", (NS, D)).reshape(128, 1, D)
    conv_prompt = cat("convp", (3, DL))[None]
    lru_prompt = cat("lrup", (DL,))[None]
    s5_re_prompt = cat("s5rp", (32, 64))[None]
    s5_im_prompt = cat("s5ip", (32, 64))[None]
    conv_sample = cat("convs", (NS, 3, DL)).reshape(1, 128, 3, DL)
    lru_sample = cat("lrus", (NS, DL)).reshape(1, 128, DL)
    s5_re_sample = cat("s5rs", (NS, 32, 64)).reshape(1, 128, 32, 64)
    s5_im_sample = cat("s5is", (NS, 32, 64)).reshape(1, 128, 32, 64)
    return (y_prompt, y_sample, conv_prompt, lru_prompt, s5_re_prompt, s5_im_prompt,
            conv_sample, lru_sample, s5_re_sample, s5_im_sample)
```

```python
import math
import contextlib
import numpy as np
import concourse.bass as bass
import concourse.mybir as mybir
from concourse.bass_utils import run_bass_kernel_spmd

F32 = mybir.dt.float32
BF16 = mybir.dt.bfloat16
AF = mybir.ActivationFunctionType
ALU = mybir.AluOpType

ENGINES = ("tensor", "vector", "scalar", "gpsimd", "sync")


class Buf:
    __slots__ = ("name", "writers", "readers", "excl")

    def __init__(self, name, excl=False):
        self.name = name
        self.writers = []
        self.readers = []
        self.excl = excl


class Op:
    __slots__ = ("eng", "fn", "deps", "is_dma", "grp", "grp_val", "signal", "sig_val", "finish")

    def __init__(self, eng, fn, is_dma=False, grp=None):
        self.eng = eng
        self.fn = fn
        self.deps = []
        self.is_dma = is_dma
        self.grp = grp
        self.grp_val = 0
        self.signal = False
        self.sig_val = 0
        self.finish = 0.0


class Prog:
    def __init__(self, nc):
        self.nc = nc
        self.ops = {e: [] for e in ENGINES}
        self.grp_count = {}
        self.all_ops = []
        self.barrier_op = None
        self.eng_free = {e: 0.0 for e in ENGINES}
        self.step_max = 0.0
        self.next_cost = None
        self.xlat = 0.4

    def _add_deps(self, op, reads, writes, partial):
        deps = []
        if self.barrier_op is not None:
            deps.append(self.barrier_op)
        for b in reads:
            deps.extend(b.writers)
            if b.excl:
                deps.extend(r for r in b.readers if r.eng != op.eng)
        for b in writes:
            deps.extend(b.readers)
            if partial:
                deps.extend(w for w in b.writers
                            if not ((w.is_dma and op.is_dma) or (w.eng == op.eng and not w.is_dma and not op.is_dma)))
            else:
                deps.extend(b.writers)
        seen = set()
        for d in deps:
            if d is op or id(d) in seen:
                continue
            seen.add(id(d))
            if d.eng == "tensor" and op.eng == "tensor" and not d.is_dma and not op.is_dma:
                continue
            op.deps.append(d)
        for b in reads:
            b.readers.append(op)
        for b in writes:
            if partial and not b.readers:
                b.writers.append(op)
            else:
                b.writers = [op]
                b.readers = []

    def _push(self, o):
        cost = self.next_cost if self.next_cost is not None else 0.3
        self.next_cost = None
        ready = max([d.finish + (0.0 if d.eng == o.eng else self.xlat) for d in o.deps], default=0.0)
        if o.is_dma:
            start = max(self.eng_free[o.eng], ready)
            self.eng_free[o.eng] = start + 0.1
            o.finish = start + 2.0 + cost
        else:
            start = max(self.eng_free[o.eng], ready)
            o.finish = start + cost
            self.eng_free[o.eng] = o.finish
        if o.finish > self.step_max:
            self.step_max = o.finish
        self.ops[o.eng].append(o)
        self.all_ops.append(o)
        return o

    def op(self, eng, fn, reads=(), writes=(), partial=False):
        o = Op(eng, fn)
        self._add_deps(o, reads, writes, partial)
        return self._push(o)

    def dma(self, eng, out, in_, reads=(), writes=(), partial=False, **kw):
        grp = writes[0] if writes else reads[0]
        key = id(grp)
        o = Op(eng, lambda e: e.dma_start(out=out, in_=in_, **kw), is_dma=True, grp=key)
        self.grp_count[key] = self.grp_count.get(key, 0) + 1
        o.grp_val = 16 * self.grp_count[key]
        self._add_deps(o, reads, writes, partial)
        return self._push(o)

    def barrier(self, fn, eng="vector"):
        o = Op(eng, fn)
        for e in ENGINES:
            for prev in reversed(self.ops[e]):
                if not prev.is_dma:
                    o.deps.append(prev)
                    break
        lastg = {}
        for prev in self.all_ops:
            if prev.is_dma:
                lastg[prev.grp] = prev
        o.deps.extend(lastg.values())
        self.barrier_op = o
        return self._push(o)

    def emit(self, final_wait_eng="sync"):
        nc = self.nc
        for o in self.all_ops:
            for d in o.deps:
                if not d.is_dma:
                    d.signal = True
        last_sig = {}
        for e in ENGINES:
            c = 0
            for o in self.ops[e]:
                if o.signal:
                    c += 1
                    o.sig_val = c
            last_sig[e] = c
        stack = contextlib.ExitStack()
        esem = {e: stack.enter_context(nc.semaphore("s_" + e)) for e in ENGINES}
        gsem = {}
        for k in self.grp_count:
            gsem[k] = stack.enter_context(nc.semaphore("g%d" % len(gsem)))
        final = [(k, 16 * n) for k, n in self.grp_count.items()]
        ops = self.ops
        block = stack.enter_context(nc.Block())

        def make(ename):
            def body(eng):
                waited_e = {e: 0 for e in ENGINES}
                waited_g = {}
                for o in ops[ename]:
                    need_g = {}
                    need_e = {}
                    for d in o.deps:
                        if d.is_dma:
                            if need_g.get(d.grp, 0) < d.grp_val:
                                need_g[d.grp] = d.grp_val
                        else:
                            if need_e.get(d.eng, 0) < d.sig_val:
                                need_e[d.eng] = d.sig_val
                    for gk, gv in need_g.items():
                        if waited_g.get(gk, 0) < gv:
                            eng.wait_ge(gsem[gk], gv)
                            waited_g[gk] = gv
                    for ek, ev in need_e.items():
                        if waited_e[ek] < ev:
                            eng.wait_ge(esem[ek], ev)
                            waited_e[ek] = ev
                    ins = o.fn(eng)
                    if o.is_dma:
                        ins.then_inc(gsem[o.grp], 16)
                    elif o.signal:
                        ins.then_inc(esem[ename], 1)
                if ename == final_wait_eng:
                    for k, v in final:
                        if waited_g.get(k, 0) < v:
                            eng.wait_ge(gsem[k], v)
                    for e in ENGINES:
                        if e != ename and last_sig[e] and waited_e[e] < last_sig[e]:
                            eng.wait_ge(esem[e], last_sig[e])
            return body

        for e in ENGINES:
            getattr(block, e)(make(e))
        stack.close()


class Ring:
    def __init__(self, items):
        self.items = items
        self.i = 0

    def next(self):
        it = self.items[self.i % len(self.items)]
        self.i += 1
        return it


P = 128
D = 1024
KD = 8
T = 2048
NS = 16
NB = NS + 1
DL = 512
DFF = 2816
NF = 22
CH = 256
NCH = T // CH
L = 4
NK = CH // L
CH2 = 512
NCH2 = T // CH2
EPS = 1e-6
MAGIC = 12582912.0
TWO_PI = 2.0 * math.pi

C_ADB, C_N1G, C_N2G, C_CW, C_CB, C_BA, C_BX, C_LAM, C_SD, C_GLO, C_GSO, C_FNG = (
    0, 48, 56, 64, 80, 84, 88, 92, 96, 100, 104, 108)
M_SH1, M_SC1, M_G1, M_SH2, M_SC2, M_G2 = 0, 8, 16, 24, 32, 40


class Ctx:
    pass


DEBUG_STOP = None


class _Stop(Exception):
    pass


def build_program():
    try:
        return _build_program()
    except _Stop as e:
        return e.args[0]


def _build_program():
    nc = bass.Bass("TRN2", target_bir_lowering=False)
    din = lambda name, shape: nc.dram_tensor(name, list(shape), F32, kind="ExternalInput").ap()
    dout = lambda name, shape: nc.dram_tensor(name, list(shape), F32, kind="ExternalOutput").ap()
    xp_d = din("xp", (T, D)); xs_d = din("xs", (NS, D))
    sconv_d = din("sconv", (NS, 3, DL)); slru_d = din("slru", (NS, DL))
    ss5r_d = din("ss5r", (NS, 2048)); ss5i_d = din("ss5i", (NS, 2048))
    call_d = din("c_all", (NB, D))
    adaw_d = din("ada_w", (D, 6 * D))
    rowpack_d = din("rowpack", (P, P)); s5pack_d = din("s5pack", (48, P))
    win_d = din("w_in", (D, 1536)); wa_d = din("lru_wa", (8, 64, 64)); wx_d = din("lru_wx", (8, 64, 64))
    bre_d = din("s5_b_re", (32, 64, 16)); bim_d = din("s5_b_im", (32, 64, 16))
    cre_d = din("s5_c_re", (32, 16, 64)); cim_d = din("s5_c_im", (32, 16, 64))
    wglu_d = din("w_glu", (DL, 2 * DL)); wout_d = din("w_out", (D, D))
    wg_d = din("ffn_w_gate", (D, DFF)); wu_d = din("ffn_w_up", (D, DFF)); wd_d = din("ffn_w_down", (DFF, D))

    yp_d = dout("yp", (T, D)); ys_d = dout("ys", (NS, D))
    convp_d = dout("convp", (3, DL)); lrup_d = dout("lrup", (4, P))
    s5rp_d = dout("s5rp", (16, P)); s5ip_d = dout("s5ip", (16, P))
    convs_d = dout("convs", (NS, 3, DL)); lrus_d = dout("lrus", (NS, DL))
    s5rs_d = dout("s5rs", (NS, 2048)); s5is_d = dout("s5is", (NS, 2048))
    x1_d = nc.dram_tensor("x1_scratch", [T + NS, D], F32, kind="Internal").ap()

    pr = Prog(nc)
    B = Buf

    def dbg_stop(tag):
        if DEBUG_STOP == tag:
            pr.emit()
            raise _Stop(nc)
    es = contextlib.ExitStack()
    sb = lambda name, shape, dt=F32: es.enter_context(nc.sbuf_tensor(name, list(shape), dt))
    ps = lambda name, shape, dt=F32: es.enter_context(nc.psum_tensor(name, list(shape), dt))

    def _n(ap):
        n = 1
        for s_ in ap.shape[1:]:
            n *= int(s_)
        return n

    def _cost(eng, out, f):
        n = _n(out)
        if eng == "vector":
            return 0.08 + n * f / 960.0
        if eng == "scalar":
            return 0.22 + n / 1400.0
        if eng == "gpsimd":
            return 0.15 + n * 2.3 / 1000.0
        return 0.3

    def tt(eng, out, in0, in1, op, r, w, partial=False):
        pr.next_cost = _cost(eng, out, 1.5)
        pr.op(eng, lambda e: e.tensor_tensor(out=out, in0=in0, in1=in1, op=op), r, w, partial)

    def ts(eng, out, in0, s1, s2, op0, op1, r, w, partial=False):
        pr.next_cost = _cost(eng, out, 1.0)
        if op1 is None:
            pr.op(eng, lambda e: e.tensor_scalar(out=out, in0=in0, scalar1=s1, scalar2=None, op0=op0), r, w, partial)
        else:
            pr.op(eng, lambda e: e.tensor_scalar(out=out, in0=in0, scalar1=s1, scalar2=s2, op0=op0, op1=op1), r, w, partial)

    def stt(out, in0, scalar, in1, op0, op1, r, w, partial=False):
        pr.next_cost = _cost("vector", out, 1.7)
        pr.op("vector", lambda e: e.scalar_tensor_tensor(out=out, in0=in0, scalar=scalar, in1=in1, op0=op0, op1=op1), r, w, partial)

    def cp(eng, out, in_, r, w, partial=False):
        pr.next_cost = _cost(eng, out, 1.0)
        if eng == "scalar":
            pr.op(eng, lambda e: e.copy(out=out, in_=in_), r, w, partial)
        else:
            pr.op(eng, lambda e: e.tensor_copy(out=out, in_=in_), r, w, partial)

    def act(out, in_, func, r, w, scale=None, bias=None, accum=None, partial=False):
        pr.next_cost = _cost("scalar", out, 1.0)
        kw = {}
        if scale is not None:
            kw["scale"] = scale
        if bias is not None:
            kw["bias"] = bias
        if accum is not None:
            kw["accum_out"] = accum
        pr.op("scalar", lambda e: e.activation(out=out, in_=in_, func=func, **kw), r, w, partial)

    def mset(eng, ap, val, w, partial=False):
        pr.op(eng, lambda e: e.memset(ap, val), (), w, partial)

    def mm(out, lhsT, rhs, start, stop, r, w, tp=None):
        pr.next_cost = 0.05 + max(64, _n(out)) / 2400.0 * (4.0 if lhsT.dtype == F32 else 1.0)
        if tp is None:
            pr.op("tensor", lambda e: e.matmul(out, lhsT, rhs, start=start, stop=stop), r, w, True)
        else:
            pr.op("tensor", lambda e: e.matmul(out, lhsT, rhs, start=start, stop=stop, tile_position=tp), r, w, True)

    def tr(out, in_, ident, r, w):
        pr.next_cost = 0.05 + max(64, _n(out)) / 2400.0
        pr.op("tensor", lambda e: e.transpose(out, in_, ident), r, w, True)

    def scan(out, d0, d1, init, r, w, partial=False):
        pr.next_cost = _cost("vector", out, 2.0)
        pr.op("vector", lambda e: e.tensor_tensor_scan(out=out, data0=d0, data1=d1, initial=init, op0=ALU.mult, op1=ALU.add), r, w, partial)

    def recip(out, in_, r, w, partial=False):
        pr.op("vector", lambda e: e.reciprocal(out=out, in_=in_), r, w, partial)

    def round_to_int(out, in_, tmp, r, w):
        ts("vector", tmp, in_, MAGIC, None, ALU.add, None, r, w)
        ts("vector", out, tmp, MAGIC, None, ALU.subtract, None, r, w)

    pA = [ps("pA%d" % i, (P, 512)) for i in range(4)]
    pS = [ps("pS%d" % i, (P, 512)) for i in range(2)]
    pT = [ps("pT%d" % i, (P, 1024), BF16) for i in range(1)]
    pD = ps("pD", (P, 512))
    bA = [B("pA%d" % i, True) for i in range(4)]
    bS = [B("pS%d" % i, True) for i in range(2)]
    bT = [B("pT%d" % i, True) for i in range(1)]
    bD = B("pD", True)
    ringA = Ring(list(zip(pA, bA)))
    ringS = Ring(list(zip(pS, bS)))
    ringT = Ring(list(zip(pT, bT)))
    ringD = Ring([(pD, bD)])
    ringAll = Ring(list(zip(pA + pS + [pD], bA + bS + [bD])))
    ringA01 = Ring([(pA[0], bA[0]), (pA[1], bA[1])])
    ringA23 = Ring([(pS[0], bS[0]), (pS[1], bS[1]), (pA[2], bA[2]), (pA[3], bA[3])])

    identF = sb("identF", (P, P)); identB = sb("identB", (P, P), BF16)
    onesF = sb("onesF", (P, P)); onesB = sb("onesB", (P, P), BF16)
    cols = sb("cols", (P, P))
    pmod = sb("pmod", (P, 32))
    modT = sb("modT", (P, 48, NB))
    hn2Ts = sb("hn2Ts", (P, KD, NS), BF16)
    barr_t = sb("barr_t", (P, 1))
    b_const = B("const"); b_cols = B("cols"); b_pmod = B("pmod"); b_modT = B("modT"); b_hn2Ts = B("hn2Ts")
    b_x1s_scr = B("x1s_scr")

    mset("gpsimd", identF[:], 1.0, [b_const])
    pr.op("gpsimd", lambda e: e.affine_select(out=identF[:], in_=identF[:], pattern=[[-1, P]], compare_op=ALU.is_equal,
                                              fill=0.0, base=0, channel_multiplier=1), [b_const], [b_const])
    cp("gpsimd", identB[:], identF[:], [b_const], [b_const])
    mset("gpsimd", onesF[:], 1.0, [b_const])
    mset("gpsimd", onesB[:], 1.0, [b_const])
    dbg_stop('consts')

    def bcast_rows(dst, bdst, src_cols, bsrc, nrows, diag, b_diag, first=True, ring=None):
        for h in range(2):
            for i in range(4):
                ts("vector", diag[:, i * P:(i + 1) * P], identF[:], src_cols[:, 4 * h + i:4 * h + i + 1], None, ALU.mult, None,
                   [b_const, bsrc], [b_diag], i > 0)
            pM, bM = (ring or ringA).next()
            mm(pM[0:nrows, :], onesF[:, 0:nrows], diag[:], True, True, [b_const, b_diag], [bM])
            cp("scalar", dst[0:nrows, h * 512:(h + 1) * 512], pM[0:nrows, :], [bM], [bdst], (h > 0) or not first)

    s1 = contextlib.ExitStack()
    sb1 = lambda name, shape, dt=F32: s1.enter_context(nc.sbuf_tensor(name, list(shape), dt))
    win_sb = sb1("win_sb", (P, KD, 1536), BF16); b_win = B("win")
    wa_blk = sb1("wa_blk", (P, 4, P), BF16); wx_blk = sb1("wx_blk", (P, 4, P), BF16); b_wab = B("wa"); b_wxb = B("wx")
    wglu_sb = sb1("wglu_sb", (P, 4, 2 * DL), BF16); b_wglu = B("wglu")
    wout_sb = sb1("wout_sb", (P, KD, D), BF16); b_wout = B("wout")
    lruc = sb1("lruc", (P, 16)); b_lruc = B("lruc")
    s5t = sb1("s5t", (P, 16, 16)); b_s5t = B("s5t")
    W1 = sb1("W1", (P, 2, 4, L, P), BF16); b_W1 = B("W1")
    CAb = sb1("CAb", (P, 2, 16, L, 32), BF16); b_CAb = B("CAb")
    Kblk = sb1("Kblk", (P, 4, L, P), BF16); b_Kblk = B("Kblk")
    CT0 = sb1("CT0", (P, 2, 16, 32), BF16); b_CT0 = B("CT0")
    rot = sb1("rot", (P, 2, 16, NK)); b_rot = B("rot")
    rho_tab = sb1("rho_tab", (P, 16, NK)); b_rho = B("rho_tab")
    Dblk = sb1("Dblk", (P, 4, P), BF16); b_Dblk = B("Dblk")

    st_ = contextlib.ExitStack()
    sbt = lambda name, shape, dt=F32: st_.enter_context(nc.sbuf_tensor(name, list(shape), dt))
    jv = sbt("jv", (P, 64)); b_jv = B("jv")
    for j in range(64):
        mset("gpsimd", jv[:, j:j + 1], float(j), [b_jv], j > 0)
    G1bc = sbt("G1bc", (P, D)); b_G1 = B("G1bc")
    rowpack_sb = sbt("rowpack_sb", (P, P)); b_rowpack = B("rowpack")
    pr.dma("sync", rowpack_sb[:], rowpack_d[:, :], writes=[b_rowpack])
    s5pack_sb = sbt("s5pack_sb", (48, P)); b_s5pack = B("s5pack")
    pr.dma("sync", s5pack_sb[:], s5pack_d[:, :], writes=[b_s5pack])
    call_sb = sbt("call_sb", (NB, D)); b_call = B("call")
    pr.dma("sync", call_sb[:], call_d[:, :], writes=[b_call])

    Bsrc = sbt("Bsrc", (P, 2, 16, 32)); b_Bsrc = B("Bsrc")
    mset("vector", Bsrc[:], 0.0, [b_Bsrc])
    for ri, bd in enumerate((bre_d, bim_d)):
        bv = bd.rearrange("(p two) n c -> two n p c", two=2)
        pr.dma("sync", Bsrc[0:64, ri, :, 0:16], bv[0], writes=[b_Bsrc], partial=True)
        pr.dma("sync", Bsrc[64:128, ri, :, 16:32], bv[1], writes=[b_Bsrc], partial=True)
    Csrc = sbt("Csrc", (P, 2, 4, P)); b_Csrc = B("Csrc")
    mset("vector", Csrc[:], 0.0, [b_Csrc])
    for ri, cd in enumerate((cre_d, cim_d)):
        cv = cd.rearrange("(q pl two) c n -> pl two c q n", pl=4, two=2)
        for pl in range(4):
            for g2 in range(2):
                p0 = pl * 32 + g2 * 16
                pr.dma("sync", Csrc[p0:p0 + 16, ri, :, g2 * 64:(g2 + 1) * 64], cv[pl, g2], writes=[b_Csrc], partial=True)
    NAB = 5
    adaw_bufs = [sbt("adaw%d" % i, (P, KD, 512), BF16) for i in range(NAB)]
    b_adaw = [B("adaw%d" % i) for i in range(NAB)]
    adaw_v = adaw_d.rearrange("(k p) n -> p k n", p=P)

    def load_adaw(j):
        pr.dma("gpsimd", adaw_bufs[j % NAB][:], adaw_v[:, :, j * 512:(j + 1) * 512], writes=[b_adaw[j % NAB]])

    for j in range(NAB):
        load_adaw(j)

    pS_, bS_ = ringS.next()
    tr(pS_[:, 0:P], rowpack_sb[:], identF[:], [b_rowpack, b_const], [bS_])
    cp("vector", cols[:], pS_[:, 0:P], [bS_], [b_cols])

    csil = sbt("csil", (NB, D)); b_csil = B("csil")
    act(csil[:], call_sb[:], AF.Silu, [b_call], [b_csil])
    cT = sbt("cT", (P, KD, NB), BF16); b_cT = B("cT")
    pS_, bS_ = ringS.next()
    for k in range(KD):
        tr(pS_[:, k * NB:(k + 1) * NB], csil[:, k * P:(k + 1) * P], identF[0:NB, 0:NB], [b_csil, b_const], [bS_])
    cp("vector", cT[:], pS_[:, 0:KD * NB].rearrange("p (k n) -> p k n", n=NB), [bS_], [b_cT])
    dbg_stop('loads')

    win_v = win_d.rearrange("(k p) n -> p k n", p=P)
    for k in range(KD):
        pr.dma("gpsimd", win_sb[:, k, :], win_v[:, k, :], writes=[b_win], partial=True)
    mset("vector", wa_blk[:], 0.0, [b_wab]); mset("vector", wx_blk[:], 0.0, [b_wxb])
    for (blk, src, bb) in ((wa_blk, wa_d, b_wab), (wx_blk, wx_d, b_wxb)):
        sv = src.rearrange("(t two) i j -> two i t j", two=2)
        pr.dma("gpsimd", blk[0:64, :, 0:64], sv[0], writes=[bb], partial=True)
        pr.dma("gpsimd", blk[64:128, :, 64:128], sv[1], writes=[bb], partial=True)
    dbg_stop('weights')

    def gen_modT():
        pMa, bMa = pA[2], bA[2]
        pMb, bMb = pA[3], bA[3]
        for j in range(12):
            buf = adaw_bufs[j % NAB]; bb = b_adaw[j % NAB]
            pM, bM = (pMa, bMa) if j < 6 else (pMb, bMb)
            for i in range(4):
                c = (j % 6) * 4 + i
                for k in range(KD):
                    mm(pM[:, c * NB:(c + 1) * NB], buf[:, k, i * P:(i + 1) * P], cT[:, k, :], k == 0, k == KD - 1, [b_cT, bb], [bM])
            if j + NAB < 12:
                load_adaw(j + NAB)
            yield
            if j == 5:
                tt("vector", modT[:, 0:24, :], pMa[:, 0:24 * NB].rearrange("p (c n) -> p c n", n=NB),
                   cols[:, C_ADB:C_ADB + 24].unsqueeze(2).to_broadcast([P, 24, NB]), ALU.add, [bMa, b_cols], [b_modT])
                stt(pmod[:, 0:8], modT[:, M_SC1:M_SC1 + 8, NS], 1.0, cols[:, C_N1G:C_N1G + 8], ALU.add, ALU.mult, [b_modT, b_cols], [b_pmod])
                cp("vector", pmod[:, 8:16], modT[:, M_SH1:M_SH1 + 8, NS], [b_modT], [b_pmod], True)
                diag = sbt("diag", (P, 512)); b_diag = B("diag")
                bcast_rows(G1bc, b_G1, modT[:, M_G1:M_G1 + 8, NS], b_modT, P, diag, b_diag, ring=Ring([(pA[2], bA[2])]))
                yield
                for k in range(KD):
                    gc = (C_GLO + k) if k < 4 else (C_GSO + k - 4)
                    ts("vector", wout_sb[:, k, :], wout_sb[:, k, :], cols[:, gc:gc + 1], None, ALU.mult, None, [b_wout, b_cols], [b_wout])
                    if k % 4 == 3:
                        yield
                for k in range(KD):
                    tt("vector" if k % 2 else "gpsimd", wout_sb[:, k, :], wout_sb[:, k, :], G1bc[:], ALU.mult, [b_wout, b_G1], [b_wout])
                    if k % 2 == 1:
                        yield
        b_modT2 = B("modT2")
        tt("vector", modT[:, 24:48, :], pMb[:, 0:24 * NB].rearrange("p (c n) -> p c n", n=NB),
           cols[:, C_ADB + 24:C_ADB + 48].unsqueeze(2).to_broadcast([P, 24, NB]), ALU.add, [bMb, b_cols], [b_modT], True)
        stt(pmod[:, 16:24], modT[:, M_SC2:M_SC2 + 8, NS], 1.0, cols[:, C_N2G:C_N2G + 8], ALU.add, ALU.mult, [b_modT, b_cols], [b_pmod], True)
        cp("vector", pmod[:, 24:32], modT[:, M_SH2:M_SH2 + 8, NS], [b_modT], [b_pmod], True)
        yield

    ringM = Ring([(pA[2], bA[2]), (pA[3], bA[3])])
    ringA2 = Ring([(pA[0], bA[0]), (pA[1], bA[1])])
    gm = gen_modT()

    def pump(n=1):
        for _ in range(n):
            next(gm, None)

    wglu_v = wglu_d.rearrange("(k p) n -> p k n", p=P)
    for k in range(4):
        pr.dma("gpsimd", wglu_sb[:, k, :], wglu_v[:, k, :], writes=[b_wglu], partial=True)
    wout_v = wout_d.rearrange("(k p) n -> p k n", p=P)
    for k in range(KD):
        pr.dma("gpsimd", wout_sb[:, k, :], wout_v[:, k, :], writes=[b_wout], partial=True)

    act(lruc[:, 0:4], cols[:, C_LAM:C_LAM + 4], AF.Exp, [b_cols], [b_lruc], scale=-1.0)
    Rl_ = [b_lruc]
    ts("vector", lruc[:, 4:8], lruc[:, 0:4], -0.25, 1.0 / 3.0, ALU.mult, ALU.add, Rl_, Rl_)
    tt("vector", lruc[:, 4:8], lruc[:, 4:8], lruc[:, 0:4], ALU.mult, Rl_, Rl_)
    ts("vector", lruc[:, 4:8], lruc[:, 4:8], -1.0, 0.5, ALU.mult, ALU.add, Rl_, Rl_)
    tt("vector", lruc[:, 4:8], lruc[:, 4:8], lruc[:, 0:4], ALU.mult, Rl_, Rl_)
    ts("vector", lruc[:, 4:8], lruc[:, 4:8], -1.0, 1.0, ALU.mult, ALU.add, Rl_, Rl_)
    tt("vector", lruc[:, 4:8], lruc[:, 4:8], lruc[:, 0:4], ALU.mult, Rl_, Rl_)
    ts("vector", lruc[:, 8:12], lruc[:, 4:8], -8.0, None, ALU.mult, None, Rl_, Rl_)
    ts("vector", lruc[:, 12:16], lruc[:, 4:8], -16.0, None, ALU.mult, None, Rl_, Rl_)
    cl = lruc[:, 8:12]; cl2 = lruc[:, 12:16]
    dbg_stop('lruc')

    s5c = sbt("s5c", (P, 48)); b_s5c = B("s5c")
    pS_, bS_ = ringS.next()
    tr(pS_[:, 0:48], s5pack_sb[:], identF[0:48, 0:48], [b_s5pack, b_const], [bS_])
    cp("vector", s5c[:], pS_[:, 0:48], [bS_], [b_s5c])
    lr = s5c[:, 0:16]; li = s5c[:, 16:32]; ldt = s5c[:, 32:48]
    V = lambda i: s5t[:, i, :]
    I_DT, I_LRDT, I_TT, I_TR, I_TMP, I_TMP2, I_DEN, I_M1, I_FR, I_FI, I_ABR, I_ABI, I_F8, I_RHO, I_CN, I_SN = range(16)
    R = [b_s5t, b_s5c]; W = [b_s5t]
    act(V(I_DT), ldt, AF.Exp, R, W)
    tt("vector", V(I_LRDT), lr, V(I_DT), ALU.mult, R, W)
    tt("vector", V(I_TT), li, V(I_DT), ALU.mult, R, W)
    ts("vector", V(I_TT), V(I_TT), 1.0 / TWO_PI, None, ALU.mult, None, R, W)
    round_to_int(V(I_TMP2), V(I_TT), V(I_TMP), R, W)
    tt("vector", V(I_TR), V(I_TT), V(I_TMP2), ALU.subtract, R, W)
    pump(2)
    NJ = L + 1
    pw = sbt("pw", (P, 6, 16, NJ)); b_pw = B("pw")

    def bc3(ap2, n):
        return ap2.unsqueeze(2).to_broadcast([P, 16, n])

    def jb3(n):
        return jv[:, 0:n].unsqueeze(1).to_broadcast([P, 16, n])

    def sincos(dst_sin, dst_cos, ang, tmp, tmp2, Rr, Ww):
        round_to_int(tmp2, ang, tmp, Rr, Ww)
        tt("vector", tmp2, ang, tmp2, ALU.subtract, Rr, Ww)
        act(dst_sin, tmp2, AF.Sin, Rr, Ww, scale=TWO_PI)
        ts("vector", tmp, ang, 0.25, None, ALU.add, None, Rr, Ww)
        round_to_int(tmp2, tmp, dst_cos, Rr, Ww)
        tt("vector", tmp2, tmp, tmp2, ALU.subtract, Rr, Ww)
        act(dst_cos, tmp2, AF.Sin, Rr, Ww, scale=TWO_PI)

    Rp = [b_pw, b_s5t, b_jv]; Wp = [b_pw]
    tt("vector", pw[:, 0], bc3(V(I_LRDT), NJ), jb3(NJ), ALU.mult, Rp, Wp)
    act(pw[:, 0], pw[:, 0], AF.Exp, Rp, Wp)
    tt("vector", pw[:, 1], bc3(V(I_TR), NJ), jb3(NJ), ALU.mult, Rp, Wp)
    sincos(pw[:, 5], pw[:, 4], pw[:, 1], pw[:, 2], pw[:, 3], Rp, Wp)
    tt("vector", pw[:, 4], pw[:, 4], pw[:, 0], ALU.mult, Rp, Wp)
    tt("vector", pw[:, 5], pw[:, 5], pw[:, 0], ALU.mult, Rp, Wp)
    PwRe = pw[:, 4]; PwIm = pw[:, 5]
    Rq = [b_pw, b_s5t, b_s5c]
    cp("vector", V(I_ABR), PwRe[:, :, 1], Rq, W)
    cp("vector", V(I_ABI), PwIm[:, :, 1], Rq, W)
    pump(1)
    tt("vector", V(I_DEN), lr, lr, ALU.mult, Rq, W)
    tt("vector", V(I_TMP), li, li, ALU.mult, Rq, W)
    tt("vector", V(I_DEN), V(I_DEN), V(I_TMP), ALU.add, Rq, W)
    recip(V(I_DEN), V(I_DEN), Rq, W)
    ts("vector", V(I_M1), V(I_ABR), -1.0, None, ALU.add, None, Rq, W)
    tt("vector", V(I_FR), V(I_M1), lr, ALU.mult, Rq, W)
    tt("vector", V(I_TMP), V(I_ABI), li, ALU.mult, Rq, W)
    tt("vector", V(I_FR), V(I_FR), V(I_TMP), ALU.add, Rq, W)
    tt("vector", V(I_FR), V(I_FR), V(I_DEN), ALU.mult, Rq, W)
    tt("vector", V(I_FI), V(I_ABI), lr, ALU.mult, Rq, W)
    tt("vector", V(I_TMP), V(I_M1), li, ALU.mult, Rq, W)
    tt("vector", V(I_FI), V(I_FI), V(I_TMP), ALU.subtract, Rq, W)
    tt("vector", V(I_FI), V(I_FI), V(I_DEN), ALU.mult, Rq, W)
    yy = sbt("yy", (P, 3, 16, L)); b_yy = B("yy")
    Ry = [b_yy, b_pw, b_s5t]; Wy = [b_yy]
    tt("vector", yy[:, 0], PwRe[:, :, 0:L], bc3(V(I_FR), L), ALU.mult, Ry, Wy)
    tt("vector", yy[:, 2], PwIm[:, :, 0:L], bc3(V(I_FI), L), ALU.mult, Ry, Wy)
    tt("vector", yy[:, 0], yy[:, 0], yy[:, 2], ALU.subtract, Ry, Wy)
    tt("vector", yy[:, 1], PwIm[:, :, 0:L], bc3(V(I_FR), L), ALU.mult, Ry, Wy)
    tt("vector", yy[:, 2], PwRe[:, :, 0:L], bc3(V(I_FI), L), ALU.mult, Ry, Wy)
    tt("vector", yy[:, 1], yy[:, 1], yy[:, 2], ALU.add, Ry, Wy)
    Yre = yy[:, 0]; Yim = yy[:, 1]
    dbg_stop('s5a')

    pump(1)
    CT = sbt("CT", (P, 2, 16, 32)); b_CT = B("CT")
    for ri in range(2):
        pS_, bS_ = ringS.next()
        for q in range(4):
            tr(pS_[:, q * P:(q + 1) * P], Csrc[:, ri, q, :], identF[:], [b_Csrc, b_const], [bS_])
        cp("vector", CT[:, ri].rearrange("p a b -> p (a b)"), pS_[:, :], [bS_], [b_CT], ri > 0)
    cp("vector", CT0[:, 0], CT[:, 0], [b_CT], [b_CT0])
    ts("vector", CT0[:, 1], CT[:, 1], -1.0, None, ALU.mult, None, [b_CT], [b_CT0], True)
    dbg_stop('s5b')

    Xb = [sbt("Xb%d" % i, (P, 2, 4, L, 32), BF16) for i in range(2)]; b_Xb = [B("Xb%d" % i) for i in range(2)]
    xtmp = sbt("xtmp", (P, 2, 4, L, 16)); b_xtmp = [B("xtmp_h0"), B("xtmp_h1")]
    CTn = sbt("CTn", (P, 16, 32)); b_CTn = B("CTn")
    ts("vector", CTn[:], CT[:, 1], -1.0, None, ALU.mult, None, [b_CT], [b_CTn])
    mset("vector", Kblk[:], 0.0, [b_Kblk])
    mset("vector", CAb[:], 0.0, [b_CAb])
    mset("vector", Xb[0][:], 0.0, [b_Xb[0]])
    mset("vector", Xb[1][:], 0.0, [b_Xb[1]])

    def half_prod(eng, hb, out_ap, a1, b1, a2, b2, op, r, w, p0):
        ps_ = slice(64 * hb, 64 * hb + 64); cs_ = slice(16 * hb, 16 * hb + 16)
        ya = lambda t: t[ps_, p0:p0 + 4, :].unsqueeze(3).to_broadcast([64, 4, L, 16])
        bb_ = lambda t: t[ps_, p0:p0 + 4, cs_].unsqueeze(2).to_broadcast([64, 4, L, 16])
        bt = b_xtmp[hb]
        tt(eng, xtmp[ps_, 0], ya(a1), bb_(b1), ALU.mult, r, [bt])
        tt(eng, xtmp[ps_, 1], ya(a2), bb_(b2), ALU.mult, r + [bt], [bt])
        tt(eng, out_ap[ps_, :, :, cs_], xtmp[ps_, 0], xtmp[ps_, 1], op, [bt], w, True)

    for q in range(4):
        p0 = q * 4
        xb = Xb[q % 2]; bxb = b_Xb[q % 2]
        PwRe1 = PwRe[:, :, 1:L + 1]; PwIm1 = PwIm[:, :, 1:L + 1]
        RX = [b_yy, b_Bsrc]; RCA = [b_pw, b_CT, b_CTn]
        for hb in range(2):
            half_prod("vector", hb, xb[:, 0], Yre, Bsrc[:, 0], Yim, Bsrc[:, 1], ALU.subtract, RX, [bxb], p0)
            half_prod("vector", hb, xb[:, 1], Yre, Bsrc[:, 1], Yim, Bsrc[:, 0], ALU.add, RX, [bxb], p0)
        half_prod("vector", 0, CAb[:, 0, p0:p0 + 4], PwRe1, CT[:, 0], PwIm1, CT[:, 1], ALU.subtract, RCA, [b_CAb], p0)
        pump(1)
        half_prod("gpsimd", 1, CAb[:, 0, p0:p0 + 4], PwRe1, CT[:, 0], PwIm1, CT[:, 1], ALU.subtract, RCA, [b_CAb], p0)
        pump(1)
        half_prod("gpsimd", 0, CAb[:, 1, p0:p0 + 4], PwRe1, CTn, PwIm1, CT[:, 0], ALU.subtract, RCA, [b_CAb], p0)
        pump(1)
        half_prod("gpsimd", 1, CAb[:, 1, p0:p0 + 4], PwRe1, CTn, PwIm1, CT[:, 0], ALU.subtract, RCA, [b_CAb], p0)
        pump(1)
        for ri in range(2):
            for jq in range(L // 4):
                pM, bM = ringA2.next()
                for pl in range(4):
                    for jj in range(4):
                        j = jq * 4 + jj
                        mm(pM[pl * 32:(pl + 1) * 32, jj * P:(jj + 1) * P], xb[:, ri, pl, L - 1 - j, :], identB[:],
                           True, True, [bxb, b_const], [bM], tp=(0, pl * 32))
                cp("scalar", W1[:, ri, q, jq * 4:(jq + 1) * 4, :].rearrange("p a b -> p (a b)"), pM[:, :], [bM], [b_W1], True)
        pump(1)
        for tq in range(L // 4):
            pM, bM = ringA2.next()
            for pl in range(4):
                p = q * 4 + pl
                for t4 in range(4):
                    tau = tq * 4 + t4
                    o = pM[pl * 32:(pl + 1) * 32, t4 * P + pl * 32: t4 * P + pl * 32 + 32]
                    mm(o, xb[:, 0, pl, tau, :], CT0[:, 0, p, :], True, False, [bxb, b_CT0], [bM], tp=(0, pl * 32))
                    mm(o, xb[:, 1, pl, tau, :], CT0[:, 1, p, :], False, True, [bxb, b_CT0], [bM], tp=(0, pl * 32))
            for pl in range(4):
                src = pM[pl * 32:(pl + 1) * 32, :].rearrange("p (t c) -> p t c", c=P)[:, :, pl * 32:pl * 32 + 32]
                cp("scalar" if pl % 2 else "vector", Kblk[pl * 32:(pl + 1) * 32, q, tq * 4:(tq + 1) * 4, pl * 32:pl * 32 + 32], src,
                   [bM], [b_Kblk], True)

    for q in range(4):
        ts("vector", Dblk[:, q, :], identF[:], cols[:, C_SD + q:C_SD + q + 1], None, ALU.mult, None, [b_const, b_cols], [b_Dblk], q > 0)
    for q in range(4):
        tt("vector", Kblk[:, q, 0, :], Kblk[:, q, 0, :], Dblk[:, q, :], ALU.add, [b_Kblk, b_Dblk], [b_Kblk])
    dbg_stop('s5c')
    lv2 = sbt("lv2", (P, 3, 16, NK)); b_lv2 = B("lv2")
    ts("vector", V(I_F8), V(I_TR), float(L), None, ALU.mult, None, Rq, W)
    round_to_int(V(I_TMP2), V(I_F8), V(I_TMP), Rq, W)
    tt("vector", V(I_F8), V(I_F8), V(I_TMP2), ALU.subtract, Rq, W)
    Rl = [b_lv2, b_s5t, b_jv, b_rot]; Wl = [b_lv2, b_rot]
    tt("vector", lv2[:, 0], bc3(V(I_F8), NK), jb3(NK), ALU.mult, Rl, Wl)
    sincos(rot[:, 1], rot[:, 0], lv2[:, 0], lv2[:, 1], lv2[:, 2], Rl, Wl)
    Rc = rot[:, 0]; Rs = rot[:, 1]
    s5u = sbt("s5u", (P, 3, 16)); b_s5u = B("s5u")
    ts("vector", s5u[:, 0], V(I_F8), float(NK), None, ALU.mult, None, [b_s5t], [b_s5u])
    sincos(V(I_SN), V(I_CN), s5u[:, 0], s5u[:, 1], s5u[:, 2], [b_s5u, b_s5t], [b_s5t, b_s5u])
    act(V(I_RHO), V(I_LRDT), AF.Exp, Rq, W, scale=float(L))
    cp("vector", rho_tab[:], bc3(V(I_RHO), NK), [b_s5t], [b_rho])
    dbg_stop('s5d')
    ts("vector", rho_tab[:, :, 0:1], rho_tab[:, :, 0:1], 0.0, None, ALU.mult, None, [b_rho], [b_rho])

    pump(40)
    dbg_stop('setup')
    pr.barrier(lambda e: e.memset(barr_t[:], 0.0))
    st_.close()

    def rms_stats(x_ap, nrows, st, bst, bx, junk_ap, bjunk):
        act(junk_ap, x_ap, AF.Square, [bx], [bjunk, bst], accum=st[0:nrows, 0:1])
        act(st[0:nrows, 1:2], st[0:nrows, 0:1], AF.Ln, [bst], [bst], scale=1.0 / D, bias=EPS)
        act(st[0:nrows, 1:2], st[0:nrows, 1:2], AF.Exp, [bst], [bst], scale=-0.5)

    def run(gen):
        for _ in gen:
            pass

    def proj_in(cx, hT, bh, n, lrux_t, blx, col0, ring=None):
        per = min(12, 512 // n)
        ot = 0
        while ot < 12:
            pM, bM = (ring or ringA).next()
            cnt = min(per, 12 - ot)
            for i in range(cnt):
                o = ot + i
                for k in range(KD):
                    mm(pM[:, i * n:(i + 1) * n], win_sb[:, k, o * P:(o + 1) * P], hT[:, k, 0:n], k == 0, k == KD - 1, [b_win, bh], [bM])
            for i in range(cnt):
                o = ot + i
                src = pM[:, i * n:(i + 1) * n]
                if o < 4:
                    cp("scalar", lrux_t[:, o, col0:col0 + n], src, [bM], [blx], True)
                elif o < 8:
                    act(cx.gg[:, o - 4, 0:n], src, AF.Gelu_apprx_tanh, [bM], [cx.b_gg], partial=True)
                else:
                    cp("scalar", cx.ub[:, o - 8, 0:n], src, [bM], [cx.b_ub], True)
            ot += cnt
            yield

    def lru_gates(cx, n, ring=None):
        conv, aa, bb_, convb = cx.conv, cx.aa, cx.bb, cx.convb
        b_conv, b_aa, b_bb, b_convb = cx.b_conv, cx.b_aa, cx.b_bb, cx.b_convb
        cp("scalar", convb[:, :, 0:n], conv[:, :, 0:n], [b_conv], [b_convb])
        yield
        rg = ring or ringA
        groups = [(0, 1), (2, 3)] if n > 128 else [(0, 1, 2, 3)]
        for grp in groups:
            dr, dbr = rg.next()
            di, dbi = rg.next()
            for i, t in enumerate(grp):
                o = i * n
                mm(dr[:, o:o + n], wa_blk[:, t, :], convb[:, t, 0:n], True, True, [b_wab, b_convb], [dbr])
                mm(di[:, o:o + n], wx_blk[:, t, :], convb[:, t, 0:n], True, True, [b_wxb, b_convb], [dbi])
            for i, t in enumerate(grp):
                o = i * n
                act(aa[:, t, 0:n], dr[:, o:o + n], AF.Sigmoid, [dbr, b_cols], [b_aa], bias=cols[:, C_BA + t:C_BA + t + 1], partial=True)
                act(bb_[:, t, 0:n], di[:, o:o + n], AF.Sigmoid, [dbi, b_cols], [b_bb], bias=cols[:, C_BX + t:C_BX + t + 1], partial=True)
            yield
        tt("gpsimd", bb_[:, :, 0:n], bb_[:, :, 0:n], conv[:, :, 0:n], ALU.mult, [b_bb, b_conv], [b_bb])
        for t in range(4):
            act(conv[:, t, 0:n], aa[:, t, 0:n], AF.Exp, [b_aa, b_lruc], [b_conv], scale=cl2[:, t:t + 1], partial=t > 0)
        yield
        for t in range(4):
            act(aa[:, t, 0:n], aa[:, t, 0:n], AF.Exp, [b_aa, b_lruc, b_conv], [b_aa], scale=cl[:, t:t + 1], partial=t > 0)
        yield
        ts("gpsimd", conv[:, :, 0:n], conv[:, :, 0:n], -1.0, 1.0, ALU.mult, ALU.add, [b_conv], [b_conv])
        act(conv[:, :, 0:n], conv[:, :, 0:n], AF.Ln, [b_conv], [b_conv], bias=1e-30)
        act(conv[:, :, 0:n], conv[:, :, 0:n], AF.Exp, [b_conv], [b_conv], scale=0.5)
        yield
        tt("gpsimd", bb_[:, :, 0:n], bb_[:, :, 0:n], conv[:, :, 0:n], ALU.mult, [b_bb, b_conv], [b_bb])
        yield

    def s5_glu_and_merge(cx, n, mg, bmg, ring=None):
        gy, sg, s5o, sq, rstd, gg = cx.gy, cx.sg, cx.s5o, cx.sq, cx.rstd, cx.gg
        b_gy, b_sg, b_s5o, b_sq, b_rstd, b_gg = cx.b_gy, cx.b_sg, cx.b_s5o, cx.b_sq, cx.b_rstd, cx.b_gg
        act(sq[:, 0:4, 0:n], gg[:, :, 0:n], AF.Square, [b_gg], [b_sq])
        yield
        per = max(1, min(4, 512 // n))
        for grp in range(0, 4, per):
            pZa, bZa = (ring or ringA).next()
            pZb, bZb = (ring or ringA).next()
            cnt = min(per, 4 - grp)
            for i in range(cnt):
                t = grp + i
                for k in range(4):
                    mm(pZa[:, i * n:(i + 1) * n], wglu_sb[:, k, t * P:(t + 1) * P], gy[:, k, 0:n], k == 0, k == 3, [b_wglu, b_gy], [bZa])
                for k in range(4):
                    mm(pZb[:, i * n:(i + 1) * n], wglu_sb[:, k, (4 + t) * P:(5 + t) * P], gy[:, k, 0:n], k == 0, k == 3, [b_wglu, b_gy], [bZb])
            for i in range(cnt):
                t = grp + i
                act(sg[:, t, 0:n], pZb[:, i * n:(i + 1) * n], AF.Sigmoid, [bZb], [b_sg], partial=True)
                tt("vector", s5o[:, t, 0:n], pZa[:, i * n:(i + 1) * n], sg[:, t, 0:n], ALU.mult, [bZa, b_sg], [b_s5o], True)
            yield
        tt("gpsimd", sq[:, 4:8, 0:n], s5o[:, :, 0:n], s5o[:, :, 0:n], ALU.mult, [b_s5o], [b_sq], True)
        pQ, bQ = (ring or ringA).next()
        for h in range(2):
            for t in range(4):
                mm(pQ[:, h * n:(h + 1) * n], onesB[:], sq[:, h * 4 + t, 0:n], t == 0, t == 3, [b_const, b_sq], [bQ])
        act(rstd[:, :, 0:n], pQ[:, 0:2 * n].rearrange("p (h n) -> p h n", h=2), AF.Ln, [bQ], [b_rstd], scale=1.0 / DL, bias=EPS)
        act(rstd[:, :, 0:n], rstd[:, :, 0:n], AF.Exp, [b_rstd], [b_rstd], scale=-0.5)
        yield
        tt("gpsimd", mg[:, 0:4, 0:n], gg[:, :, 0:n], rstd[:, 0, 0:n].unsqueeze(1).to_broadcast([P, 4, n]), ALU.mult,
           [b_gg, b_rstd], [bmg], True)
        tt("gpsimd", mg[:, 4:8, 0:n], s5o[:, :, 0:n], rstd[:, 1, 0:n].unsqueeze(1).to_broadcast([P, 4, n]), ALU.mult,
           [b_s5o, b_rstd], [bmg], True)
        yield

    def make_ctx(alloc, n, alias, pfx):
        cx = Ctx()
        def mk(name, shape, dt=F32):
            return alloc(pfx + name, shape, dt), B(pfx + name)
        cx.gg, cx.b_gg = mk("gg", (P, 4, n)); cx.ub, cx.b_ub = mk("ub", (P, 4, n), BF16)
        cx.conv, cx.b_conv = mk("conv", (P, 4, n)); cx.convb, cx.b_convb = mk("convb", (P, 4, n), BF16)
        cx.aa, cx.b_aa = mk("aa", (P, 4, n)); cx.bb, cx.b_bb = mk("bb", (P, 4, n))
        cx.gy, cx.b_gy = mk("gy", (P, 4, n), BF16)
        cx.sg, cx.b_sg = mk("sg", (P, 4, n))
        if alias:
            cx.s5o, cx.b_s5o = cx.bb, cx.b_bb
        else:
            cx.s5o, cx.b_s5o = mk("s5o", (P, 4, n))
        cx.sq, cx.b_sq = mk("sq", (P, 8, n), BF16); cx.rstd, cx.b_rstd = mk("rstd", (P, 2, n))
        return cx

    def clone_ctx_A(cx, alloc, n, pfx):
        c2 = Ctx()
        c2.__dict__.update(cx.__dict__)
        c2.gg = alloc(pfx + "gg", [P, 4, n], F32); c2.b_gg = B(pfx + "gg")
        c2.ub = alloc(pfx + "ub", [P, 4, n], BF16); c2.b_ub = B(pfx + "ub")
        return c2

    sp = contextlib.ExitStack()
    sbp = lambda name, shape, dt=F32: sp.enter_context(nc.sbuf_tensor(name, list(shape), dt))
    NXT = 2
    xt = [sbp("xt%d" % i, (P, D)) for i in range(NXT)]; b_xt = [B("xt%d" % i) for i in range(NXT)]
    ringX = Ring(list(zip(xt, b_xt)))
    stat = [sbp("stat%d" % i, (P, 4)) for i in range(4)]; b_stat = [B("stat%d" % i) for i in range(4)]
    ringStat = Ring(list(zip(stat, b_stat)))
    xn = [sbp("xn%d" % i, (P, D), BF16) for i in range(1)]; b_xn = [B("xn%d" % i) for i in range(1)]
    ringXn = Ring(list(zip(xn, b_xn)))
    hnT = sbp("hnT", (P, KD, CH), BF16); b_hnT = B("hnT")
    lrux = [sbp("lrux%d" % i, (P, 4, CH + 3)) for i in range(2)]; b_lrux = [B("lrux%d" % i) for i in range(2)]
    cx0 = make_ctx(sbp, CH, False, "p_")
    gg3 = [(cx0.gg, cx0.b_gg)] + [(sbp("gg%d" % i, (P, 4, CH)), B("gg%d" % i)) for i in (1, 2)]
    ub2 = [(cx0.ub, cx0.b_ub), (sbp("ub1", (P, 4, CH), BF16), B("ub1"))]

    class _CtxList:
        def __getitem__(self, c):
            cx = Ctx()
            cx.__dict__.update(cx0.__dict__)
            cx.gg, cx.b_gg = gg3[c % 3]
            cx.ub, cx.b_ub = ub2[c % 2]
            return cx
    cxs_p = _CtxList()
    hs = cx0.conv; b_hs = cx0.b_conv
    hcar = sbp("hcar", (P, 4)); b_hcar = B("hcar")
    s5e = sbp("s5e", (P, 4, 16, NK)); b_s5x = [B("s5e%d" % i) for i in range(4)]
    Hprev = sbp("Hprev", (P, 2, 16, NK), BF16); b_Hprev = B("Hprev")
    carry = sbp("carry", (P, 4, 16)); b_carry = B("carry")
    ctmp = sbp("ctmp", (P, 4, 16)); b_ctmp = B("ctmp")
    mrg2 = [sbp("mrg%d" % i, (P, KD, CH), BF16) for i in range(2)]; b_mrg2 = [B("mrg%d" % i) for i in range(2)]
    xres = [sbp("xres%d" % i, (P, 512)) for i in range(2)]; b_xres = [B("xres%d" % i) for i in range(2)]
    ringXres = Ring(list(zip(xres, b_xres)))

    mset("vector", carry[:], 0.0, [b_carry])
    mset("vector", lrux[0][:], 0.0, [b_lrux[0]])
    V_RHO = V(I_RHO); V_CN = V(I_CN); V_SN = V(I_SN)
    fl = lambda a: a.rearrange("p a m -> p (a m)")
    qv = lambda a, pl: a.rearrange("p (q pl) m -> p pl q m", pl=4)[:, pl]
    x1_tiles = {}

    def genA(c):
        cx = cxs_p[c]
        for tt_i in range(CH // P):
            tok0 = c * CH + tt_i * P
            xT, bx = ringX.next()
            pr.dma("sync", xT[:], xp_d[tok0:tok0 + P, :], writes=[bx])
            st, bst = ringStat.next()
            xnT, bxn = ringXn.next()
            rms_stats(xT[:], P, st, bst, bx, xnT[:], bxn)
            act(xnT[:], xT[:], AF.Identity, [bx, bst], [bxn], scale=st[:, 1:2])
            yield
            pTr, bTr = ringT.next()
            for k in range(KD):
                tr(pTr[:, k * P:(k + 1) * P], xnT[:, k * P:(k + 1) * P], identB[:], [bxn, b_const], [bTr])
            yield
            for k in range(KD):
                act(hnT[:, k, tt_i * P:(tt_i + 1) * P], pTr[:, k * P:(k + 1) * P], AF.Identity, [bTr, b_pmod], [b_hnT],
                    scale=pmod[:, k:k + 1], bias=pmod[:, 8 + k:9 + k], partial=True)
                if k % 4 == 3:
                    yield
        yield from proj_in(cx, hnT, b_hnT, CH, lrux[c % 2], b_lrux[c % 2], 3, ring=ringA01)

    def genB(c):
        cx = cxs_p[c]
        lx = lrux[c % 2]; blx = b_lrux[c % 2]
        lxn = lrux[(c + 1) % 2]; blxn = b_lrux[(c + 1) % 2]
        cp("gpsimd", lxn[:, :, 0:3], lx[:, :, CH:CH + 3], [blx], [blxn], True)
        conv = cx.conv; b_conv = cx.b_conv
        for t in range(4):
            ts("vector", conv[:, t, :], lx[:, t, 0:CH], cols[:, C_CW + t:C_CW + t + 1], cols[:, C_CB + t:C_CB + t + 1], ALU.mult, ALU.add,
               [blx, b_cols], [b_conv], t > 0)
            for k in range(1, 4):
                stt(conv[:, t, :], lx[:, t, k:k + CH], cols[:, C_CW + 4 * k + t:C_CW + 4 * k + t + 1], conv[:, t, :], ALU.mult, ALU.add,
                    [blx, b_cols, b_conv], [b_conv], True)
            yield
        yield from lru_gates(cx, CH, ring=ringA23)
        for t in range(4):
            init = 0.0 if c == 0 else hcar[:, t:t + 1]
            scan(hs[:, t, :], cx.aa[:, t, :], cx.bb[:, t, :], init, [cx.b_aa, cx.b_bb, b_hcar], [b_hs], t > 0)
            if t % 2 == 1:
                yield
        cp("vector", hcar[:], hs[:, :, CH - 1], [b_hs], [b_hcar])
        tt("gpsimd", cx.gg[:], cx.gg[:], hs[:], ALU.mult, [cx.b_gg, b_hs], [cx.b_gg])
        yield

    def genC(c):
        cx = cxs_p[c]
        ub = cx.ub; b_ub = cx.b_ub
        Epr = s5e[:, 0]; Epi = s5e[:, 1]; t1 = s5e[:, 2]; t2 = s5e[:, 3]
        bEpr, bEpi, bT1, bT2 = b_s5x
        for ri in range(2):
            for plh in range(2):
                banks = {2 * plh: ringA23.next(), 2 * plh + 1: ringA23.next()}
                for q in range(4):
                    for pl in (2 * plh, 2 * plh + 1):
                        pE, bE = banks[pl]
                        uv = ub[pl * 32:(pl + 1) * 32, q, :].rearrange("p (m j) -> p j m", j=L)
                        for j in range(L):
                            mm(pE[:, q * NK:(q + 1) * NK], W1[pl * 32:(pl + 1) * 32, ri, q, j, :], uv[:, j, :], j == 0, j == L - 1,
                               [b_W1, b_ub], [bE], tp=(pl * 32, 0))
                for pl in (2 * plh, 2 * plh + 1):
                    pE, bE = banks[pl]
                    Ev = pE[:, 0:4 * NK].rearrange("p (q m) -> p q m", m=NK)
                    if ri == 0:
                        tt("vector", qv(Epr, pl), Ev, qv(Rc, pl), ALU.mult, [bE, b_rot], [bEpr], True)
                        tt("vector", qv(t2, pl), Ev, qv(Rs, pl), ALU.mult, [bE, b_rot], [bT2], True)
                    else:
                        tt("vector", qv(t1, pl), Ev, qv(Rs, pl), ALU.mult, [bE, b_rot], [bT1], True)
                        tt("vector", qv(Epi, pl), Ev, qv(Rc, pl), ALU.mult, [bE, b_rot], [bEpi], True)
                yield
        tt("vector", Epr, Epr, t1, ALU.add, [bEpr, bT1], [bEpr])
        tt("gpsimd", Epi, Epi, t2, ALU.subtract, [bEpi, bT2], [bEpi])
        yield
        tt("vector", ctmp[:, 0], carry[:, 2], V_RHO, ALU.mult, [b_carry, b_s5t], [b_ctmp])
        tt("vector", ctmp[:, 1], carry[:, 3], V_RHO, ALU.mult, [b_carry, b_s5t, b_ctmp], [b_ctmp])
        tt("vector", Epr[:, :, 0], Epr[:, :, 0], ctmp[:, 0], ALU.add, [bEpr, b_ctmp], [bEpr])
        tt("vector", Epi[:, :, 0], Epi[:, :, 0], ctmp[:, 1], ALU.add, [bEpi, b_ctmp], [bEpi])
        yield
        scan(fl(t1), fl(rho_tab[:]), fl(Epr), 0.0, [bEpr, b_rho], [bT1])
        scan(fl(t2), fl(rho_tab[:]), fl(Epi), 0.0, [bEpi, b_rho], [bT2])
        yield
        cp("vector", Hprev[:, 0, :, 0], carry[:, 0], [b_carry], [b_Hprev])
        cp("vector", Hprev[:, 1, :, 0], carry[:, 1], [b_carry, b_Hprev], [b_Hprev])
        gl_r = t1[:, :, NK - 1]; gl_i = t2[:, :, NK - 1]
        RC = [bT1, bT2, b_s5t, b_ctmp]
        tt("vector", ctmp[:, 0], gl_r, V_CN, ALU.mult, RC, [b_ctmp])
        tt("vector", ctmp[:, 1], gl_i, V_SN, ALU.mult, RC, [b_ctmp])
        tt("vector", ctmp[:, 2], gl_i, V_CN, ALU.mult, RC, [b_ctmp])
        tt("vector", ctmp[:, 3], gl_r, V_SN, ALU.mult, RC, [b_ctmp])
        yield
        tt("vector", carry[:, 2], ctmp[:, 0], ctmp[:, 1], ALU.subtract, [b_ctmp, b_carry], [b_carry])
        tt("vector", carry[:, 3], ctmp[:, 2], ctmp[:, 3], ALU.add, [b_ctmp, b_carry], [b_carry])
        tt("vector", Epr, t1, Rc, ALU.mult, [bT1, b_rot], [bEpr])
        tt("gpsimd", Epi, t2, Rc, ALU.mult, [bT2, b_rot], [bEpi])
        yield
        tt("vector", t1, t1, Rs, ALU.mult, [bT1, b_rot], [bT1])
        tt("gpsimd", t2, t2, Rs, ALU.mult, [bT2, b_rot], [bT2])
        yield
        tt("vector", Hprev[:, 0, :, 1:NK], Epr[:, :, 0:NK - 1], t2[:, :, 0:NK - 1], ALU.subtract, [bEpr, bT2, b_Hprev], [b_Hprev])
        tt("gpsimd", Hprev[:, 1, :, 1:NK], Epi[:, :, 0:NK - 1], t1[:, :, 0:NK - 1], ALU.add, [bEpi, bT1, b_Hprev], [b_Hprev])
        yield
        tt("vector", carry[:, 0], Epr[:, :, NK - 1], t2[:, :, NK - 1], ALU.subtract, [bEpr, bT2, b_carry], [b_carry])
        tt("vector", carry[:, 1], Epi[:, :, NK - 1], t1[:, :, NK - 1], ALU.add, [bEpi, bT1, b_carry], [b_carry])
        yield
        gy = cx.gy; b_gy = cx.b_gy
        for qq in range(2):
            pY, bY = ringA23.next()
            for qi in range(2):
                q = qq * 2 + qi
                uvq = ub[:, q, :].rearrange("p (m j) -> p j m", j=L)
                for j in range(L):
                    o0 = qi * CH + j * NK
                    for i in range(j + 1):
                        mm(pY[:, o0:o0 + NK], Kblk[:, q, j - i, :], uvq[:, i, :], i == 0, False, [b_Kblk, b_ub], [bY])
                    for pl in range(4):
                        p = q * 4 + pl
                        mm(pY[pl * 32:(pl + 1) * 32, o0:o0 + NK], CAb[:, 0, p, j, :], Hprev[:, 0, p, :], False, False,
                           [b_CAb, b_Hprev], [bY], tp=(0, pl * 32))
                        mm(pY[pl * 32:(pl + 1) * 32, o0:o0 + NK], CAb[:, 1, p, j, :], Hprev[:, 1, p, :], False, True,
                           [b_CAb, b_Hprev], [bY], tp=(0, pl * 32))
            for qi in range(2):
                q = qq * 2 + qi
                src = pY[:, qi * CH:(qi + 1) * CH].rearrange("p (j m) -> p j m", j=L)
                act(gy[:, q, :].rearrange("p (m j) -> p j m", j=L), src, AF.Gelu_apprx_tanh, [bY], [b_gy], partial=q > 0)
            yield

    def genD1(c):
        yield from s5_glu_and_merge(cxs_p[c], CH, mrg2[c % 2], b_mrg2[c % 2], ring=ringA23)

    def genD2(c):
        mrg = mrg2[c % 2]; b_mrg = b_mrg2[c % 2]
        for tt_i in range(CH // P):
            tok0 = c * CH + tt_i * P
            x1_tiles[tok0] = [B("x1s_%d_0" % tok0), B("x1s_%d_1" % tok0)]
            for h in range(2):
                bscr = x1_tiles[tok0][h]
                xr_, bxr = ringXres.next()
                pr.dma("sync", xr_[:], xp_d[tok0:tok0 + P, h * 512:(h + 1) * 512], writes=[bxr])
                pO, bO = ringD.next()
                for k in range(KD):
                    mm(pO[:, :], mrg[:, k, tt_i * P:(tt_i + 1) * P], wout_sb[:, k, h * 512:(h + 1) * 512], k == 0, k == KD - 1,
                       [b_mrg, b_wout], [bO])
                yield
                tt("vector", xr_[:], pO[:, :], xr_[:], ALU.add, [bO, bxr], [bxr])
                pr.dma("sync", x1_d[tok0:tok0 + P, h * 512:(h + 1) * 512], xr_[:], reads=[bxr], writes=[bscr])
                yield

    def rr(named):
        alive = {nm: g for nm, g in named}
        tlast = {nm: 0.0 for nm, _ in named}
        while alive:
            nm = min(alive, key=lambda k_: tlast[k_])
            pr.step_max = 0.0
            try:
                next(alive[nm])
                tlast[nm] = max(tlast[nm], pr.step_max)
            except StopIteration:
                del alive[nm]

    run(genA(0))
    for k in range(NCH + 2):
        streams = []
        if 0 <= k - 1 < NCH:
            streams.append(("D1", genD1(k - 1)))
        if k < NCH:
            streams.append(("C", genC(k)))
            streams.append(("B", genB(k)))
        if 0 <= k - 2 < NCH:
            streams.append(("D2", genD2(k - 2)))
        if k + 1 < NCH:
            streams.append(("A", genA(k + 1)))
        rr(streams)

    outst = xt[0]; b_outst = b_xt[0]
    lastc = NCH - 1
    lxl = lrux[lastc % 2]; blxl = b_lrux[lastc % 2]
    pS_, bS_ = ringS.next()
    for t in range(4):
        tr(pS_[0:3, t * P:(t + 1) * P], lxl[:, t, CH:CH + 3], identF[:], [blxl, b_const], [bS_])
    cp("vector", outst[0:3, 0:512], pS_[0:3, :], [bS_], [b_outst])
    pr.dma("sync", convp_d[:, :], outst[0:3, 0:512], reads=[b_outst])
    pS_, bS_ = ringS.next()
    tr(pS_[0:4, 0:P], hcar[:], identF[:], [b_hcar, b_const], [bS_])
    tr(pS_[0:16, P:2 * P], carry[:, 0], identF[:], [b_carry, b_const], [bS_])
    tr(pS_[0:16, 2 * P:3 * P], carry[:, 1], identF[:], [b_carry, b_const], [bS_])
    b_outst2 = B("outst2")
    cp("vector", outst[0:16, 512:512 + 3 * P], pS_[0:16, 0:3 * P], [bS_], [b_outst2])
    pr.dma("sync", lrup_d[:, :], outst[0:4, 512:512 + P], reads=[b_outst2])
    pr.dma("sync", s5rp_d[:, :], outst[0:16, 512 + P:512 + 2 * P], reads=[b_outst2])
    pr.dma("sync", s5ip_d[:, :], outst[0:16, 512 + 2 * P:512 + 3 * P], reads=[b_outst2])

    dbg_stop('prompt')
    pr.barrier(lambda e: e.memset(barr_t[:], 0.0))
    sp.close()

    ss = contextlib.ExitStack()
    sbs = lambda name, shape, dt=F32: ss.enter_context(nc.sbuf_tensor(name, list(shape), dt))
    n = NS
    cxs = make_ctx(sbs, NS, False, "s_")
    xs_sb = sbs("xs_sb", (NS, D)); b_xs = B("xs")
    pr.dma("sync", xs_sb[:], xs_d[:, :], writes=[b_xs])
    sst = sbs("sst", (NS, 4)); b_sst = B("sst")
    stmp = sbs("stmp", (NS, D)); b_stmp = B("stmp")
    hsb = sbs("hsb", (NS, D), BF16); b_hsb = B("hsb")
    smod = sbs("smod", (P, 3, KD, NS)); b_smod = B("smod")
    hTs = sbs("hTs", (P, KD, NS), BF16); b_hTs = B("hTs")
    g1s = sbs("g1s", (NS, D)); b_g1s = B("g1s")
    x1s = sbs("x1s", (NS, D)); b_x1s = B("x1s")
    outs = sbs("outs", (NS, 1024)); b_outs = B("outs")

    def sample_norm_mod(x_ap, bx, gcol0, sc_off, sh_off, dstT, bdst):
        rms_stats(x_ap, NS, sst, b_sst, bx, hsb[:], b_hsb)
        ts("vector", hsb[:], x_ap, sst[:, 1:2], None, ALU.mult, None, [bx, b_sst], [b_hsb])
        pTr, bTr = ringT.next()
        for k in range(KD):
            tr(pTr[:, k * NS:(k + 1) * NS], hsb[:, k * P:(k + 1) * P], identB[0:NS, 0:NS], [b_hsb, b_const], [bTr])
        stt(smod[:, 0], modT[:, sc_off:sc_off + 8, 0:NS], 1.0, cols[:, gcol0:gcol0 + 8].unsqueeze(2).to_broadcast([P, KD, NS]),
            ALU.add, ALU.mult, [b_modT, b_cols], [b_smod])
        tt("vector", smod[:, 1], pTr[:, 0:KD * NS].rearrange("p (k n) -> p k n", n=NS), smod[:, 0], ALU.mult, [bTr, b_smod], [b_smod])
        tt("vector", dstT[:, :, :], smod[:, 1], modT[:, sh_off:sh_off + 8, 0:NS], ALU.add, [b_smod, b_modT], [bdst])

    def gate_rows(dst, bdst, goff):
        for h in range(2):
            pM, bM = ringA.next()
            for i in range(4):
                k = h * 4 + i
                tr(pM[0:NS, i * P:(i + 1) * P], modT[:, goff + k, 0:NS], identF[:], [b_modT, b_const], [bM])
            cp("vector", dst[:, h * 512:(h + 1) * 512], pM[0:NS, :], [bM], [bdst], h > 0)

    sconv_sb = sbs("sconv_sb", (NS, 3 * DL)); b_sconv = B("sconv")
    pr.dma("sync", sconv_sb[:], sconv_d.rearrange("b k c -> b (k c)"), writes=[b_sconv])
    slru_sb = sbs("slru_sb", (NS, DL)); b_slru = B("slru")
    pr.dma("sync", slru_sb[:], slru_d[:, :], writes=[b_slru])
    s5io = [sbs("s5io%d" % i, (NS, 2048)) for i in range(2)]; b_s5io = [B("s5io%d" % i) for i in range(2)]
    for ri, sd in enumerate((ss5r_d, ss5i_d)):
        pr.dma("sync", s5io[ri][:], sd[:, :], writes=[b_s5io[ri]])
    for k in range(KD):
        pr.dma("gpsimd", wout_sb[:, k, :], wout_v[:, k, :], writes=[b_wout], partial=k > 0)
    pr.dma("sync", convs_d[:, 0:2, :].rearrange("b k c -> b (k c)"), sconv_sb[:, DL:3 * DL], reads=[b_sconv])

    sample_norm_mod(xs_sb[:], b_xs, C_N1G, M_SC1, M_SH1, hTs, b_hTs)
    lxs = sbs("lxs", (P, 4, NS)); blxs = B("lxs")
    run(proj_in(cxs, hTs, b_hTs, NS, lxs, blxs, 0))
    sstT = sbs("sstT", (P, 16, NS)); b_sstT = B("sstT")
    hss = sbs("hss", (P, 4, NS)); bhss = B("hss")
    h0 = sbs("h0", (P, 2, 16, NS)); b_h0 = B("h0")
    hn_ = sbs("hn_", (P, 2, 16, NS)); b_hn = B("hn_")
    hnb = sbs("hnb", (P, 2, 16, NS), BF16); b_hnb = B("hnb")
    htmp = sbs("htmp", (P, 2, 16, NS)); b_htmp = B("htmp")
    ubs = cxs.ub; b_ubs = cxs.b_ub

    def gen_s_lru():
        pS_, bS_ = ringS.next()
        for i in range(12):
            tr(pS_[:, i * NS:(i + 1) * NS], sconv_sb[:, i * P:(i + 1) * P], identF[0:NS, 0:NS], [b_sconv, b_const], [bS_])
        for i in range(4):
            tr(pS_[:, (12 + i) * NS:(13 + i) * NS], slru_sb[:, i * P:(i + 1) * P], identF[0:NS, 0:NS], [b_slru, b_const], [bS_])
        cp("vector", sstT[:].rearrange("p a n -> p (a n)"), pS_[:, 0:16 * NS], [bS_], [b_sstT])
        yield
        conv = cxs.conv; b_conv = cxs.b_conv
        for t in range(4):
            ts("vector", conv[:, t, :], sstT[:, t, :], cols[:, C_CW + t:C_CW + t + 1], cols[:, C_CB + t:C_CB + t + 1], ALU.mult, ALU.add,
               [b_sstT, b_cols], [b_conv], t > 0)
            for k in range(1, 3):
                stt(conv[:, t, :], sstT[:, k * 4 + t, :], cols[:, C_CW + 4 * k + t:C_CW + 4 * k + t + 1], conv[:, t, :], ALU.mult, ALU.add,
                    [b_sstT, b_cols, b_conv], [b_conv], True)
            stt(conv[:, t, :], lxs[:, t, :], cols[:, C_CW + 12 + t:C_CW + 12 + t + 1], conv[:, t, :], ALU.mult, ALU.add,
                [blxs, b_cols, b_conv], [b_conv], True)
            yield
        yield from lru_gates(cxs, NS)
        tt("vector", hss[:], cxs.aa[:], sstT[:, 12:16, :], ALU.mult, [cxs.b_aa, b_sstT], [bhss])
        tt("vector", hss[:], hss[:], cxs.bb[:], ALU.add, [bhss, cxs.b_bb], [bhss])
        tt("vector", cxs.gg[:], cxs.gg[:], hss[:], ALU.mult, [cxs.b_gg, bhss], [cxs.b_gg])
        yield
        pS_, bS_ = ringS.next()
        for t in range(4):
            tr(pS_[0:NS, t * P:(t + 1) * P], lxs[:, t, :], identF[:], [blxs, b_const], [bS_])
        cp("vector", outs[:, 0:512], pS_[0:NS, :], [bS_], [b_outs])
        pr.dma("sync", convs_d[:, 2, :], outs[:, 0:512], reads=[b_outs])
        yield
        pS_, bS_ = ringS.next()
        for t in range(4):
            tr(pS_[0:NS, t * P:(t + 1) * P], hss[:, t, :], identF[:], [bhss, b_const], [bS_])
        b_outs2 = B("outs2")
        cp("vector", outs[:, 512:1024], pS_[0:NS, :], [bS_], [b_outs2])
        pr.dma("sync", lrus_d[:, :], outs[:, 512:1024], reads=[b_outs2])
        yield

    def gen_s_s5():
        gate_rows(g1s, b_g1s, M_G1)
        yield
        for ri in range(2):
            pS_, bS_ = ringS.next()
            for p in range(16):
                tr(pS_[:, p * NS:(p + 1) * NS], s5io[ri][:, p * P:(p + 1) * P], identF[0:NS, 0:NS], [b_s5io[ri], b_const], [bS_])
            cp("vector", h0[:, ri].rearrange("p a n -> p (a n)"), pS_[:, 0:16 * NS], [bS_], [b_h0], ri > 0)
            yield
        abr3 = V(I_ABR).unsqueeze(2).to_broadcast([P, 16, NS]); abi3 = V(I_ABI).unsqueeze(2).to_broadcast([P, 16, NS])
        tt("vector", hn_[:, 0], h0[:, 0], abr3, ALU.mult, [b_h0, b_s5t], [b_hn])
        tt("gpsimd", htmp[:, 0], h0[:, 1], abi3, ALU.mult, [b_h0, b_s5t], [b_htmp])
        yield
        tt("vector", hn_[:, 0], hn_[:, 0], htmp[:, 0], ALU.subtract, [b_hn, b_htmp], [b_hn])
        tt("vector", hn_[:, 1], h0[:, 1], abr3, ALU.mult, [b_h0, b_s5t, b_hn], [b_hn])
        tt("gpsimd", htmp[:, 1], h0[:, 0], abi3, ALU.mult, [b_h0, b_s5t, b_htmp], [b_htmp])
        yield
        tt("vector", hn_[:, 1], hn_[:, 1], htmp[:, 1], ALU.add, [b_hn, b_htmp], [b_hn])
        qs = lambda a, pl: a.rearrange("p (q pl) n -> p pl q n", pl=4)[:, pl]
        for ri in range(2):
            banks = [ringA.next() for _ in range(4)]
            for q in range(4):
                for pl in range(4):
                    pBu, bBu = banks[pl]
                    mm(pBu[:, q * NS:(q + 1) * NS], W1[pl * 32:(pl + 1) * 32, ri, q, L - 1, :], ubs[pl * 32:(pl + 1) * 32, q, :], True, True,
                       [b_W1, b_ubs], [bBu], tp=(pl * 32, 0))
            for pl in range(4):
                pBu, bBu = banks[pl]
                tt("vector", qs(hn_[:, ri], pl), qs(hn_[:, ri], pl), pBu[:, 0:4 * NS].rearrange("p (q n) -> p q n", n=NS), ALU.add,
                   [b_hn, bBu], [b_hn])
            yield
        cp("vector", hnb[:], hn_[:], [b_hn], [b_hnb])
        pY, bY = ringA.next()
        for q in range(4):
            mm(pY[:, q * NS:(q + 1) * NS], Dblk[:, q, :], ubs[:, q, :], True, False, [b_Dblk, b_ubs], [bY])
            for pl in range(4):
                p = q * 4 + pl
                o = pY[pl * 32:(pl + 1) * 32, q * NS:(q + 1) * NS]
                mm(o, CT0[:, 0, p, :], hnb[:, 0, p, :], False, False, [b_CT0, b_hnb], [bY], tp=(0, pl * 32))
                mm(o, CT0[:, 1, p, :], hnb[:, 1, p, :], False, True, [b_CT0, b_hnb], [bY], tp=(0, pl * 32))
        act(cxs.gy[:, :, :], pY[:, 0:4 * NS].rearrange("p (q n) -> p q n", n=NS), AF.Gelu_apprx_tanh, [bY], [cxs.b_gy])
        yield
        for ri, sd in enumerate((s5rs_d, s5is_d)):
            for g4 in range(4):
                pS_, bS_ = ringS.next()
                for i in range(4):
                    p = g4 * 4 + i
                    tr(pS_[0:NS, i * P:(i + 1) * P], hn_[:, ri, p, :], identF[:], [b_hn, b_const], [bS_])
                cp("scalar", s5io[ri][:, g4 * 512:(g4 + 1) * 512], pS_[0:NS, :], [bS_], [b_s5io[ri]], g4 > 0)
                yield
            pr.dma("sync", sd[:, :], s5io[ri][:], reads=[b_s5io[ri]])

    def rr_plain(gens):
        gens = list(gens)
        while gens:
            for g in list(gens):
                try:
                    next(g)
                except StopIteration:
                    gens.remove(g)

    rr_plain([gen_s_lru(), gen_s_s5()])
    mgs = sbs("mgs", (P, KD, NS), BF16); bmgs = B("mgs")
    run(s5_glu_and_merge(cxs, NS, mgs, bmgs))
    for k in range(KD):
        gc = (C_GLO + k) if k < 4 else (C_GSO + k - 4)
        ts("vector", wout_sb[:, k, :], wout_sb[:, k, :], cols[:, gc:gc + 1], None, ALU.mult, None, [b_wout, b_cols], [b_wout])
    for h in range(2):
        pO, bO = ringA.next()
        for k in range(KD):
            mm(pO[0:NS, :], mgs[:, k, :], wout_sb[:, k, h * 512:(h + 1) * 512], k == 0, k == KD - 1, [bmgs, b_wout], [bO])
        tt("vector", stmp[:, h * 512:(h + 1) * 512], pO[0:NS, :], g1s[:, h * 512:(h + 1) * 512], ALU.mult,
           [bO, b_g1s], [b_stmp], h > 0)
    tt("vector", x1s[:], xs_sb[:], stmp[:], ALU.add, [b_xs, b_stmp], [b_x1s])
    pr.dma("sync", x1_d[T:T + NS, :], x1s[:], reads=[b_x1s], writes=[b_x1s_scr])
    sample_norm_mod(x1s[:], b_x1s, C_N2G, M_SC2, M_SH2, hn2Ts, b_hn2Ts)

    dbg_stop('sample')
    pr.barrier(lambda e: e.memset(barr_t[:], 0.0))
    ss.close()
    s1.close()
    s2 = contextlib.ExitStack()
    sb2 = lambda name, shape, dt=F32: s2.enter_context(nc.sbuf_tensor(name, list(shape), dt))
    wg_sb = sb2("wg_sb", (P, KD, DFF), BF16)
    wu_sb = sb2("wu_sb", (P, KD, DFF), BF16)
    wd_sb = sb2("wd_sb", (P, NF, D), BF16)
    NBLK = (DFF + 511) // 512
    b_wg = [B("wg%d" % i) for i in range(NBLK)]; b_wu = [B("wu%d" % i) for i in range(NBLK)]
    b_wd = [B("wd%d" % i) for i in range(NF // 2)]
    wg_v = wg_d.rearrange("(k p) n -> p k n", p=P)
    wu_v = wu_d.rearrange("(k p) n -> p k n", p=P)
    wd_v = wd_d.rearrange("(f p) n -> p f n", p=P)
    for blk in range(NBLK):
        c0 = blk * 512; c1 = min(DFF, c0 + 512)
        pr.dma("gpsimd", wg_sb[:, :, c0:c1], wg_v[:, :, c0:c1], writes=[b_wg[blk]])
        pr.dma("gpsimd", wu_sb[:, :, c0:c1], wu_v[:, :, c0:c1], writes=[b_wu[blk]])
    for i in range(NF // 2):
        pr.dma("gpsimd", wd_sb[:, 2 * i:2 * i + 2, :], wd_v[:, 2 * i:2 * i + 2, :], writes=[b_wd[i]])

    G2bc = sb2("G2bc", (P, D)); FNGbc = sb2("FNGbc", (P, D)); b_G2 = B("G2bc"); b_FNG = B("FNGbc")
    xa = [sb2("xa%d" % i, (P, D)) for i in range(2)]; b_xa = [B("xa%d" % i) for i in range(2)]
    xr = [sb2("xr%d" % i, (P, D)) for i in range(2)]; b_xr = [B("xr%d" % i) for i in range(2)]
    ringXa = Ring(list(zip(xa, b_xa))); ringXr = Ring(list(zip(xr, b_xr)))
    stat2 = [sb2("stat2_%d" % i, (P, 4)) for i in range(4)]; b_stat2 = [B("stat2_%d" % i) for i in range(4)]
    ringStat2 = Ring(list(zip(stat2, b_stat2)))
    xn2 = [sb2("xn2_%d" % i, (P, D), BF16) for i in range(2)]; b_xn2 = [B("xn2_%d" % i) for i in range(2)]
    ringXn2 = Ring(list(zip(xn2, b_xn2)))
    hn2T = [sb2("hn2T%d" % i, (P, KD, CH2), BF16) for i in range(2)]; b_hn2T = [B("hn2T%d" % i) for i in range(2)]
    actT = sb2("actT", (P, NF, CH2), BF16); b_actT = B("actT")
    sil = [sb2("sil%d" % i, (P, CH2), BF16) for i in range(1)]; b_sil = [B("sil%d" % i) for i in range(1)]
    ringSil = Ring(list(zip(sil, b_sil)))
    tmp2x = sb2("tmp2x", (P, 512)); b_tmp2x = B("tmp2x")

    bcast_rows(G2bc, b_G2, modT[:, M_G2:M_G2 + 8, NS], b_modT, P, tmp2x, b_tmp2x)
    bcast_rows(FNGbc, b_FNG, cols[:, C_FNG:C_FNG + 8], b_cols, P, tmp2x, b_tmp2x)

    actTs = sb2("actTs", (P, NF, NS), BF16); b_actTs = B("actTs")

    def gen_gate_up(hT_ap, bh, n, with_sample=False):
        for f in range(NF):
            if with_sample:
                pGs, bGs = ringAll.next()
                for k in range(KD):
                    mm(pGs[:, 0:NS], wg_sb[:, k, f * P:(f + 1) * P], hn2Ts[:, k, :], k == 0, k == KD - 1, [b_wg[f // 4], b_hn2Ts], [bGs])
                for k in range(KD):
                    mm(pGs[:, NS:2 * NS], wu_sb[:, k, f * P:(f + 1) * P], hn2Ts[:, k, :], k == 0, k == KD - 1, [b_wu[f // 4], b_hn2Ts], [bGs])
                sls, bsls = ringSil.next()
                act(sls[:, 0:NS], pGs[:, 0:NS], AF.Silu, [bGs], [bsls])
                tt("vector", actTs[:, f, :], pGs[:, NS:2 * NS], sls[:, 0:NS], ALU.mult, [bGs, bsls], [b_actTs], f > 0)
            pG, bG = ringAll.next()
            pU, bU = ringAll.next()
            for k in range(KD):
                mm(pG[:, 0:n], wg_sb[:, k, f * P:(f + 1) * P], hT_ap[:, k, :], k == 0, k == KD - 1, [b_wg[f // 4], bh], [bG])
            for k in range(KD):
                mm(pU[:, 0:n], wu_sb[:, k, f * P:(f + 1) * P], hT_ap[:, k, :], k == 0, k == KD - 1, [b_wu[f // 4], bh], [bU])
            sl, bsl = ringSil.next()
            act(sl[:, 0:n], pG[:, 0:n], AF.Silu, [bG], [bsl])
            tt("vector", actT[:, f, 0:n], pU[:, 0:n], sl[:, 0:n], ALU.mult, [bU, bsl], [b_actT], f > 0)
            yield

    def gen_down(x_ap, bx, rows, col0, g2_ap, bg2, out_ap, junk_ap, bjunk, act_src=None):
        aT, b_aT = act_src if act_src is not None else (actT, b_actT)
        for h in range(2):
            pO, bO = ringAll.next()
            for f in range(NF):
                mm(pO[0:rows, :], aT[:, f, col0:col0 + rows], wd_sb[:, f, h * 512:(h + 1) * 512], f == 0, f == NF - 1,
                   [b_aT, b_wd[f // 2]], [bO])
            tt("vector", tmp2x[0:rows, :], pO[0:rows, :], g2_ap[0:rows, h * 512:(h + 1) * 512], ALU.mult, [bO, bg2], [b_tmp2x])
            tt("gpsimd", x_ap[:, h * 512:(h + 1) * 512], x_ap[:, h * 512:(h + 1) * 512], tmp2x[0:rows, :], ALU.add, [bx, b_tmp2x], [bx])
            yield
        st, bst = ringStat2.next()
        rms_stats(x_ap, rows, st, bst, bx, junk_ap, bjunk)
        stt(x_ap, x_ap, st[0:rows, 1:2], FNGbc[0:rows, :], ALU.mult, ALU.mult, [bx, bst, b_FNG], [bx])
        pr.dma("sync", out_ap, x_ap, reads=[bx])
        yield

    def gen_norm2(c):
        hT = hn2T[c % 2]; bh = b_hn2T[c % 2]
        for tt_i in range(CH2 // P):
            tok0 = c * CH2 + tt_i * P
            xT, bx = ringXa.next()
            pr.dma("sync", xT[:], x1_d[tok0:tok0 + P, :], reads=x1_tiles[tok0], writes=[bx])
            st, bst = ringStat2.next()
            xnT, bxn = ringXn2.next()
            rms_stats(xT[:], P, st, bst, bx, xnT[:], bxn)
            ts("vector", xnT[:], xT[:], st[:, 1:2], None, ALU.mult, None, [bx, bst], [bxn])
            yield
            pTr, bTr = ringT.next()
            for k in range(KD):
                tr(pTr[:, k * P:(k + 1) * P], xnT[:, k * P:(k + 1) * P], identB[:], [bxn, b_const], [bTr])
            for k in range(KD):
                ts("vector", hT[:, k, tt_i * P:(tt_i + 1) * P], pTr[:, k * P:(k + 1) * P],
                   pmod[:, 16 + k:17 + k], pmod[:, 24 + k:25 + k], ALU.mult, ALU.add, [bTr, b_pmod], [bh], True)
            yield

    def gen_chunk(c):
        yield from gen_gate_up(hn2T[c % 2][:, :, :], b_hn2T[c % 2], CH2, with_sample=(c == 0))
        for tt_i in range(CH2 // P):
            tok0 = c * CH2 + tt_i * P
            xT, bx = ringXr.next()
            pr.dma("sync", xT[:], x1_d[tok0:tok0 + P, :], reads=x1_tiles[tok0], writes=[bx])
            xnT, bxn = ringXn2.next()
            yield from gen_down(xT[:], bx, P, tt_i * P, G2bc, b_G2, yp_d[tok0:tok0 + P, :], xnT[:], bxn)

    for _ in gen_norm2(0):
        pass
    for c in range(NCH2):
        main = gen_chunk(c)
        side = gen_norm2(c + 1) if c + 1 < NCH2 else iter(())
        step = 0
        for _ in main:
            step += 1
            if step % 3 == 0:
                next(side, None)
        for _ in side:
            pass
    xS, bxS = ringXr.next()
    gS, bgS = ringXa.next()
    pr.dma("sync", xS[0:NS, :], x1_d[T:T + NS, :], reads=[b_x1s_scr], writes=[bxS])
    for h in range(2):
        pM, bM = ringAll.next()
        for i in range(4):
            k = h * 4 + i
            tr(pM[0:NS, i * P:(i + 1) * P], modT[:, M_G2 + k, 0:NS], identF[:], [b_modT, b_const], [bM])
        cp("vector", gS[0:NS, h * 512:(h + 1) * 512], pM[0:NS, :], [bM], [bgS], h > 0)
    xnT, bxn = ringXn2.next()
    for _ in gen_down(xS[0:NS, :], bxS, NS, 0, gS, bgS, ys_d[:, :], xnT[0:NS, :], bxn, act_src=(actTs, b_actTs)):
        pass

    pr.emit()
    s2.close()
    es.close()
    return nc


_CACHE = {}


def kernel(**inputs):
    f32 = lambda a: np.ascontiguousarray(np.asarray(a, dtype=np.float32))
    g = {k: f32(v) for k, v in inputs.items()}
    if "nc" not in _CACHE:
        _CACHE["nc"] = build_program()
    nc = _CACHE["nc"]
    rowpack = np.zeros((128, 128), np.float32)
    rowpack[0:48] = g["ada_b"][0].reshape(48, 128)
    rowpack[48:56] = g["norm1_g"][0].reshape(8, 128)
    rowpack[56:64] = g["norm2_g"][0].reshape(8, 128)
    rowpack[64:80] = g["conv_w"][0].reshape(16, 128)
    rowpack[80:84] = g["conv_b"][0].reshape(4, 128)
    rowpack[84:88] = g["lru_ba"][0].reshape(4, 128)
    rowpack[88:92] = g["lru_bx"][0].reshape(4, 128)
    rowpack[92:96] = g["lru_lambda"][0].reshape(4, 128)
    rowpack[96:100] = g["s5_d"][0].reshape(4, 128)
    rowpack[100:104] = g["g_lru_out"][0].reshape(4, 128)
    rowpack[104:108] = g["g_s5_out"][0].reshape(4, 128)
    rowpack[108:116] = g["final_norm_g"].reshape(8, 128)
    s5pack = np.zeros((48, 128), np.float32)
    s5pack[0:16] = g["s5_lambda_re"][0].reshape(16, 128)
    s5pack[16:32] = g["s5_lambda_im"][0].reshape(16, 128)
    s5pack[32:48] = np.repeat(g["s5_log_dt"][0].reshape(16, 2, 1), 64, axis=2).reshape(16, 128)
    shared = {
        "ada_w": g["ada_w"][0], "rowpack": rowpack, "s5pack": s5pack,
        "w_in": g["w_in"][0], "lru_wa": g["lru_wa"][0], "lru_wx": g["lru_wx"][0],
        "s5_b_re": g["s5_b_re"][0], "s5_b_im": g["s5_b_im"][0], "s5_c_re": g["s5_c_re"][0], "s5_c_im": g["s5_c_im"][0],
        "w_glu": g["s5_w_glu"][0], "w_out": g["w_out"][0],
        "ffn_w_gate": g["ffn_w_gate"][0], "ffn_w_up": g["ffn_w_up"][0], "ffn_w_down": g["ffn_w_down"][0],
    }
    shared = {k: np.ascontiguousarray(v) for k, v in shared.items()}
    in_maps = []
    for i in range(8):
        sl = slice(16 * i, 16 * i + 16)
        m = dict(shared)
        m["xp"] = g["x_prompt"][i]
        m["xs"] = np.ascontiguousarray(g["x_sample"][sl, 0, :])
        m["sconv"] = np.ascontiguousarray(g["state_conv"][0, sl])
        m["slru"] = np.ascontiguousarray(g["state_lru"][0, sl])
        m["ss5r"] = np.ascontiguousarray(g["state_s5_re"][0, sl].reshape(16, 2048))
        m["ss5i"] = np.ascontiguousarray(g["state_s5_im"][0, sl].reshape(16, 2048))
        m["c_all"] = np.ascontiguousarray(np.concatenate([g["c_sample"][sl], g["c_prompt"][i:i + 1]], axis=0))
        in_maps.append(m)
    res = run_bass_kernel_spmd(nc, in_maps, core_ids=list(range(8)))
    r = res.results
    cat = lambda key, shp: np.stack([np.asarray(r[i][key], dtype=np.float32).reshape(shp) for i in range(8)], 0)
    y_prompt = cat("yp", (T, D))
    y_sample = cat("ys", (NS, D)).reshape(128, 1, D)
    conv_prompt = cat("convp", (3, DL))[None]
    lru_prompt = cat("lrup", (DL,))[None]
    s5_re_prompt = cat("s5rp", (32, 64))[None]
    s5_im_prompt = cat("s5ip", (32, 64))[None]
    conv_sample = cat("convs", (NS, 3, DL)).reshape(1, 128, 3, DL)
    lru_sample = cat("lrus", (NS, DL)).reshape(1, 128, DL)
    s5_re_sample = cat("s5rs", (NS, 32, 64)).reshape(1, 128, 32, 64)
    s5_im_sample = cat("s5is", (NS, 32, 64)).reshape(1, 128, 32, 64)
    return (y_prompt, y_sample, conv_prompt, lru_prompt, s5_re_prompt, s5_im_prompt,
            conv_sample, lru_sample, s5_re_sample, s5_im_sample)
```

```python
import math
import contextlib
import numpy as np
import concourse.bass as bass
import concourse.mybir as mybir
from concourse.bass_utils import run_bass_kernel_spmd

F32 = mybir.dt.float32
BF16 = mybir.dt.bfloat16
AF = mybir.ActivationFunctionType
ALU = mybir.AluOpType

ENGINES = ("tensor", "vector", "scalar", "gpsimd", "sync")


class Buf:
    __slots__ = ("name", "writers", "readers", "excl")

    def __init__(self, name, excl=False):
        self.name = name
        self.writers = []
        self.readers = []
        self.excl = excl


class Op:
    __slots__ = ("eng", "fn", "deps", "is_dma", "grp", "grp_val", "signal", "sig_val", "finish")

    def __init__(self, eng, fn, is_dma=False, grp=None):
        self.eng = eng
        self.fn = fn
        self.deps = []
        self.is_dma = is_dma
        self.grp = grp
        self.grp_val = 0
        self.signal = False
        self.sig_val = 0
        self.finish = 0.0


class Prog:
    def __init__(self, nc):
        self.nc = nc
        self.ops = {e: [] for e in ENGINES}
        self.grp_count = {}
        self.all_ops = []
        self.barrier_op = None
        self.eng_free = {e: 0.0 for e in ENGINES}
        self.step_max = 0.0
        self.next_cost = None
        self.xlat = 0.4

    def _add_deps(self, op, reads, writes, partial):
        deps = []
        if self.barrier_op is not None:
            deps.append(self.barrier_op)
        for b in reads:
            deps.extend(b.writers)
            if b.excl:
                deps.extend(r for r in b.readers if r.eng != op.eng)
        for b in writes:
            deps.extend(b.readers)
            if partial:
                deps.extend(w for w in b.writers
                            if not ((w.is_dma and op.is_dma) or (w.eng == op.eng and not w.is_dma and not op.is_dma)))
            else:
                deps.extend(b.writers)
        seen = set()
        for d in deps:
            if d is op or id(d) in seen:
                continue
            seen.add(id(d))
            if d.eng == "tensor" and op.eng == "tensor" and not d.is_dma and not op.is_dma:
                continue
            op.deps.append(d)
        for b in reads:
            b.readers.append(op)
        for b in writes:
            if partial and not b.readers:
                b.writers.append(op)
            else:
                b.writers = [op]
                b.readers = []

    def _push(self, o):
        cost = self.next_cost if self.next_cost is not None else 0.3
        self.next_cost = None
        ready = max([d.finish + (0.0 if d.eng == o.eng else self.xlat) for d in o.deps], default=0.0)
        if o.is_dma:
            start = max(self.eng_free[o.eng], ready)
            self.eng_free[o.eng] = start + 0.1
            o.finish = start + 2.0 + cost
        else:
            start = max(self.eng_free[o.eng], ready)
            o.finish = start + cost
            self.eng_free[o.eng] = o.finish
        if o.finish > self.step_max:
            self.step_max = o.finish
        self.ops[o.eng].append(o)
        self.all_ops.append(o)
        return o

    def op(self, eng, fn, reads=(), writes=(), partial=False):
        o = Op(eng, fn)
        self._add_deps(o, reads, writes, partial)
        return self._push(o)

    def dma(self, eng, out, in_, reads=(), writes=(), partial=False, **kw):
        grp = writes[0] if writes else reads[0]
        key = id(grp)
        o = Op(eng, lambda e: e.dma_start(out=out, in_=in_, **kw), is_dma=True, grp=key)
        self.grp_count[key] = self.grp_count.get(key, 0) + 1
        o.grp_val = 16 * self.grp_count[key]
        self._add_deps(o, reads, writes, partial)
        return self._push(o)

    def barrier(self, fn, eng="vector", exempt=()):
        o = Op(eng, fn)
        for e in ENGINES:
            for prev in reversed(self.ops[e]):
                if not prev.is_dma:
                    o.deps.append(prev)
                    break
        lastg = {}
        skip = set(id(b_) for b_ in exempt)
        for prev in self.all_ops:
            if prev.is_dma and prev.grp not in skip:
                lastg[prev.grp] = prev
        o.deps.extend(lastg.values())
        self.barrier_op = o
        return self._push(o)

    def emit(self, final_wait_eng="sync"):
        nc = self.nc
        for o in self.all_ops:
            for d in o.deps:
                if not d.is_dma:
                    d.signal = True
        last_sig = {}
        for e in ENGINES:
            c = 0
            for o in self.ops[e]:
                if o.signal:
                    c += 1
                    o.sig_val = c
            last_sig[e] = c
        stack = contextlib.ExitStack()
        esem = {e: stack.enter_context(nc.semaphore("s_" + e)) for e in ENGINES}
        gsem = {}
        for k in self.grp_count:
            gsem[k] = stack.enter_context(nc.semaphore("g%d" % len(gsem)))
        final = [(k, 16 * n) for k, n in self.grp_count.items()]
        ops = self.ops
        block = stack.enter_context(nc.Block())

        def make(ename):
            def body(eng):
                waited_e = {e: 0 for e in ENGINES}
                waited_g = {}
                for o in ops[ename]:
                    need_g = {}
                    need_e = {}
                    for d in o.deps:
                        if d.is_dma:
                            if need_g.get(d.grp, 0) < d.grp_val:
                                need_g[d.grp] = d.grp_val
                        else:
                            if need_e.get(d.eng, 0) < d.sig_val:
                                need_e[d.eng] = d.sig_val
                    for gk, gv in need_g.items():
                        if waited_g.get(gk, 0) < gv:
                            eng.wait_ge(gsem[gk], gv)
                            waited_g[gk] = gv
                    for ek, ev in need_e.items():
                        if waited_e[ek] < ev:
                            eng.wait_ge(esem[ek], ev)
                            waited_e[ek] = ev
                    ins = o.fn(eng)
                    if o.is_dma:
                        ins.then_inc(gsem[o.grp], 16)
                    elif o.signal:
                        ins.then_inc(esem[ename], 1)
                if ename == final_wait_eng:
                    for k, v in final:
                        if waited_g.get(k, 0) < v:
                            eng.wait_ge(gsem[k], v)
                    for e in ENGINES:
                        if e != ename and last_sig[e] and waited_e[e] < last_sig[e]:
                            eng.wait_ge(esem[e], last_sig[e])
            return body

        for e in ENGINES:
            getattr(block, e)(make(e))
        stack.close()


class Ring:
    def __init__(self, items):
        self.items = items
        self.i = 0

    def next(self):
        it = self.items[self.i % len(self.items)]
        self.i += 1
        return it


P = 128
D = 1024
KD = 8
T = 2048
NS = 16
NB = NS + 1
DL = 512
DFF = 2816
NF = 22
CH = 256
NCH = T // CH
L = 4
NK = CH // L
CH2 = 512
NCH2 = T // CH2
EPS = 1e-6
MAGIC = 12582912.0
TWO_PI = 2.0 * math.pi

C_ADB, C_N1G, C_N2G, C_CW, C_CB, C_BA, C_BX, C_LAM, C_SD, C_GLO, C_GSO, C_FNG = (
    0, 48, 56, 64, 80, 84, 88, 92, 96, 100, 104, 108)
M_SH1, M_SC1, M_G1, M_SH2, M_SC2, M_G2 = 0, 8, 16, 24, 32, 40


class Ctx:
    pass


DEBUG_STOP = None


class _Stop(Exception):
    pass


def build_program():
    try:
        return _build_program()
    except _Stop as e:
        return e.args[0]


def _build_program():
    nc = bass.Bass("TRN2", target_bir_lowering=False)
    din = lambda name, shape: nc.dram_tensor(name, list(shape), F32, kind="ExternalInput").ap()
    dout = lambda name, shape: nc.dram_tensor(name, list(shape), F32, kind="ExternalOutput").ap()
    xp_d = din("xp", (T, D)); xs_d = din("xs", (NS, D))
    sconv_d = din("sconv", (NS, 3, DL)); slru_d = din("slru", (NS, DL))
    ss5r_d = din("ss5r", (NS, 2048)); ss5i_d = din("ss5i", (NS, 2048))
    call_d = din("c_all", (NB, D))
    adaw_d = din("ada_w", (D, 6 * D))
    rowpack_d = din("rowpack", (P, P)); s5pack_d = din("s5pack", (48, P))
    win_d = din("w_in", (D, 1536)); wa_d = din("lru_wa", (8, 64, 64)); wx_d = din("lru_wx", (8, 64, 64))
    bre_d = din("s5_b_re", (32, 64, 16)); bim_d = din("s5_b_im", (32, 64, 16))
    cre_d = din("s5_c_re", (32, 16, 64)); cim_d = din("s5_c_im", (32, 16, 64))
    wglu_d = din("w_glu", (DL, 2 * DL)); wout_d = din("w_out", (D, D))
    wg_d = din("ffn_w_gate", (D, DFF)); wu_d = din("ffn_w_up", (D, DFF)); wd_d = din("ffn_w_down", (DFF, D))

    yp_d = dout("yp", (T, D)); ys_d = dout("ys", (NS, D))
    convp_d = dout("convp", (3, DL)); lrup_d = dout("lrup", (4, P))
    s5rp_d = dout("s5rp", (16, P)); s5ip_d = dout("s5ip", (16, P))
    convs_d = dout("convs", (NS, 3, DL)); lrus_d = dout("lrus", (NS, DL))
    s5rs_d = dout("s5rs", (NS, 2048)); s5is_d = dout("s5is", (NS, 2048))
    x1_d = nc.dram_tensor("x1_scratch", [T + NS, D], F32, kind="Internal").ap()

    pr = Prog(nc)
    B = Buf

    def dbg_stop(tag):
        if DEBUG_STOP == tag:
            pr.emit()
            raise _Stop(nc)
    es = contextlib.ExitStack()
    sb = lambda name, shape, dt=F32: es.enter_context(nc.sbuf_tensor(name, list(shape), dt))
    ps = lambda name, shape, dt=F32: es.enter_context(nc.psum_tensor(name, list(shape), dt))

    def _n(ap):
        n = 1
        for s_ in ap.shape[1:]:
            n *= int(s_)
        return n

    def _cost(eng, out, f):
        n = _n(out)
        if eng == "vector":
            return 0.08 + n * f / 960.0
        if eng == "scalar":
            return 0.22 + n / 1400.0
        if eng == "gpsimd":
            return 0.15 + n * 2.3 / 1000.0
        return 0.3

    def tt(eng, out, in0, in1, op, r, w, partial=False):
        pr.next_cost = _cost(eng, out, 1.5)
        pr.op(eng, lambda e: e.tensor_tensor(out=out, in0=in0, in1=in1, op=op), r, w, partial)

    def ts(eng, out, in0, s1, s2, op0, op1, r, w, partial=False):
        pr.next_cost = _cost(eng, out, 1.0)
        if op1 is None:
            pr.op(eng, lambda e: e.tensor_scalar(out=out, in0=in0, scalar1=s1, scalar2=None, op0=op0), r, w, partial)
        else:
            pr.op(eng, lambda e: e.tensor_scalar(out=out, in0=in0, scalar1=s1, scalar2=s2, op0=op0, op1=op1), r, w, partial)

    def stt(out, in0, scalar, in1, op0, op1, r, w, partial=False):
        pr.next_cost = _cost("vector", out, 1.7)
        pr.op("vector", lambda e: e.scalar_tensor_tensor(out=out, in0=in0, scalar=scalar, in1=in1, op0=op0, op1=op1), r, w, partial)

    def cp(eng, out, in_, r, w, partial=False):
        pr.next_cost = _cost(eng, out, 1.0)
        if eng == "scalar":
            pr.op(eng, lambda e: e.copy(out=out, in_=in_), r, w, partial)
        else:
            pr.op(eng, lambda e: e.tensor_copy(out=out, in_=in_), r, w, partial)

    def act(out, in_, func, r, w, scale=None, bias=None, accum=None, partial=False):
        pr.next_cost = _cost("scalar", out, 1.0)
        kw = {}
        if scale is not None:
            kw["scale"] = scale
        if bias is not None:
            kw["bias"] = bias
        if accum is not None:
            kw["accum_out"] = accum
        pr.op("scalar", lambda e: e.activation(out=out, in_=in_, func=func, **kw), r, w, partial)

    def mset(eng, ap, val, w, partial=False):
        pr.op(eng, lambda e: e.memset(ap, val), (), w, partial)

    def mm(out, lhsT, rhs, start, stop, r, w, tp=None):
        pr.next_cost = 0.05 + max(64, _n(out)) / 2400.0 * (4.0 if lhsT.dtype == F32 else 1.0)
        if tp is None:
            pr.op("tensor", lambda e: e.matmul(out, lhsT, rhs, start=start, stop=stop), r, w, True)
        else:
            pr.op("tensor", lambda e: e.matmul(out, lhsT, rhs, start=start, stop=stop, tile_position=tp), r, w, True)

    def tr(out, in_, ident, r, w):
        pr.next_cost = 0.05 + max(64, _n(out)) / 2400.0
        pr.op("tensor", lambda e: e.transpose(out, in_, ident), r, w, True)

    def scan(out, d0, d1, init, r, w, partial=False):
        pr.next_cost = _cost("vector", out, 2.0)
        pr.op("vector", lambda e: e.tensor_tensor_scan(out=out, data0=d0, data1=d1, initial=init, op0=ALU.mult, op1=ALU.add), r, w, partial)

    def recip(out, in_, r, w, partial=False):
        pr.op("vector", lambda e: e.reciprocal(out=out, in_=in_), r, w, partial)

    def round_to_int(out, in_, tmp, r, w):
        ts("vector", tmp, in_, MAGIC, None, ALU.add, None, r, w)
        ts("vector", out, tmp, MAGIC, None, ALU.subtract, None, r, w)

    pA = [ps("pA%d" % i, (P, 512)) for i in range(4)]
    pS = [ps("pS%d" % i, (P, 512)) for i in range(2)]
    pT = [ps("pT%d" % i, (P, 1024), BF16) for i in range(1)]
    pD = ps("pD", (P, 512))
    bA = [B("pA%d" % i, True) for i in range(4)]
    bS = [B("pS%d" % i, True) for i in range(2)]
    bT = [B("pT%d" % i, True) for i in range(1)]
    bD = B("pD", True)
    ringA = Ring(list(zip(pA, bA)))
    ringS = Ring(list(zip(pS, bS)))
    ringT = Ring(list(zip(pT, bT)))
    ringD = Ring([(pD, bD)])
    ringAll = Ring(list(zip(pA + pS + [pD], bA + bS + [bD])))
    ringA01 = Ring([(pA[0], bA[0]), (pA[1], bA[1])])
    ringA23 = Ring([(pS[0], bS[0]), (pS[1], bS[1]), (pA[2], bA[2]), (pA[3], bA[3])])

    identF = sb("identF", (P, P)); identB = sb("identB", (P, P), BF16)
    onesF = sb("onesF", (P, P)); onesB = sb("onesB", (P, P), BF16)
    cols = sb("cols", (P, P))
    pmod = sb("pmod", (P, 32))
    modT = sb("modT", (P, 48, NB))
    hn2Ts = sb("hn2Ts", (P, KD, NS), BF16)
    barr_t = sb("barr_t", (P, 1))
    b_const = B("const"); b_cols = B("cols"); b_pmod = B("pmod"); b_modT = B("modT"); b_hn2Ts = B("hn2Ts")
    b_x1s_scr = B("x1s_scr")

    mset("gpsimd", identF[:], 1.0, [b_const])
    pr.op("gpsimd", lambda e: e.affine_select(out=identF[:], in_=identF[:], pattern=[[-1, P]], compare_op=ALU.is_equal,
                                              fill=0.0, base=0, channel_multiplier=1), [b_const], [b_const])
    cp("gpsimd", identB[:], identF[:], [b_const], [b_const])
    mset("gpsimd", onesF[:], 1.0, [b_const])
    mset("gpsimd", onesB[:], 1.0, [b_const])
    dbg_stop('consts')

    def bcast_rows(dst, bdst, src_cols, bsrc, nrows, diag, b_diag, first=True, ring=None):
        for h in range(2):
            for i in range(4):
                ts("vector", diag[:, i * P:(i + 1) * P], identF[:], src_cols[:, 4 * h + i:4 * h + i + 1], None, ALU.mult, None,
                   [b_const, bsrc], [b_diag], i > 0)
            pM, bM = (ring or ringA).next()
            mm(pM[0:nrows, :], onesF[:, 0:nrows], diag[:], True, True, [b_const, b_diag], [bM])
            cp("scalar", dst[0:nrows, h * 512:(h + 1) * 512], pM[0:nrows, :], [bM], [bdst], (h > 0) or not first)

    s1 = contextlib.ExitStack()
    sb1 = lambda name, shape, dt=F32: s1.enter_context(nc.sbuf_tensor(name, list(shape), dt))
    win_sb = sb1("win_sb", (P, KD, 1536), BF16); b_win = B("win")
    wa_blk = sb1("wa_blk", (P, 4, P), BF16); wx_blk = sb1("wx_blk", (P, 4, P), BF16); b_wab = B("wa"); b_wxb = B("wx")
    wglu_sb = sb1("wglu_sb", (P, 4, 2 * DL), BF16); b_wglu = B("wglu")
    wout_sb = sb1("wout_sb", (P, KD, D), BF16); b_wout = B("wout")
    lruc = sb1("lruc", (P, 16)); b_lruc = B("lruc")
    s5t = sb1("s5t", (P, 16, 16)); b_s5t = B("s5t")
    W1 = sb1("W1", (P, 2, 4, L, P), BF16); b_W1 = B("W1")
    CAb = sb1("CAb", (P, 2, 16, L, 32), BF16); b_CAb = B("CAb")
    Kblk = sb1("Kblk", (P, 4, L, P), BF16); b_Kblk = B("Kblk")
    CT0 = sb1("CT0", (P, 2, 16, 32), BF16); b_CT0 = B("CT0")
    rot = sb1("rot", (P, 2, 16, NK)); b_rot = B("rot")
    rho_tab = sb1("rho_tab", (P, 16, NK)); b_rho = B("rho_tab")
    Dblk = sb1("Dblk", (P, 4, P), BF16); b_Dblk = B("Dblk")

    st_ = contextlib.ExitStack()
    sbt = lambda name, shape, dt=F32: st_.enter_context(nc.sbuf_tensor(name, list(shape), dt))
    jv = sbt("jv", (P, 64)); b_jv = B("jv")
    for j in range(64):
        mset("gpsimd", jv[:, j:j + 1], float(j), [b_jv], j > 0)
    G1bc = sbt("G1bc", (P, D)); b_G1 = B("G1bc")
    rowpack_sb = sbt("rowpack_sb", (P, P)); b_rowpack = B("rowpack")
    pr.dma("sync", rowpack_sb[:], rowpack_d[:, :], writes=[b_rowpack])
    s5pack_sb = sbt("s5pack_sb", (48, P)); b_s5pack = B("s5pack")
    pr.dma("sync", s5pack_sb[:], s5pack_d[:, :], writes=[b_s5pack])
    call_sb = sbt("call_sb", (NB, D)); b_call = B("call")
    pr.dma("sync", call_sb[:], call_d[:, :], writes=[b_call])

    Bsrc = sbt("Bsrc", (P, 2, 16, 32)); b_Bsrc = B("Bsrc")
    mset("vector", Bsrc[:], 0.0, [b_Bsrc])
    for ri, bd in enumerate((bre_d, bim_d)):
        bv = bd.rearrange("(p two) n c -> two n p c", two=2)
        pr.dma("sync", Bsrc[0:64, ri, :, 0:16], bv[0], writes=[b_Bsrc], partial=True)
        pr.dma("sync", Bsrc[64:128, ri, :, 16:32], bv[1], writes=[b_Bsrc], partial=True)
    Csrc = sbt("Csrc", (P, 2, 4, P)); b_Csrc = B("Csrc")
    mset("vector", Csrc[:], 0.0, [b_Csrc])
    for ri, cd in enumerate((cre_d, cim_d)):
        cv = cd.rearrange("(q pl two) c n -> pl two c q n", pl=4, two=2)
        for pl in range(4):
            for g2 in range(2):
                p0 = pl * 32 + g2 * 16
                pr.dma("sync", Csrc[p0:p0 + 16, ri, :, g2 * 64:(g2 + 1) * 64], cv[pl, g2], writes=[b_Csrc], partial=True)
    NAB = 5
    adaw_bufs = [sbt("adaw%d" % i, (P, KD, 512), BF16) for i in range(NAB)]
    b_adaw = [B("adaw%d" % i) for i in range(NAB)]
    adaw_v = adaw_d.rearrange("(k p) n -> p k n", p=P)

    def load_adaw(j):
        pr.dma("gpsimd", adaw_bufs[j % NAB][:], adaw_v[:, :, j * 512:(j + 1) * 512], writes=[b_adaw[j % NAB]])

    for j in range(NAB):
        load_adaw(j)

    pS_, bS_ = ringS.next()
    tr(pS_[:, 0:P], rowpack_sb[:], identF[:], [b_rowpack, b_const], [bS_])
    cp("vector", cols[:], pS_[:, 0:P], [bS_], [b_cols])

    csil = sbt("csil", (NB, D)); b_csil = B("csil")
    act(csil[:], call_sb[:], AF.Silu, [b_call], [b_csil])
    cT = sbt("cT", (P, KD, NB), BF16); b_cT = B("cT")
    pS_, bS_ = ringS.next()
    for k in range(KD):
        tr(pS_[:, k * NB:(k + 1) * NB], csil[:, k * P:(k + 1) * P], identF[0:NB, 0:NB], [b_csil, b_const], [bS_])
    cp("vector", cT[:], pS_[:, 0:KD * NB].rearrange("p (k n) -> p k n", n=NB), [bS_], [b_cT])
    dbg_stop('loads')

    win_v = win_d.rearrange("(k p) n -> p k n", p=P)
    for k in range(KD):
        pr.dma("gpsimd", win_sb[:, k, :], win_v[:, k, :], writes=[b_win], partial=True)
    mset("vector", wa_blk[:], 0.0, [b_wab]); mset("vector", wx_blk[:], 0.0, [b_wxb])
    for (blk, src, bb) in ((wa_blk, wa_d, b_wab), (wx_blk, wx_d, b_wxb)):
        sv = src.rearrange("(t two) i j -> two i t j", two=2)
        pr.dma("gpsimd", blk[0:64, :, 0:64], sv[0], writes=[bb], partial=True)
        pr.dma("gpsimd", blk[64:128, :, 64:128], sv[1], writes=[bb], partial=True)
    dbg_stop('weights')

    def gen_modT():
        pMa, bMa = pA[2], bA[2]
        pMb, bMb = pA[3], bA[3]
        for j in range(12):
            buf = adaw_bufs[j % NAB]; bb = b_adaw[j % NAB]
            pM, bM = (pMa, bMa) if j < 6 else (pMb, bMb)
            for i in range(4):
                c = (j % 6) * 4 + i
                for k in range(KD):
                    mm(pM[:, c * NB:(c + 1) * NB], buf[:, k, i * P:(i + 1) * P], cT[:, k, :], k == 0, k == KD - 1, [b_cT, bb], [bM])
            if j + NAB < 12:
                load_adaw(j + NAB)
            yield
            if j == 5:
                tt("vector", modT[:, 0:24, :], pMa[:, 0:24 * NB].rearrange("p (c n) -> p c n", n=NB),
                   cols[:, C_ADB:C_ADB + 24].unsqueeze(2).to_broadcast([P, 24, NB]), ALU.add, [bMa, b_cols], [b_modT])
                stt(pmod[:, 0:8], modT[:, M_SC1:M_SC1 + 8, NS], 1.0, cols[:, C_N1G:C_N1G + 8], ALU.add, ALU.mult, [b_modT, b_cols], [b_pmod])
                cp("vector", pmod[:, 8:16], modT[:, M_SH1:M_SH1 + 8, NS], [b_modT], [b_pmod], True)
                diag = sbt("diag", (P, 512)); b_diag = B("diag")
                bcast_rows(G1bc, b_G1, modT[:, M_G1:M_G1 + 8, NS], b_modT, P, diag, b_diag, ring=Ring([(pA[2], bA[2])]))
                yield
                for k in range(KD):
                    gc = (C_GLO + k) if k < 4 else (C_GSO + k - 4)
                    ts("vector", wout_sb[:, k, :], wout_sb[:, k, :], cols[:, gc:gc + 1], None, ALU.mult, None, [b_wout, b_cols], [b_wout])
                    if k % 4 == 3:
                        yield
                for k in range(KD):
                    tt("vector" if k % 2 else "gpsimd", wout_sb[:, k, :], wout_sb[:, k, :], G1bc[:], ALU.mult, [b_wout, b_G1], [b_wout])
                    if k % 2 == 1:
                        yield
        b_modT2 = B("modT2")
        tt("vector", modT[:, 24:48, :], pMb[:, 0:24 * NB].rearrange("p (c n) -> p c n", n=NB),
           cols[:, C_ADB + 24:C_ADB + 48].unsqueeze(2).to_broadcast([P, 24, NB]), ALU.add, [bMb, b_cols], [b_modT], True)
        stt(pmod[:, 16:24], modT[:, M_SC2:M_SC2 + 8, NS], 1.0, cols[:, C_N2G:C_N2G + 8], ALU.add, ALU.mult, [b_modT, b_cols], [b_pmod], True)
        cp("vector", pmod[:, 24:32], modT[:, M_SH2:M_SH2 + 8, NS], [b_modT], [b_pmod], True)
        yield

    ringM = Ring([(pA[2], bA[2]), (pA[3], bA[3])])
    ringA2 = Ring([(pA[0], bA[0]), (pA[1], bA[1])])
    gm = gen_modT()

    def pump(n=1):
        for _ in range(n):
            next(gm, None)

    wglu_v = wglu_d.rearrange("(k p) n -> p k n", p=P)
    wout_v = wout_d.rearrange("(k p) n -> p k n", p=P)
    for k in range(KD):
        pr.dma("gpsimd", wout_sb[:, k, :], wout_v[:, k, :], writes=[b_wout], partial=True)

    act(lruc[:, 0:4], cols[:, C_LAM:C_LAM + 4], AF.Exp, [b_cols], [b_lruc], scale=-1.0)
    Rl_ = [b_lruc]
    ts("vector", lruc[:, 4:8], lruc[:, 0:4], -0.25, 1.0 / 3.0, ALU.mult, ALU.add, Rl_, Rl_)
    tt("vector", lruc[:, 4:8], lruc[:, 4:8], lruc[:, 0:4], ALU.mult, Rl_, Rl_)
    ts("vector", lruc[:, 4:8], lruc[:, 4:8], -1.0, 0.5, ALU.mult, ALU.add, Rl_, Rl_)
    tt("vector", lruc[:, 4:8], lruc[:, 4:8], lruc[:, 0:4], ALU.mult, Rl_, Rl_)
    ts("vector", lruc[:, 4:8], lruc[:, 4:8], -1.0, 1.0, ALU.mult, ALU.add, Rl_, Rl_)
    tt("vector", lruc[:, 4:8], lruc[:, 4:8], lruc[:, 0:4], ALU.mult, Rl_, Rl_)
    ts("vector", lruc[:, 8:12], lruc[:, 4:8], -8.0, None, ALU.mult, None, Rl_, Rl_)
    ts("vector", lruc[:, 12:16], lruc[:, 4:8], -16.0, None, ALU.mult, None, Rl_, Rl_)
    cl = lruc[:, 8:12]; cl2 = lruc[:, 12:16]
    dbg_stop('lruc')

    s5c = sbt("s5c", (P, 48)); b_s5c = B("s5c")
    pS_, bS_ = ringS.next()
    tr(pS_[:, 0:48], s5pack_sb[:], identF[0:48, 0:48], [b_s5pack, b_const], [bS_])
    cp("vector", s5c[:], pS_[:, 0:48], [bS_], [b_s5c])
    lr = s5c[:, 0:16]; li = s5c[:, 16:32]; ldt = s5c[:, 32:48]
    V = lambda i: s5t[:, i, :]
    I_DT, I_LRDT, I_TT, I_TR, I_TMP, I_TMP2, I_DEN, I_M1, I_FR, I_FI, I_ABR, I_ABI, I_F8, I_RHO, I_CN, I_SN = range(16)
    R = [b_s5t, b_s5c]; W = [b_s5t]
    act(V(I_DT), ldt, AF.Exp, R, W)
    tt("vector", V(I_LRDT), lr, V(I_DT), ALU.mult, R, W)
    tt("vector", V(I_TT), li, V(I_DT), ALU.mult, R, W)
    ts("vector", V(I_TT), V(I_TT), 1.0 / TWO_PI, None, ALU.mult, None, R, W)
    round_to_int(V(I_TMP2), V(I_TT), V(I_TMP), R, W)
    tt("vector", V(I_TR), V(I_TT), V(I_TMP2), ALU.subtract, R, W)
    pump(2)
    NJ = L + 1
    pw = sbt("pw", (P, 6, 16, NJ)); b_pw = B("pw")

    def bc3(ap2, n):
        return ap2.unsqueeze(2).to_broadcast([P, 16, n])

    def jb3(n):
        return jv[:, 0:n].unsqueeze(1).to_broadcast([P, 16, n])

    def sincos(dst_sin, dst_cos, ang, tmp, tmp2, Rr, Ww):
        round_to_int(tmp2, ang, tmp, Rr, Ww)
        tt("vector", tmp2, ang, tmp2, ALU.subtract, Rr, Ww)
        act(dst_sin, tmp2, AF.Sin, Rr, Ww, scale=TWO_PI)
        ts("vector", tmp, ang, 0.25, None, ALU.add, None, Rr, Ww)
        round_to_int(tmp2, tmp, dst_cos, Rr, Ww)
        tt("vector", tmp2, tmp, tmp2, ALU.subtract, Rr, Ww)
        act(dst_cos, tmp2, AF.Sin, Rr, Ww, scale=TWO_PI)

    Rp = [b_pw, b_s5t, b_jv]; Wp = [b_pw]
    tt("vector", pw[:, 0], bc3(V(I_LRDT), NJ), jb3(NJ), ALU.mult, Rp, Wp)
    act(pw[:, 0], pw[:, 0], AF.Exp, Rp, Wp)
    tt("vector", pw[:, 1], bc3(V(I_TR), NJ), jb3(NJ), ALU.mult, Rp, Wp)
    sincos(pw[:, 5], pw[:, 4], pw[:, 1], pw[:, 2], pw[:, 3], Rp, Wp)
    tt("vector", pw[:, 4], pw[:, 4], pw[:, 0], ALU.mult, Rp, Wp)
    tt("vector", pw[:, 5], pw[:, 5], pw[:, 0], ALU.mult, Rp, Wp)
    PwRe = pw[:, 4]; PwIm = pw[:, 5]
    Rq = [b_pw, b_s5t, b_s5c]
    cp("vector", V(I_ABR), PwRe[:, :, 1], Rq, W)
    cp("vector", V(I_ABI), PwIm[:, :, 1], Rq, W)
    pump(1)
    tt("vector", V(I_DEN), lr, lr, ALU.mult, Rq, W)
    tt("vector", V(I_TMP), li, li, ALU.mult, Rq, W)
    tt("vector", V(I_DEN), V(I_DEN), V(I_TMP), ALU.add, Rq, W)
    recip(V(I_DEN), V(I_DEN), Rq, W)
    ts("vector", V(I_M1), V(I_ABR), -1.0, None, ALU.add, None, Rq, W)
    tt("vector", V(I_FR), V(I_M1), lr, ALU.mult, Rq, W)
    tt("vector", V(I_TMP), V(I_ABI), li, ALU.mult, Rq, W)
    tt("vector", V(I_FR), V(I_FR), V(I_TMP), ALU.add, Rq, W)
    tt("vector", V(I_FR), V(I_FR), V(I_DEN), ALU.mult, Rq, W)
    tt("vector", V(I_FI), V(I_ABI), lr, ALU.mult, Rq, W)
    tt("vector", V(I_TMP), V(I_M1), li, ALU.mult, Rq, W)
    tt("vector", V(I_FI), V(I_FI), V(I_TMP), ALU.subtract, Rq, W)
    tt("vector", V(I_FI), V(I_FI), V(I_DEN), ALU.mult, Rq, W)
    yy = sbt("yy", (P, 3, 16, L)); b_yy = B("yy")
    Ry = [b_yy, b_pw, b_s5t]; Wy = [b_yy]
    tt("vector", yy[:, 0], PwRe[:, :, 0:L], bc3(V(I_FR), L), ALU.mult, Ry, Wy)
    tt("vector", yy[:, 2], PwIm[:, :, 0:L], bc3(V(I_FI), L), ALU.mult, Ry, Wy)
    tt("vector", yy[:, 0], yy[:, 0], yy[:, 2], ALU.subtract, Ry, Wy)
    tt("vector", yy[:, 1], PwIm[:, :, 0:L], bc3(V(I_FR), L), ALU.mult, Ry, Wy)
    tt("vector", yy[:, 2], PwRe[:, :, 0:L], bc3(V(I_FI), L), ALU.mult, Ry, Wy)
    tt("vector", yy[:, 1], yy[:, 1], yy[:, 2], ALU.add, Ry, Wy)
    Yre = yy[:, 0]; Yim = yy[:, 1]
    dbg_stop('s5a')

    pump(1)
    CT = sbt("CT", (P, 2, 16, 32)); b_CT = B("CT")
    for ri in range(2):
        pS_, bS_ = ringS.next()
        for q in range(4):
            tr(pS_[:, q * P:(q + 1) * P], Csrc[:, ri, q, :], identF[:], [b_Csrc, b_const], [bS_])
        cp("vector", CT[:, ri].rearrange("p a b -> p (a b)"), pS_[:, :], [bS_], [b_CT], ri > 0)
    cp("vector", CT0[:, 0], CT[:, 0], [b_CT], [b_CT0])
    ts("vector", CT0[:, 1], CT[:, 1], -1.0, None, ALU.mult, None, [b_CT], [b_CT0], True)
    dbg_stop('s5b')

    Xb = [sbt("Xb%d" % i, (P, 2, 4, L, 32), BF16) for i in range(2)]; b_Xb = [B("Xb%d" % i) for i in range(2)]
    xtmp = sbt("xtmp", (P, 2, 4, L, 16)); b_xtmp = [B("xtmp_h0"), B("xtmp_h1")]
    CTn = sbt("CTn", (P, 16, 32)); b_CTn = B("CTn")
    ts("vector", CTn[:], CT[:, 1], -1.0, None, ALU.mult, None, [b_CT], [b_CTn])
    mset("vector", Kblk[:], 0.0, [b_Kblk])
    mset("vector", CAb[:], 0.0, [b_CAb])
    mset("vector", Xb[0][:], 0.0, [b_Xb[0]])
    mset("vector", Xb[1][:], 0.0, [b_Xb[1]])

    def half_prod(eng, hb, out_ap, a1, b1, a2, b2, op, r, w, p0):
        ps_ = slice(64 * hb, 64 * hb + 64); cs_ = slice(16 * hb, 16 * hb + 16)
        ya = lambda t: t[ps_, p0:p0 + 4, :].unsqueeze(3).to_broadcast([64, 4, L, 16])
        bb_ = lambda t: t[ps_, p0:p0 + 4, cs_].unsqueeze(2).to_broadcast([64, 4, L, 16])
        bt = b_xtmp[hb]
        tt(eng, xtmp[ps_, 0], ya(a1), bb_(b1), ALU.mult, r, [bt])
        tt(eng, xtmp[ps_, 1], ya(a2), bb_(b2), ALU.mult, r + [bt], [bt])
        tt(eng, out_ap[ps_, :, :, cs_], xtmp[ps_, 0], xtmp[ps_, 1], op, [bt], w, True)

    for q in range(4):
        p0 = q * 4
        xb = Xb[q % 2]; bxb = b_Xb[q % 2]
        PwRe1 = PwRe[:, :, 1:L + 1]; PwIm1 = PwIm[:, :, 1:L + 1]
        RX = [b_yy, b_Bsrc]; RCA = [b_pw, b_CT, b_CTn]
        for hb in range(2):
            half_prod("vector", hb, xb[:, 0], Yre, Bsrc[:, 0], Yim, Bsrc[:, 1], ALU.subtract, RX, [bxb], p0)
            half_prod("vector", hb, xb[:, 1], Yre, Bsrc[:, 1], Yim, Bsrc[:, 0], ALU.add, RX, [bxb], p0)
        half_prod("vector", 0, CAb[:, 0, p0:p0 + 4], PwRe1, CT[:, 0], PwIm1, CT[:, 1], ALU.subtract, RCA, [b_CAb], p0)
        pump(1)
        half_prod("gpsimd", 1, CAb[:, 0, p0:p0 + 4], PwRe1, CT[:, 0], PwIm1, CT[:, 1], ALU.subtract, RCA, [b_CAb], p0)
        pump(1)
        half_prod("gpsimd", 0, CAb[:, 1, p0:p0 + 4], PwRe1, CTn, PwIm1, CT[:, 0], ALU.subtract, RCA, [b_CAb], p0)
        pump(1)
        half_prod("gpsimd", 1, CAb[:, 1, p0:p0 + 4], PwRe1, CTn, PwIm1, CT[:, 0], ALU.subtract, RCA, [b_CAb], p0)
        pump(1)
        for ri in range(2):
            for jq in range(L // 4):
                pM, bM = ringA2.next()
                for pl in range(4):
                    for jj in range(4):
                        j = jq * 4 + jj
                        mm(pM[pl * 32:(pl + 1) * 32, jj * P:(jj + 1) * P], xb[:, ri, pl, L - 1 - j, :], identB[:],
                           True, True, [bxb, b_const], [bM], tp=(0, pl * 32))
                cp("scalar", W1[:, ri, q, jq * 4:(jq + 1) * 4, :].rearrange("p a b -> p (a b)"), pM[:, :], [bM], [b_W1], True)
        pump(1)
        for tq in range(L // 4):
            pM, bM = ringA2.next()
            for pl in range(4):
                p = q * 4 + pl
                for t4 in range(4):
                    tau = tq * 4 + t4
                    o = pM[pl * 32:(pl + 1) * 32, t4 * P + pl * 32: t4 * P + pl * 32 + 32]
                    mm(o, xb[:, 0, pl, tau, :], CT0[:, 0, p, :], True, False, [bxb, b_CT0], [bM], tp=(0, pl * 32))
                    mm(o, xb[:, 1, pl, tau, :], CT0[:, 1, p, :], False, True, [bxb, b_CT0], [bM], tp=(0, pl * 32))
            for pl in range(4):
                src = pM[pl * 32:(pl + 1) * 32, :].rearrange("p (t c) -> p t c", c=P)[:, :, pl * 32:pl * 32 + 32]
                cp("scalar" if pl % 2 else "vector", Kblk[pl * 32:(pl + 1) * 32, q, tq * 4:(tq + 1) * 4, pl * 32:pl * 32 + 32], src,
                   [bM], [b_Kblk], True)

    for q in range(4):
        ts("vector", Dblk[:, q, :], identF[:], cols[:, C_SD + q:C_SD + q + 1], None, ALU.mult, None, [b_const, b_cols], [b_Dblk], q > 0)
    for q in range(4):
        tt("vector", Kblk[:, q, 0, :], Kblk[:, q, 0, :], Dblk[:, q, :], ALU.add, [b_Kblk, b_Dblk], [b_Kblk])
    dbg_stop('s5c')
    lv2 = sbt("lv2", (P, 3, 16, NK)); b_lv2 = B("lv2")
    ts("vector", V(I_F8), V(I_TR), float(L), None, ALU.mult, None, Rq, W)
    round_to_int(V(I_TMP2), V(I_F8), V(I_TMP), Rq, W)
    tt("vector", V(I_F8), V(I_F8), V(I_TMP2), ALU.subtract, Rq, W)
    Rl = [b_lv2, b_s5t, b_jv, b_rot]; Wl = [b_lv2, b_rot]
    tt("vector", lv2[:, 0], bc3(V(I_F8), NK), jb3(NK), ALU.mult, Rl, Wl)
    sincos(rot[:, 1], rot[:, 0], lv2[:, 0], lv2[:, 1], lv2[:, 2], Rl, Wl)
    Rc = rot[:, 0]; Rs = rot[:, 1]
    s5u = sbt("s5u", (P, 3, 16)); b_s5u = B("s5u")
    ts("vector", s5u[:, 0], V(I_F8), float(NK), None, ALU.mult, None, [b_s5t], [b_s5u])
    sincos(V(I_SN), V(I_CN), s5u[:, 0], s5u[:, 1], s5u[:, 2], [b_s5u, b_s5t], [b_s5t, b_s5u])
    act(V(I_RHO), V(I_LRDT), AF.Exp, Rq, W, scale=float(L))
    cp("vector", rho_tab[:], bc3(V(I_RHO), NK), [b_s5t], [b_rho])
    dbg_stop('s5d')
    ts("vector", rho_tab[:, :, 0:1], rho_tab[:, :, 0:1], 0.0, None, ALU.mult, None, [b_rho], [b_rho])

    pump(40)
    for k in range(4):
        pr.dma("gpsimd", wglu_sb[:, k, :], wglu_v[:, k, :], writes=[b_wglu], partial=True)
    dbg_stop('setup')
    pr.barrier(lambda e: e.memset(barr_t[:], 0.0), exempt=[b_wglu])
    st_.close()

    def rms_stats(x_ap, nrows, st, bst, bx, junk_ap, bjunk):
        act(junk_ap, x_ap, AF.Square, [bx], [bjunk, bst], accum=st[0:nrows, 0:1])
        act(st[0:nrows, 1:2], st[0:nrows, 0:1], AF.Ln, [bst], [bst], scale=1.0 / D, bias=EPS)
        act(st[0:nrows, 1:2], st[0:nrows, 1:2], AF.Exp, [bst], [bst], scale=-0.5)

    def run(gen):
        for _ in gen:
            pass

    def proj_in(cx, hT, bh, n, lrux_t, blx, col0, ring=None):
        per = min(12, 512 // n)
        ot = 0
        while ot < 12:
            pM, bM = (ring or ringA).next()
            cnt = min(per, 12 - ot)
            for i in range(cnt):
                o = ot + i
                for k in range(KD):
                    mm(pM[:, i * n:(i + 1) * n], win_sb[:, k, o * P:(o + 1) * P], hT[:, k, 0:n], k == 0, k == KD - 1, [b_win, bh], [bM])
            for i in range(cnt):
                o = ot + i
                src = pM[:, i * n:(i + 1) * n]
                if o < 4:
                    cp("scalar", lrux_t[:, o, col0:col0 + n], src, [bM], [blx], True)
                elif o < 8:
                    act(cx.gg[:, o - 4, 0:n], src, AF.Gelu_apprx_tanh, [bM], [cx.b_gg], partial=True)
                else:
                    cp("scalar", cx.ub[:, o - 8, 0:n], src, [bM], [cx.b_ub], True)
            ot += cnt
            yield

    def lru_gates(cx, n, ring=None):
        conv, aa, bb_, convb = cx.conv, cx.aa, cx.bb, cx.convb
        b_conv, b_aa, b_bb, b_convb = cx.b_conv, cx.b_aa, cx.b_bb, cx.b_convb
        cp("scalar", convb[:, :, 0:n], conv[:, :, 0:n], [b_conv], [b_convb])
        yield
        rg = ring or ringA
        groups = [(0, 1), (2, 3)] if n > 128 else [(0, 1, 2, 3)]
        for grp in groups:
            dr, dbr = rg.next()
            di, dbi = rg.next()
            for i, t in enumerate(grp):
                o = i * n
                mm(dr[:, o:o + n], wa_blk[:, t, :], convb[:, t, 0:n], True, True, [b_wab, b_convb], [dbr])
                mm(di[:, o:o + n], wx_blk[:, t, :], convb[:, t, 0:n], True, True, [b_wxb, b_convb], [dbi])
            for i, t in enumerate(grp):
                o = i * n
                act(aa[:, t, 0:n], dr[:, o:o + n], AF.Sigmoid, [dbr, b_cols], [b_aa], bias=cols[:, C_BA + t:C_BA + t + 1], partial=True)
                act(bb_[:, t, 0:n], di[:, o:o + n], AF.Sigmoid, [dbi, b_cols], [b_bb], bias=cols[:, C_BX + t:C_BX + t + 1], partial=True)
            yield
        tt("gpsimd", bb_[:, :, 0:n], bb_[:, :, 0:n], conv[:, :, 0:n], ALU.mult, [b_bb, b_conv], [b_bb])
        for t in range(4):
            act(conv[:, t, 0:n], aa[:, t, 0:n], AF.Exp, [b_aa, b_lruc], [b_conv], scale=cl2[:, t:t + 1], partial=t > 0)
        yield
        for t in range(4):
            act(aa[:, t, 0:n], aa[:, t, 0:n], AF.Exp, [b_aa, b_lruc, b_conv], [b_aa], scale=cl[:, t:t + 1], partial=t > 0)
        yield
        ts("gpsimd", conv[:, :, 0:n], conv[:, :, 0:n], -1.0, 1.0, ALU.mult, ALU.add, [b_conv], [b_conv])
        act(conv[:, :, 0:n], conv[:, :, 0:n], AF.Ln, [b_conv], [b_conv], bias=1e-30)
        act(conv[:, :, 0:n], conv[:, :, 0:n], AF.Exp, [b_conv], [b_conv], scale=0.5)
        yield
        tt("gpsimd", bb_[:, :, 0:n], bb_[:, :, 0:n], conv[:, :, 0:n], ALU.mult, [b_bb, b_conv], [b_bb])
        yield

    def s5_glu_and_merge(cx, n, mg, bmg, ring=None):
        gy, sg, s5o, sq, rstd, gg = cx.gy, cx.sg, cx.s5o, cx.sq, cx.rstd, cx.gg
        b_gy, b_sg, b_s5o, b_sq, b_rstd, b_gg = cx.b_gy, cx.b_sg, cx.b_s5o, cx.b_sq, cx.b_rstd, cx.b_gg
        act(sq[:, 0:4, 0:n], gg[:, :, 0:n], AF.Square, [b_gg], [b_sq])
        yield
        per = max(1, min(4, 512 // n))
        for grp in range(0, 4, per):
            pZa, bZa = (ring or ringA).next()
            pZb, bZb = (ring or ringA).next()
            cnt = min(per, 4 - grp)
            for i in range(cnt):
                t = grp + i
                for k in range(4):
                    mm(pZa[:, i * n:(i + 1) * n], wglu_sb[:, k, t * P:(t + 1) * P], gy[:, k, 0:n], k == 0, k == 3, [b_wglu, b_gy], [bZa])
                for k in range(4):
                    mm(pZb[:, i * n:(i + 1) * n], wglu_sb[:, k, (4 + t) * P:(5 + t) * P], gy[:, k, 0:n], k == 0, k == 3, [b_wglu, b_gy], [bZb])
            for i in range(cnt):
                t = grp + i
                act(sg[:, t, 0:n], pZb[:, i * n:(i + 1) * n], AF.Sigmoid, [bZb], [b_sg], partial=True)
                tt("vector", s5o[:, t, 0:n], pZa[:, i * n:(i + 1) * n], sg[:, t, 0:n], ALU.mult, [bZa, b_sg], [b_s5o], True)
            yield
        tt("gpsimd", sq[:, 4:8, 0:n], s5o[:, :, 0:n], s5o[:, :, 0:n], ALU.mult, [b_s5o], [b_sq], True)
        pQ, bQ = (ring or ringA).next()
        for h in range(2):
            for t in range(4):
                mm(pQ[:, h * n:(h + 1) * n], onesB[:], sq[:, h * 4 + t, 0:n], t == 0, t == 3, [b_const, b_sq], [bQ])
        act(rstd[:, :, 0:n], pQ[:, 0:2 * n].rearrange("p (h n) -> p h n", h=2), AF.Ln, [bQ], [b_rstd], scale=1.0 / DL, bias=EPS)
        act(rstd[:, :, 0:n], rstd[:, :, 0:n], AF.Exp, [b_rstd], [b_rstd], scale=-0.5)
        yield
        tt("gpsimd", mg[:, 0:4, 0:n], gg[:, :, 0:n], rstd[:, 0, 0:n].unsqueeze(1).to_broadcast([P, 4, n]), ALU.mult,
           [b_gg, b_rstd], [bmg], True)
        tt("gpsimd", mg[:, 4:8, 0:n], s5o[:, :, 0:n], rstd[:, 1, 0:n].unsqueeze(1).to_broadcast([P, 4, n]), ALU.mult,
           [b_s5o, b_rstd], [bmg], True)
        yield

    def make_ctx(alloc, n, alias, pfx):
        cx = Ctx()
        def mk(name, shape, dt=F32):
            return alloc(pfx + name, shape, dt), B(pfx + name)
        cx.gg, cx.b_gg = mk("gg", (P, 4, n)); cx.ub, cx.b_ub = mk("ub", (P, 4, n), BF16)
        cx.conv, cx.b_conv = mk("conv", (P, 4, n)); cx.convb, cx.b_convb = mk("convb", (P, 4, n), BF16)
        cx.aa, cx.b_aa = mk("aa", (P, 4, n)); cx.bb, cx.b_bb = mk("bb", (P, 4, n))
        cx.gy, cx.b_gy = mk("gy", (P, 4, n), BF16)
        cx.sg, cx.b_sg = mk("sg", (P, 4, n))
        if alias:
            cx.s5o, cx.b_s5o = cx.bb, cx.b_bb
        else:
            cx.s5o, cx.b_s5o = mk("s5o", (P, 4, n))
        cx.sq, cx.b_sq = mk("sq", (P, 8, n), BF16); cx.rstd, cx.b_rstd = mk("rstd", (P, 2, n))
        return cx

    def clone_ctx_A(cx, alloc, n, pfx):
        c2 = Ctx()
        c2.__dict__.update(cx.__dict__)
        c2.gg = alloc(pfx + "gg", [P, 4, n], F32); c2.b_gg = B(pfx + "gg")
        c2.ub = alloc(pfx + "ub", [P, 4, n], BF16); c2.b_ub = B(pfx + "ub")
        return c2

    sp = contextlib.ExitStack()
    sbp = lambda name, shape, dt=F32: sp.enter_context(nc.sbuf_tensor(name, list(shape), dt))
    NXT = 2
    xt = [sbp("xt%d" % i, (P, D)) for i in range(NXT)]; b_xt = [B("xt%d" % i) for i in range(NXT)]
    ringX = Ring(list(zip(xt, b_xt)))
    stat = [sbp("stat%d" % i, (P, 4)) for i in range(4)]; b_stat = [B("stat%d" % i) for i in range(4)]
    ringStat = Ring(list(zip(stat, b_stat)))
    xn = [sbp("xn%d" % i, (P, D), BF16) for i in range(1)]; b_xn = [B("xn%d" % i) for i in range(1)]
    ringXn = Ring(list(zip(xn, b_xn)))
    hnT = sbp("hnT", (P, KD, CH), BF16); b_hnT = B("hnT")
    lrux = [sbp("lrux%d" % i, (P, 4, CH + 3)) for i in range(2)]; b_lrux = [B("lrux%d" % i) for i in range(2)]
    cx0 = make_ctx(sbp, CH, False, "p_")
    gg3 = [(cx0.gg, cx0.b_gg)] + [(sbp("gg%d" % i, (P, 4, CH)), B("gg%d" % i)) for i in (1, 2)]
    ub2 = [(cx0.ub, cx0.b_ub), (sbp("ub1", (P, 4, CH), BF16), B("ub1"))]

    class _CtxList:
        def __getitem__(self, c):
            cx = Ctx()
            cx.__dict__.update(cx0.__dict__)
            cx.gg, cx.b_gg = gg3[c % 3]
            cx.ub, cx.b_ub = ub2[c % 2]
            return cx
    cxs_p = _CtxList()
    hs = cx0.conv; b_hs = cx0.b_conv
    hcar = sbp("hcar", (P, 4)); b_hcar = B("hcar")
    s5e = sbp("s5e", (P, 4, 16, NK)); b_s5x = [B("s5e%d" % i) for i in range(4)]
    Hprev = sbp("Hprev", (P, 2, 16, NK), BF16); b_Hprev = B("Hprev")
    carry = sbp("carry", (P, 4, 16)); b_carry = B("carry")
    ctmp = sbp("ctmp", (P, 4, 16)); b_ctmp = B("ctmp")
    mrg2 = [sbp("mrg%d" % i, (P, KD, CH), BF16) for i in range(2)]; b_mrg2 = [B("mrg%d" % i) for i in range(2)]
    xres = [sbp("xres%d" % i, (P, 512)) for i in range(2)]; b_xres = [B("xres%d" % i) for i in range(2)]
    ringXres = Ring(list(zip(xres, b_xres)))

    mset("vector", carry[:], 0.0, [b_carry])
    mset("vector", lrux[0][:], 0.0, [b_lrux[0]])
    V_RHO = V(I_RHO); V_CN = V(I_CN); V_SN = V(I_SN)
    fl = lambda a: a.rearrange("p a m -> p (a m)")
    qv = lambda a, pl: a.rearrange("p (q pl) m -> p pl q m", pl=4)[:, pl]
    x1_tiles = {}

    def genA(c):
        cx = cxs_p[c]
        for tt_i in range(CH // P):
            tok0 = c * CH + tt_i * P
            xT, bx = ringX.next()
            pr.dma("sync", xT[:], xp_d[tok0:tok0 + P, :], writes=[bx])
            st, bst = ringStat.next()
            xnT, bxn = ringXn.next()
            rms_stats(xT[:], P, st, bst, bx, xnT[:], bxn)
            act(xnT[:], xT[:], AF.Identity, [bx, bst], [bxn], scale=st[:, 1:2])
            yield
            pTr, bTr = ringT.next()
            for k in range(KD):
                tr(pTr[:, k * P:(k + 1) * P], xnT[:, k * P:(k + 1) * P], identB[:], [bxn, b_const], [bTr])
            yield
            for k in range(KD):
                act(hnT[:, k, tt_i * P:(tt_i + 1) * P], pTr[:, k * P:(k + 1) * P], AF.Identity, [bTr, b_pmod], [b_hnT],
                    scale=pmod[:, k:k + 1], bias=pmod[:, 8 + k:9 + k], partial=True)
                if k % 4 == 3:
                    yield
        yield from proj_in(cx, hnT, b_hnT, CH, lrux[c % 2], b_lrux[c % 2], 3, ring=ringA01)

    def genB(c):
        cx = cxs_p[c]
        lx = lrux[c % 2]; blx = b_lrux[c % 2]
        lxn = lrux[(c + 1) % 2]; blxn = b_lrux[(c + 1) % 2]
        cp("gpsimd", lxn[:, :, 0:3], lx[:, :, CH:CH + 3], [blx], [blxn], True)
        conv = cx.conv; b_conv = cx.b_conv
        for t in range(4):
            ts("vector", conv[:, t, :], lx[:, t, 0:CH], cols[:, C_CW + t:C_CW + t + 1], cols[:, C_CB + t:C_CB + t + 1], ALU.mult, ALU.add,
               [blx, b_cols], [b_conv], t > 0)
            for k in range(1, 4):
                stt(conv[:, t, :], lx[:, t, k:k + CH], cols[:, C_CW + 4 * k + t:C_CW + 4 * k + t + 1], conv[:, t, :], ALU.mult, ALU.add,
                    [blx, b_cols, b_conv], [b_conv], True)
            yield
        yield from lru_gates(cx, CH, ring=ringA23)
        for t in range(4):
            init = 0.0 if c == 0 else hcar[:, t:t + 1]
            scan(hs[:, t, :], cx.aa[:, t, :], cx.bb[:, t, :], init, [cx.b_aa, cx.b_bb, b_hcar], [b_hs], t > 0)
            if t % 2 == 1:
                yield
        cp("vector", hcar[:], hs[:, :, CH - 1], [b_hs], [b_hcar])
        tt("gpsimd", cx.gg[:], cx.gg[:], hs[:], ALU.mult, [cx.b_gg, b_hs], [cx.b_gg])
        yield

    def genC(c):
        cx = cxs_p[c]
        ub = cx.ub; b_ub = cx.b_ub
        Epr = s5e[:, 0]; Epi = s5e[:, 1]; t1 = s5e[:, 2]; t2 = s5e[:, 3]
        bEpr, bEpi, bT1, bT2 = b_s5x
        for ri in range(2):
            for plh in range(2):
                banks = {2 * plh: ringA23.next(), 2 * plh + 1: ringA23.next()}
                for q in range(4):
                    for pl in (2 * plh, 2 * plh + 1):
                        pE, bE = banks[pl]
                        uv = ub[pl * 32:(pl + 1) * 32, q, :].rearrange("p (m j) -> p j m", j=L)
                        for j in range(L):
                            mm(pE[:, q * NK:(q + 1) * NK], W1[pl * 32:(pl + 1) * 32, ri, q, j, :], uv[:, j, :], j == 0, j == L - 1,
                               [b_W1, b_ub], [bE], tp=(pl * 32, 0))
                for pl in (2 * plh, 2 * plh + 1):
                    pE, bE = banks[pl]
                    Ev = pE[:, 0:4 * NK].rearrange("p (q m) -> p q m", m=NK)
                    if ri == 0:
                        tt("vector", qv(Epr, pl), Ev, qv(Rc, pl), ALU.mult, [bE, b_rot], [bEpr], True)
                        tt("vector", qv(t2, pl), Ev, qv(Rs, pl), ALU.mult, [bE, b_rot], [bT2], True)
                    else:
                        tt("vector", qv(t1, pl), Ev, qv(Rs, pl), ALU.mult, [bE, b_rot], [bT1], True)
                        tt("vector", qv(Epi, pl), Ev, qv(Rc, pl), ALU.mult, [bE, b_rot], [bEpi], True)
                yield
        tt("vector", Epr, Epr, t1, ALU.add, [bEpr, bT1], [bEpr])
        tt("gpsimd", Epi, Epi, t2, ALU.subtract, [bEpi, bT2], [bEpi])
        yield
        tt("vector", ctmp[:, 0], carry[:, 2], V_RHO, ALU.mult, [b_carry, b_s5t], [b_ctmp])
        tt("vector", ctmp[:, 1], carry[:, 3], V_RHO, ALU.mult, [b_carry, b_s5t, b_ctmp], [b_ctmp])
        tt("vector", Epr[:, :, 0], Epr[:, :, 0], ctmp[:, 0], ALU.add, [bEpr, b_ctmp], [bEpr])
        tt("vector", Epi[:, :, 0], Epi[:, :, 0], ctmp[:, 1], ALU.add, [bEpi, b_ctmp], [bEpi])
        yield
        scan(fl(t1), fl(rho_tab[:]), fl(Epr), 0.0, [bEpr, b_rho], [bT1])
        scan(fl(t2), fl(rho_tab[:]), fl(Epi), 0.0, [bEpi, b_rho], [bT2])
        yield
        cp("vector", Hprev[:, 0, :, 0], carry[:, 0], [b_carry], [b_Hprev])
        cp("vector", Hprev[:, 1, :, 0], carry[:, 1], [b_carry, b_Hprev], [b_Hprev])
        gl_r = t1[:, :, NK - 1]; gl_i = t2[:, :, NK - 1]
        RC = [bT1, bT2, b_s5t, b_ctmp]
        tt("vector", ctmp[:, 0], gl_r, V_CN, ALU.mult, RC, [b_ctmp])
        tt("vector", ctmp[:, 1], gl_i, V_SN, ALU.mult, RC, [b_ctmp])
        tt("vector", ctmp[:, 2], gl_i, V_CN, ALU.mult, RC, [b_ctmp])
        tt("vector", ctmp[:, 3], gl_r, V_SN, ALU.mult, RC, [b_ctmp])
        yield
        tt("vector", carry[:, 2], ctmp[:, 0], ctmp[:, 1], ALU.subtract, [b_ctmp, b_carry], [b_carry])
        tt("vector", carry[:, 3], ctmp[:, 2], ctmp[:, 3], ALU.add, [b_ctmp, b_carry], [b_carry])
        tt("vector", Epr, t1, Rc, ALU.mult, [bT1, b_rot], [bEpr])
        tt("gpsimd", Epi, t2, Rc, ALU.mult, [bT2, b_rot], [bEpi])
        yield
        tt("vector", t1, t1, Rs, ALU.mult, [bT1, b_rot], [bT1])
        tt("gpsimd", t2, t2, Rs, ALU.mult, [bT2, b_rot], [bT2])
        yield
        tt("vector", Hprev[:, 0, :, 1:NK], Epr[:, :, 0:NK - 1], t2[:, :, 0:NK - 1], ALU.subtract, [bEpr, bT2, b_Hprev], [b_Hprev])
        tt("gpsimd", Hprev[:, 1, :, 1:NK], Epi[:, :, 0:NK - 1], t1[:, :, 0:NK - 1], ALU.add, [bEpi, bT1, b_Hprev], [b_Hprev])
        yield
        tt("vector", carry[:, 0], Epr[:, :, NK - 1], t2[:, :, NK - 1], ALU.subtract, [bEpr, bT2, b_carry], [b_carry])
        tt("vector", carry[:, 1], Epi[:, :, NK - 1], t1[:, :, NK - 1], ALU.add, [bEpi, bT1, b_carry], [b_carry])
        yield
        gy = cx.gy; b_gy = cx.b_gy
        for qq in range(2):
            pY, bY = ringA23.next()
            for qi in range(2):
                q = qq * 2 + qi
                uvq = ub[:, q, :].rearrange("p (m j) -> p j m", j=L)
                for j in range(L):
                    o0 = qi * CH + j * NK
                    for i in range(j + 1):
                        mm(pY[:, o0:o0 + NK], Kblk[:, q, j - i, :], uvq[:, i, :], i == 0, False, [b_Kblk, b_ub], [bY])
                    for pl in range(4):
                        p = q * 4 + pl
                        mm(pY[pl * 32:(pl + 1) * 32, o0:o0 + NK], CAb[:, 0, p, j, :], Hprev[:, 0, p, :], False, False,
                           [b_CAb, b_Hprev], [bY], tp=(0, pl * 32))
                        mm(pY[pl * 32:(pl + 1) * 32, o0:o0 + NK], CAb[:, 1, p, j, :], Hprev[:, 1, p, :], False, True,
                           [b_CAb, b_Hprev], [bY], tp=(0, pl * 32))
            for qi in range(2):
                q = qq * 2 + qi
                src = pY[:, qi * CH:(qi + 1) * CH].rearrange("p (j m) -> p j m", j=L)
                act(gy[:, q, :].rearrange("p (m j) -> p j m", j=L), src, AF.Gelu_apprx_tanh, [bY], [b_gy], partial=q > 0)
            yield

    def genD1(c):
        yield from s5_glu_and_merge(cxs_p[c], CH, mrg2[c % 2], b_mrg2[c % 2], ring=ringA23)

    def genD2(c):
        mrg = mrg2[c % 2]; b_mrg = b_mrg2[c % 2]
        for tt_i in range(CH // P):
            tok0 = c * CH + tt_i * P
            x1_tiles[tok0] = [B("x1s_%d_0" % tok0), B("x1s_%d_1" % tok0)]
            for h in range(2):
                bscr = x1_tiles[tok0][h]
                xr_, bxr = ringXres.next()
                pr.dma("sync", xr_[:], xp_d[tok0:tok0 + P, h * 512:(h + 1) * 512], writes=[bxr])
                pO, bO = ringD.next()
                for k in range(KD):
                    mm(pO[:, :], mrg[:, k, tt_i * P:(tt_i + 1) * P], wout_sb[:, k, h * 512:(h + 1) * 512], k == 0, k == KD - 1,
                       [b_mrg, b_wout], [bO])
                yield
                tt("vector", xr_[:], pO[:, :], xr_[:], ALU.add, [bO, bxr], [bxr])
                pr.dma("sync", x1_d[tok0:tok0 + P, h * 512:(h + 1) * 512], xr_[:], reads=[bxr], writes=[bscr])
                yield

    def rr(named):
        alive = {nm: g for nm, g in named}
        tlast = {nm: 0.0 for nm, _ in named}
        while alive:
            nm = min(alive, key=lambda k_: tlast[k_])
            pr.step_max = 0.0
            try:
                next(alive[nm])
                tlast[nm] = max(tlast[nm], pr.step_max)
            except StopIteration:
                del alive[nm]

    run(genA(0))
    for k in range(NCH + 2):
        streams = []
        if 0 <= k - 1 < NCH:
            streams.append(("D1", genD1(k - 1)))
        if k < NCH:
            streams.append(("C", genC(k)))
            streams.append(("B", genB(k)))
        if 0 <= k - 2 < NCH:
            streams.append(("D2", genD2(k - 2)))
        if k + 1 < NCH:
            streams.append(("A", genA(k + 1)))
        rr(streams)

    outst = xt[0]; b_outst = b_xt[0]
    lastc = NCH - 1
    lxl = lrux[lastc % 2]; blxl = b_lrux[lastc % 2]
    pS_, bS_ = ringS.next()
    for t in range(4):
        tr(pS_[0:3, t * P:(t + 1) * P], lxl[:, t, CH:CH + 3], identF[:], [blxl, b_const], [bS_])
    cp("vector", outst[0:3, 0:512], pS_[0:3, :], [bS_], [b_outst])
    pr.dma("sync", convp_d[:, :], outst[0:3, 0:512], reads=[b_outst])
    pS_, bS_ = ringS.next()
    tr(pS_[0:4, 0:P], hcar[:], identF[:], [b_hcar, b_const], [bS_])
    tr(pS_[0:16, P:2 * P], carry[:, 0], identF[:], [b_carry, b_const], [bS_])
    tr(pS_[0:16, 2 * P:3 * P], carry[:, 1], identF[:], [b_carry, b_const], [bS_])
    b_outst2 = B("outst2")
    cp("vector", outst[0:16, 512:512 + 3 * P], pS_[0:16, 0:3 * P], [bS_], [b_outst2])
    pr.dma("sync", lrup_d[:, :], outst[0:4, 512:512 + P], reads=[b_outst2])
    pr.dma("sync", s5rp_d[:, :], outst[0:16, 512 + P:512 + 2 * P], reads=[b_outst2])
    pr.dma("sync", s5ip_d[:, :], outst[0:16, 512 + 2 * P:512 + 3 * P], reads=[b_outst2])

    dbg_stop('prompt')
    pr.barrier(lambda e: e.memset(barr_t[:], 0.0))
    sp.close()

    ss = contextlib.ExitStack()
    sbs = lambda name, shape, dt=F32: ss.enter_context(nc.sbuf_tensor(name, list(shape), dt))
    n = NS
    cxs = make_ctx(sbs, NS, False, "s_")
    xs_sb = sbs("xs_sb", (NS, D)); b_xs = B("xs")
    pr.dma("sync", xs_sb[:], xs_d[:, :], writes=[b_xs])
    sst = sbs("sst", (NS, 4)); b_sst = B("sst")
    stmp = sbs("stmp", (NS, D)); b_stmp = B("stmp")
    hsb = sbs("hsb", (NS, D), BF16); b_hsb = B("hsb")
    smod = sbs("smod", (P, 3, KD, NS)); b_smod = B("smod")
    hTs = sbs("hTs", (P, KD, NS), BF16); b_hTs = B("hTs")
    g1s = sbs("g1s", (NS, D)); b_g1s = B("g1s")
    x1s = sbs("x1s", (NS, D)); b_x1s = B("x1s")
    outs = sbs("outs", (NS, 1024)); b_outs = B("outs")

    def sample_norm_mod(x_ap, bx, gcol0, sc_off, sh_off, dstT, bdst):
        rms_stats(x_ap, NS, sst, b_sst, bx, hsb[:], b_hsb)
        ts("vector", hsb[:], x_ap, sst[:, 1:2], None, ALU.mult, None, [bx, b_sst], [b_hsb])
        pTr, bTr = ringT.next()
        for k in range(KD):
            tr(pTr[:, k * NS:(k + 1) * NS], hsb[:, k * P:(k + 1) * P], identB[0:NS, 0:NS], [b_hsb, b_const], [bTr])
        stt(smod[:, 0], modT[:, sc_off:sc_off + 8, 0:NS], 1.0, cols[:, gcol0:gcol0 + 8].unsqueeze(2).to_broadcast([P, KD, NS]),
            ALU.add, ALU.mult, [b_modT, b_cols], [b_smod])
        tt("vector", smod[:, 1], pTr[:, 0:KD * NS].rearrange("p (k n) -> p k n", n=NS), smod[:, 0], ALU.mult, [bTr, b_smod], [b_smod])
        tt("vector", dstT[:, :, :], smod[:, 1], modT[:, sh_off:sh_off + 8, 0:NS], ALU.add, [b_smod, b_modT], [bdst])

    def gate_rows(dst, bdst, goff):
        for h in range(2):
            pM, bM = ringA.next()
            for i in range(4):
                k = h * 4 + i
                tr(pM[0:NS, i * P:(i + 1) * P], modT[:, goff + k, 0:NS], identF[:], [b_modT, b_const], [bM])
            cp("vector", dst[:, h * 512:(h + 1) * 512], pM[0:NS, :], [bM], [bdst], h > 0)

    sconv_sb = sbs("sconv_sb", (NS, 3 * DL)); b_sconv = B("sconv")
    pr.dma("sync", sconv_sb[:], sconv_d.rearrange("b k c -> b (k c)"), writes=[b_sconv])
    slru_sb = sbs("slru_sb", (NS, DL)); b_slru = B("slru")
    pr.dma("sync", slru_sb[:], slru_d[:, :], writes=[b_slru])
    s5io = [sbs("s5io%d" % i, (NS, 2048)) for i in range(2)]; b_s5io = [B("s5io%d" % i) for i in range(2)]
    for ri, sd in enumerate((ss5r_d, ss5i_d)):
        pr.dma("sync", s5io[ri][:], sd[:, :], writes=[b_s5io[ri]])
    for k in range(KD):
        pr.dma("gpsimd", wout_sb[:, k, :], wout_v[:, k, :], writes=[b_wout], partial=k > 0)
    pr.dma("sync", convs_d[:, 0:2, :].rearrange("b k c -> b (k c)"), sconv_sb[:, DL:3 * DL], reads=[b_sconv])

    sample_norm_mod(xs_sb[:], b_xs, C_N1G, M_SC1, M_SH1, hTs, b_hTs)
    lxs = sbs("lxs", (P, 4, NS)); blxs = B("lxs")
    run(proj_in(cxs, hTs, b_hTs, NS, lxs, blxs, 0))
    sstT = sbs("sstT", (P, 16, NS)); b_sstT = B("sstT")
    hss = sbs("hss", (P, 4, NS)); bhss = B("hss")
    h0 = sbs("h0", (P, 2, 16, NS)); b_h0 = B("h0")
    hn_ = sbs("hn_", (P, 2, 16, NS)); b_hn = B("hn_")
    hnb = sbs("hnb", (P, 2, 16, NS), BF16); b_hnb = B("hnb")
    htmp = sbs("htmp", (P, 2, 16, NS)); b_htmp = B("htmp")
    ubs = cxs.ub; b_ubs = cxs.b_ub

    def gen_s_lru():
        pS_, bS_ = ringS.next()
        for i in range(12):
            tr(pS_[:, i * NS:(i + 1) * NS], sconv_sb[:, i * P:(i + 1) * P], identF[0:NS, 0:NS], [b_sconv, b_const], [bS_])
        for i in range(4):
            tr(pS_[:, (12 + i) * NS:(13 + i) * NS], slru_sb[:, i * P:(i + 1) * P], identF[0:NS, 0:NS], [b_slru, b_const], [bS_])
        cp("vector", sstT[:].rearrange("p a n -> p (a n)"), pS_[:, 0:16 * NS], [bS_], [b_sstT])
        yield
        conv = cxs.conv; b_conv = cxs.b_conv
        for t in range(4):
            ts("vector", conv[:, t, :], sstT[:, t, :], cols[:, C_CW + t:C_CW + t + 1], cols[:, C_CB + t:C_CB + t + 1], ALU.mult, ALU.add,
               [b_sstT, b_cols], [b_conv], t > 0)
            for k in range(1, 3):
                stt(conv[:, t, :], sstT[:, k * 4 + t, :], cols[:, C_CW + 4 * k + t:C_CW + 4 * k + t + 1], conv[:, t, :], ALU.mult, ALU.add,
                    [b_sstT, b_cols, b_conv], [b_conv], True)
            stt(conv[:, t, :], lxs[:, t, :], cols[:, C_CW + 12 + t:C_CW + 12 + t + 1], conv[:, t, :], ALU.mult, ALU.add,
                [blxs, b_cols, b_conv], [b_conv], True)
            yield
        yield from lru_gates(cxs, NS)
        tt("vector", hss[:], cxs.aa[:], sstT[:, 12:16, :], ALU.mult, [cxs.b_aa, b_sstT], [bhss])
        tt("vector", hss[:], hss[:], cxs.bb[:], ALU.add, [bhss, cxs.b_bb], [bhss])
        tt("vector", cxs.gg[:], cxs.gg[:], hss[:], ALU.mult, [cxs.b_gg, bhss], [cxs.b_gg])
        yield
        pS_, bS_ = ringS.next()
        for t in range(4):
            tr(pS_[0:NS, t * P:(t + 1) * P], lxs[:, t, :], identF[:], [blxs, b_const], [bS_])
        cp("vector", outs[:, 0:512], pS_[0:NS, :], [bS_], [b_outs])
        pr.dma("sync", convs_d[:, 2, :], outs[:, 0:512], reads=[b_outs])
        yield
        pS_, bS_ = ringS.next()
        for t in range(4):
            tr(pS_[0:NS, t * P:(t + 1) * P], hss[:, t, :], identF[:], [bhss, b_const], [bS_])
        b_outs2 = B("outs2")
        cp("vector", outs[:, 512:1024], pS_[0:NS, :], [bS_], [b_outs2])
        pr.dma("sync", lrus_d[:, :], outs[:, 512:1024], reads=[b_outs2])
        yield

    def gen_s_s5():
        gate_rows(g1s, b_g1s, M_G1)
        yield
        for ri in range(2):
            pS_, bS_ = ringS.next()
            for p in range(16):
                tr(pS_[:, p * NS:(p + 1) * NS], s5io[ri][:, p * P:(p + 1) * P], identF[0:NS, 0:NS], [b_s5io[ri], b_const], [bS_])
            cp("vector", h0[:, ri].rearrange("p a n -> p (a n)"), pS_[:, 0:16 * NS], [bS_], [b_h0], ri > 0)
            yield
        abr3 = V(I_ABR).unsqueeze(2).to_broadcast([P, 16, NS]); abi3 = V(I_ABI).unsqueeze(2).to_broadcast([P, 16, NS])
        tt("vector", hn_[:, 0], h0[:, 0], abr3, ALU.mult, [b_h0, b_s5t], [b_hn])
        tt("gpsimd", htmp[:, 0], h0[:, 1], abi3, ALU.mult, [b_h0, b_s5t], [b_htmp])
        yield
        tt("vector", hn_[:, 0], hn_[:, 0], htmp[:, 0], ALU.subtract, [b_hn, b_htmp], [b_hn])
        tt("vector", hn_[:, 1], h0[:, 1], abr3, ALU.mult, [b_h0, b_s5t, b_hn], [b_hn])
        tt("gpsimd", htmp[:, 1], h0[:, 0], abi3, ALU.mult, [b_h0, b_s5t, b_htmp], [b_htmp])
        yield
        tt("vector", hn_[:, 1], hn_[:, 1], htmp[:, 1], ALU.add, [b_hn, b_htmp], [b_hn])
        qs = lambda a, pl: a.rearrange("p (q pl) n -> p pl q n", pl=4)[:, pl]
        for ri in range(2):
            banks = [ringA.next() for _ in range(4)]
            for q in range(4):
                for pl in range(4):
                    pBu, bBu = banks[pl]
                    mm(pBu[:, q * NS:(q + 1) * NS], W1[pl * 32:(pl + 1) * 32, ri, q, L - 1, :], ubs[pl * 32:(pl + 1) * 32, q, :], True, True,
                       [b_W1, b_ubs], [bBu], tp=(pl * 32, 0))
            for pl in range(4):
                pBu, bBu = banks[pl]
                tt("vector", qs(hn_[:, ri], pl), qs(hn_[:, ri], pl), pBu[:, 0:4 * NS].rearrange("p (q n) -> p q n", n=NS), ALU.add,
                   [b_hn, bBu], [b_hn])
            yield
        cp("vector", hnb[:], hn_[:], [b_hn], [b_hnb])
        pY, bY = ringA.next()
        for q in range(4):
            mm(pY[:, q * NS:(q + 1) * NS], Dblk[:, q, :], ubs[:, q, :], True, False, [b_Dblk, b_ubs], [bY])
            for pl in range(4):
                p = q * 4 + pl
                o = pY[pl * 32:(pl + 1) * 32, q * NS:(q + 1) * NS]
                mm(o, CT0[:, 0, p, :], hnb[:, 0, p, :], False, False, [b_CT0, b_hnb], [bY], tp=(0, pl * 32))
                mm(o, CT0[:, 1, p, :], hnb[:, 1, p, :], False, True, [b_CT0, b_hnb], [bY], tp=(0, pl * 32))
        act(cxs.gy[:, :, :], pY[:, 0:4 * NS].rearrange("p (q n) -> p q n", n=NS), AF.Gelu_apprx_tanh, [bY], [cxs.b_gy])
        yield
        for ri, sd in enumerate((s5rs_d, s5is_d)):
            for g4 in range(4):
                pS_, bS_ = ringS.next()
                for i in range(4):
                    p = g4 * 4 + i
                    tr(pS_[0:NS, i * P:(i + 1) * P], hn_[:, ri, p, :], identF[:], [b_hn, b_const], [bS_])
                cp("scalar", s5io[ri][:, g4 * 512:(g4 + 1) * 512], pS_[0:NS, :], [bS_], [b_s5io[ri]], g4 > 0)
                yield
            pr.dma("sync", sd[:, :], s5io[ri][:], reads=[b_s5io[ri]])

    def rr_plain(gens):
        gens = list(gens)
        while gens:
            for g in list(gens):
                try:
                    next(g)
                except StopIteration:
                    gens.remove(g)

    rr_plain([gen_s_lru(), gen_s_s5()])
    mgs = sbs("mgs", (P, KD, NS), BF16); bmgs = B("mgs")
    run(s5_glu_and_merge(cxs, NS, mgs, bmgs))
    for k in range(KD):
        gc = (C_GLO + k) if k < 4 else (C_GSO + k - 4)
        ts("vector", wout_sb[:, k, :], wout_sb[:, k, :], cols[:, gc:gc + 1], None, ALU.mult, None, [b_wout, b_cols], [b_wout])
    for h in range(2):
        pO, bO = ringA.next()
        for k in range(KD):
            mm(pO[0:NS, :], mgs[:, k, :], wout_sb[:, k, h * 512:(h + 1) * 512], k == 0, k == KD - 1, [bmgs, b_wout], [bO])
        tt("vector", stmp[:, h * 512:(h + 1) * 512], pO[0:NS, :], g1s[:, h * 512:(h + 1) * 512], ALU.mult,
           [bO, b_g1s], [b_stmp], h > 0)
    tt("vector", x1s[:], xs_sb[:], stmp[:], ALU.add, [b_xs, b_stmp], [b_x1s])
    pr.dma("sync", x1_d[T:T + NS, :], x1s[:], reads=[b_x1s], writes=[b_x1s_scr])
    sample_norm_mod(x1s[:], b_x1s, C_N2G, M_SC2, M_SH2, hn2Ts, b_hn2Ts)

    dbg_stop('sample')
    pr.barrier(lambda e: e.memset(barr_t[:], 0.0))
    ss.close()
    s1.close()
    s2 = contextlib.ExitStack()
    sb2 = lambda name, shape, dt=F32: s2.enter_context(nc.sbuf_tensor(name, list(shape), dt))
    wg_sb = sb2("wg_sb", (P, KD, DFF), BF16)
    wu_sb = sb2("wu_sb", (P, KD, DFF), BF16)
    wd_sb = sb2("wd_sb", (P, NF, D), BF16)
    NBLK = (DFF + 511) // 512
    b_wg = [B("wg%d" % i) for i in range(NBLK)]; b_wu = [B("wu%d" % i) for i in range(NBLK)]
    b_wd = [B("wd%d" % i) for i in range(NF // 2)]
    wg_v = wg_d.rearrange("(k p) n -> p k n", p=P)
    wu_v = wu_d.rearrange("(k p) n -> p k n", p=P)
    wd_v = wd_d.rearrange("(f p) n -> p f n", p=P)
    for blk in range(NBLK):
        c0 = blk * 512; c1 = min(DFF, c0 + 512)
        pr.dma("gpsimd", wg_sb[:, :, c0:c1], wg_v[:, :, c0:c1], writes=[b_wg[blk]])
        pr.dma("gpsimd", wu_sb[:, :, c0:c1], wu_v[:, :, c0:c1], writes=[b_wu[blk]])
    for i in range(NF // 2):
        pr.dma("gpsimd", wd_sb[:, 2 * i:2 * i + 2, :], wd_v[:, 2 * i:2 * i + 2, :], writes=[b_wd[i]])

    G2bc = sb2("G2bc", (P, D)); FNGbc = sb2("FNGbc", (P, D)); b_G2 = B("G2bc"); b_FNG = B("FNGbc")
    xa = [sb2("xa%d" % i, (P, D)) for i in range(2)]; b_xa = [B("xa%d" % i) for i in range(2)]
    xr = [sb2("xr%d" % i, (P, D)) for i in range(2)]; b_xr = [B("xr%d" % i) for i in range(2)]
    ringXa = Ring(list(zip(xa, b_xa))); ringXr = Ring(list(zip(xr, b_xr)))
    stat2 = [sb2("stat2_%d" % i, (P, 4)) for i in range(4)]; b_stat2 = [B("stat2_%d" % i) for i in range(4)]
    ringStat2 = Ring(list(zip(stat2, b_stat2)))
    xn2 = [sb2("xn2_%d" % i, (P, D), BF16) for i in range(2)]; b_xn2 = [B("xn2_%d" % i) for i in range(2)]
    ringXn2 = Ring(list(zip(xn2, b_xn2)))
    hn2T = [sb2("hn2T%d" % i, (P, KD, CH2), BF16) for i in range(2)]; b_hn2T = [B("hn2T%d" % i) for i in range(2)]
    actT = sb2("actT", (P, NF, CH2), BF16); b_actT = B("actT")
    sil = [sb2("sil%d" % i, (P, CH2), BF16) for i in range(1)]; b_sil = [B("sil%d" % i) for i in range(1)]
    ringSil = Ring(list(zip(sil, b_sil)))
    tmp2x = sb2("tmp2x", (P, 512)); b_tmp2x = B("tmp2x")

    bcast_rows(G2bc, b_G2, modT[:, M_G2:M_G2 + 8, NS], b_modT, P, tmp2x, b_tmp2x)
    bcast_rows(FNGbc, b_FNG, cols[:, C_FNG:C_FNG + 8], b_cols, P, tmp2x, b_tmp2x)

    actTs = sb2("actTs", (P, NF, NS), BF16); b_actTs = B("actTs")

    def gen_gate_up(hT_ap, bh, n, with_sample=False):
        for f in range(NF):
            if with_sample:
                pGs, bGs = ringAll.next()
                for k in range(KD):
                    mm(pGs[:, 0:NS], wg_sb[:, k, f * P:(f + 1) * P], hn2Ts[:, k, :], k == 0, k == KD - 1, [b_wg[f // 4], b_hn2Ts], [bGs])
                for k in range(KD):
                    mm(pGs[:, NS:2 * NS], wu_sb[:, k, f * P:(f + 1) * P], hn2Ts[:, k, :], k == 0, k == KD - 1, [b_wu[f // 4], b_hn2Ts], [bGs])
                sls, bsls = ringSil.next()
                act(sls[:, 0:NS], pGs[:, 0:NS], AF.Silu, [bGs], [bsls])
                tt("vector", actTs[:, f, :], pGs[:, NS:2 * NS], sls[:, 0:NS], ALU.mult, [bGs, bsls], [b_actTs], f > 0)
            pG, bG = ringAll.next()
            pU, bU = ringAll.next()
            for k in range(KD):
                mm(pG[:, 0:n], wg_sb[:, k, f * P:(f + 1) * P], hT_ap[:, k, :], k == 0, k == KD - 1, [b_wg[f // 4], bh], [bG])
            for k in range(KD):
                mm(pU[:, 0:n], wu_sb[:, k, f * P:(f + 1) * P], hT_ap[:, k, :], k == 0, k == KD - 1, [b_wu[f // 4], bh], [bU])
            sl, bsl = ringSil.next()
            act(sl[:, 0:n], pG[:, 0:n], AF.Silu, [bG], [bsl])
            tt("vector", actT[:, f, 0:n], pU[:, 0:n], sl[:, 0:n], ALU.mult, [bU, bsl], [b_actT], f > 0)
            yield

    def gen_down(x_ap, bx, rows, col0, g2_ap, bg2, out_ap, junk_ap, bjunk, act_src=None):
        aT, b_aT = act_src if act_src is not None else (actT, b_actT)
        for h in range(2):
            pO, bO = ringAll.next()
            for f in range(NF):
                mm(pO[0:rows, :], aT[:, f, col0:col0 + rows], wd_sb[:, f, h * 512:(h + 1) * 512], f == 0, f == NF - 1,
                   [b_aT, b_wd[f // 2]], [bO])
            tt("vector", tmp2x[0:rows, :], pO[0:rows, :], g2_ap[0:rows, h * 512:(h + 1) * 512], ALU.mult, [bO, bg2], [b_tmp2x])
            tt("gpsimd", x_ap[:, h * 512:(h + 1) * 512], x_ap[:, h * 512:(h + 1) * 512], tmp2x[0:rows, :], ALU.add, [bx, b_tmp2x], [bx])
            yield
        st, bst = ringStat2.next()
        rms_stats(x_ap, rows, st, bst, bx, junk_ap, bjunk)
        stt(x_ap, x_ap, st[0:rows, 1:2], FNGbc[0:rows, :], ALU.mult, ALU.mult, [bx, bst, b_FNG], [bx])
        pr.dma("sync", out_ap, x_ap, reads=[bx])
        yield

    def gen_norm2(c):
        hT = hn2T[c % 2]; bh = b_hn2T[c % 2]
        for tt_i in range(CH2 // P):
            tok0 = c * CH2 + tt_i * P
            xT, bx = ringXa.next()
            pr.dma("sync", xT[:], x1_d[tok0:tok0 + P, :], reads=x1_tiles[tok0], writes=[bx])
            st, bst = ringStat2.next()
            xnT, bxn = ringXn2.next()
            rms_stats(xT[:], P, st, bst, bx, xnT[:], bxn)
            ts("vector", xnT[:], xT[:], st[:, 1:2], None, ALU.mult, None, [bx, bst], [bxn])
            yield
            pTr, bTr = ringT.next()
            for k in range(KD):
                tr(pTr[:, k * P:(k + 1) * P], xnT[:, k * P:(k + 1) * P], identB[:], [bxn, b_const], [bTr])
            for k in range(KD):
                ts("vector", hT[:, k, tt_i * P:(tt_i + 1) * P], pTr[:, k * P:(k + 1) * P],
                   pmod[:, 16 + k:17 + k], pmod[:, 24 + k:25 + k], ALU.mult, ALU.add, [bTr, b_pmod], [bh], True)
            yield

    def gen_chunk(c):
        yield from gen_gate_up(hn2T[c % 2][:, :, :], b_hn2T[c % 2], CH2, with_sample=(c == 0))
        for tt_i in range(CH2 // P):
            tok0 = c * CH2 + tt_i * P
            xT, bx = ringXr.next()
            pr.dma("sync", xT[:], x1_d[tok0:tok0 + P, :], reads=x1_tiles[tok0], writes=[bx])
            xnT, bxn = ringXn2.next()
            yield from gen_down(xT[:], bx, P, tt_i * P, G2bc, b_G2, yp_d[tok0:tok0 + P, :], xnT[:], bxn)

    for _ in gen_norm2(0):
        pass
    for c in range(NCH2):
        main = gen_chunk(c)
        side = gen_norm2(c + 1) if c + 1 < NCH2 else iter(())
        step = 0
        for _ in main:
            step += 1
            if step % 3 == 0:
                next(side, None)
        for _ in side:
            pass
    xS, bxS = ringXr.next()
    gS, bgS = ringXa.next()
    pr.dma("sync", xS[0:NS, :], x1_d[T:T + NS, :], reads=[b_x1s_scr], writes=[bxS])
    for h in range(2):
        pM, bM = ringAll.next()
        for i in range(4):
            k = h * 4 + i
            tr(pM[0:NS, i * P:(i + 1) * P], modT[:, M_G2 + k, 0:NS], identF[:], [b_modT, b_const], [bM])
        cp("vector", gS[0:NS, h * 512:(h + 1) * 512], pM[0:NS, :], [bM], [bgS], h > 0)
    xnT, bxn = ringXn2.next()
    for _ in gen_down(xS[0:NS, :], bxS, NS, 0, gS, bgS, ys_d[:, :], xnT[0:NS, :], bxn, act_src=(actTs, b_actTs)):
        pass

    pr.emit()
    s2.close()
    es.close()
    return nc


_CACHE = {}


def kernel(**inputs):
    f32 = lambda a: np.ascontiguousarray(np.asarray(a, dtype=np.float32))
    g = {k: f32(v) for k, v in inputs.items()}
    if "nc" not in _CACHE:
        _CACHE["nc"] = build_program()
    nc = _CACHE["nc"]
    rowpack = np.zeros((128, 128), np.float32)
    rowpack[0:48] = g["ada_b"][0].reshape(48, 128)
    rowpack[48:56] = g["norm1_g"][0].reshape(8, 128)
    rowpack[56:64] = g["norm2_g"][0].reshape(8, 128)
    rowpack[64:80] = g["conv_w"][0].reshape(16, 128)
    rowpack[80:84] = g["conv_b"][0].reshape(4, 128)
    rowpack[84:88] = g["lru_ba"][0].reshape(4, 128)
    rowpack[88:92] = g["lru_bx"][0].reshape(4, 128)
    rowpack[92:96] = g["lru_lambda"][0].reshape(4, 128)
    rowpack[96:100] = g["s5_d"][0].reshape(4, 128)
    rowpack[100:104] = g["g_lru_out"][0].reshape(4, 128)
    rowpack[104:108] = g["g_s5_out"][0].reshape(4, 128)
    rowpack[108:116] = g["final_norm_g"].reshape(8, 128)
    s5pack = np.zeros((48, 128), np.float32)
    s5pack[0:16] = g["s5_lambda_re"][0].reshape(16, 128)
    s5pack[16:32] = g["s5_lambda_im"][0].reshape(16, 128)
    s5pack[32:48] = np.repeat(g["s5_log_dt"][0].reshape(16, 2, 1), 64, axis=2).reshape(16, 128)
    shared = {
        "ada_w": g["ada_w"][0], "rowpack": rowpack, "s5pack": s5pack,
        "w_in": g["w_in"][0], "lru_wa": g["lru_wa"][0], "lru_wx": g["lru_wx"][0],
        "s5_b_re": g["s5_b_re"][0], "s5_b_im": g["s5_b_im"][0], "s5_c_re": g["s5_c_re"][0], "s5_c_im": g["s5_c_im"][0],
        "w_glu": g["s5_w_glu"][0], "w_out": g["w_out"][0],
        "ffn_w_gate": g["ffn_w_gate"][0], "ffn_w_up": g["ffn_w_up"][0], "ffn_w_down": g["ffn_w_down"][0],
    }
    shared = {k: np.ascontiguousarray(v) for k, v in shared.items()}
    in_maps = []
    for i in range(8):
        sl = slice(16 * i, 16 * i + 16)
        m = dict(shared)
        m["xp"] = g["x_prompt"][i]
        m["xs"] = np.ascontiguousarray(g["x_sample"][sl, 0, :])
        m["sconv"] = np.ascontiguousarray(g["state_conv"][0, sl])
        m["slru"] = np.ascontiguousarray(g["state_lru"][0, sl])
        m["ss5r"] = np.ascontiguousarray(g["state_s5_re"][0, sl].reshape(16, 2048))
        m["ss5i"] = np.ascontiguousarray(g["state_s5_im"][0, sl].reshape(16, 2048))
        m["c_all"] = np.ascontiguousarray(np.concatenate([g["c_sample"][sl], g["c_prompt"][i:i + 1]], axis=0))
        in_maps.append(m)
    res = run_bass_kernel_spmd(nc, in_maps, core_ids=list(range(8)))
    r = res.results
    cat = lambda key, shp: np.stack([np.asarray(r[i][key], dtype=np.float32).reshape(shp) for i in range(8)], 0)
    y_prompt = cat("yp", (T, D))
    y_sample = cat("ys", (NS, D)).reshape(128, 1, D)
    conv_prompt = cat("convp", (3, DL))[None]
    lru_prompt = cat("lrup", (DL,))[None]
    s5_re_prompt = cat("s5rp", (32, 64))[None]
    s5_im_prompt = cat("s5ip", (32, 64))[None]
    conv_sample = cat("convs", (NS, 3, DL)).reshape(1, 128, 3, DL)
    lru_sample = cat("lrus", (NS, DL)).reshape(1, 128, DL)
    s5_re_sample = cat("s5rs", (NS, 32, 64)).reshape(1, 128, 32, 64)
    s5_im_sample = cat("s5is", (NS, 32, 64)).reshape(1, 128, 32, 64)
    return (y_prompt, y_sample, conv_prompt, lru_prompt, s5_re_prompt, s5_im_prompt,
            conv_sample, lru_sample, s5_re_sample, s5_im_sample)
```
